# Optimizing a Trainium2 kernel written in Bass

```python
import math
import jax
import jax.numpy as jnp
from jax import lax
import numpy as np

D_MODEL = 1024
BATCH = 16
SEQ = 2048
DEPTH = 4
DEC_BATCH = 8
DEC_SEQ = 4096
PAST_LEN = 128

HEAD_DIM = 64
ROPE_THETA = 10000.0
Q_BLOCK = 128
LN_EPS = 1e-5
RMS_EPS = 1e-6
NEG_INF = -1e30

A_HEADS = 4
A_HALF = HEAD_DIM // 2
A_W = A_HEADS * HEAD_DIM

B_HEADS = 6
B_Q_RANK = 256
B_KV_RANK = 128
B_NOPE = 64
B_ROPE = 32
B_V = 64
B_W = B_HEADS * B_V

C_GROUPS = ((128, 1), (512, 4), (2048, 16))
C_HPG = 2
C_HEADS = C_HPG * len(C_GROUPS)
C_W = C_HEADS * HEAD_DIM

COL_A = 3 * A_W
COL_B = B_Q_RANK + B_KV_RANK + B_ROPE
COL_C = 3 * C_W
IN_COLS = COL_A + COL_B + COL_C
MIX_W = A_W + B_W + C_W

N_GROUPS = 4
EXPERTS_PER_GROUP = 8
N_EXPERTS = N_GROUPS * EXPERTS_PER_GROUP
TOP_K_INNER = 2
D_EXPERT = 512
MOE_BLOCK = 128

DN_ALPHA = (2 * DEPTH) ** 0.25
DN_BETA = (8 * DEPTH) ** -0.25

kernel_name = 'hybrid_diff_mla_dilated_hmoe_encoder'


def rope_tables(seq, dim):
    inv_freq = 1.0 / (ROPE_THETA ** (jnp.arange(0, dim, 2, dtype=jnp.float32) / dim))
    ang = jnp.arange(seq, dtype=jnp.float32)[:, None] * inv_freq[None, :]
    return jnp.cos(ang), jnp.sin(ang)


def apply_rope(x, cos, sin):
    shape = (1, x.shape[1]) + (1,) * (x.ndim - 3) + (cos.shape[-1],)
    c = cos.reshape(shape).astype(x.dtype)
    s = sin.reshape(shape).astype(x.dtype)
    x1, x2 = jnp.split(x, 2, axis=-1)
    return jnp.concatenate([x1 * c - x2 * s, x1 * s + x2 * c], axis=-1)


def layer_norm(x, g, b):
    xf = x.astype(jnp.float32)
    mu = jnp.mean(xf, axis=-1, keepdims=True)
    var = jnp.mean(jnp.square(xf - mu), axis=-1, keepdims=True)
    return ((xf - mu) * lax.rsqrt(var + LN_EPS) * g.astype(jnp.float32) + b.astype(jnp.float32)).astype(x.dtype)


def rms_norm(x, g):
    xf = x.astype(jnp.float32)
    ms = jnp.mean(jnp.square(xf), axis=-1, keepdims=True)
    return (xf * lax.rsqrt(ms + RMS_EPS) * g.astype(jnp.float32)).astype(x.dtype)


def to_query_blocks(t):
    b, s = t.shape[:2]
    return jnp.moveaxis(t.reshape((b, s // Q_BLOCK, Q_BLOCK) + t.shape[2:]), 1, 0)


def from_query_blocks(o):
    o = jnp.moveaxis(o, 0, 1)
    return o.reshape((o.shape[0], o.shape[1] * o.shape[2]) + o.shape[3:])


def differential_attention(q, k, v, lam):
    q = q * (A_HALF ** -0.5)

    def block(qb):
        s = jnp.einsum('bqhcd,bkhcd->bchqk', qb, k, preferred_element_type=jnp.float32)
        p = jax.nn.softmax(s, axis=-1)
        w = (p[:, 0] - lam * p[:, 1]).astype(v.dtype)
        return jnp.einsum('bhqk,bkhd->bqhd', w, v)

    return from_query_blocks(lax.map(block, to_query_blocks(q)))


def latent_attention(q_nope, q_rope, k_nope, k_rope, v):
    scale = (B_NOPE + B_ROPE) ** -0.5

    def block(args):
        qn, qr = args
        s = (jnp.einsum('bqhd,bkhd->bhqk', qn, k_nope, preferred_element_type=jnp.float32)
             + jnp.einsum('bqhr,bkr->bhqk', qr, k_rope, preferred_element_type=jnp.float32)) * scale
        p = jax.nn.softmax(s, axis=-1).astype(v.dtype)
        return jnp.einsum('bhqk,bkhd->bqhd', p, v)

    return from_query_blocks(lax.map(block, (to_query_blocks(q_nope), to_query_blocks(q_rope))))


def dilated_window_attention(q, k, v, r, side):
    b, s_len, h, d = q.shape
    L = s_len // r
    nb = -(-L // side)
    lp = nb * side

    def to_sub(t):
        return jnp.swapaxes(t.reshape(b, L, r, h, d), 1, 2).reshape(b * r, L, h, d)

    def from_sub(t):
        rest = t.shape[2:]
        return jnp.swapaxes(t.reshape((b, r, L) + rest), 1, 2).reshape((b, s_len) + rest)

    qs = to_sub(q * (d ** -0.5))
    qs = jnp.pad(qs, ((0, 0), (0, lp - L), (0, 0), (0, 0))).reshape(b * r, nb, side, h, d)
    pad_kv = ((0, 0), (side, lp - L + side), (0, 0), (0, 0))

    def windows(t):
        tb = jnp.pad(to_sub(t), pad_kv).reshape(b * r, nb + 2, side, h, d)
        return jnp.concatenate([tb[:, :-2], tb[:, 1:-1], tb[:, 2:]], axis=2)

    kw, vw = windows(k), windows(v)
    s = jnp.einsum('bnqhd,bnkhd->bnhqk', qs, kw, preferred_element_type=jnp.float32)
    qi = jnp.arange(side)[:, None]
    kt = jnp.arange(3 * side)[None, :]
    rel = kt - side - qi
    kidx = jnp.arange(nb)[:, None, None] * side + kt[None] - side
    valid = (jnp.abs(rel) <= side)[None] & (kidx >= 0) & (kidx < L)
    s = jnp.where(valid[None, :, None], s, NEG_INF)
    lse = jax.nn.logsumexp(s, axis=-1)
    p = jnp.exp(s - lse[..., None]).astype(v.dtype)
    o = jnp.einsum('bnhqk,bnkhd->bnqhd', p, vw).reshape(b * r, lp, h, d)[:, :L]
    lse = jnp.swapaxes(lse, 2, 3).reshape(b * r, lp, h)[:, :L]
    return from_sub(o), from_sub(lse)


def dilated_mixture_attention(q, k, v):
    outs, lses = [], []
    for g, (window, dil) in enumerate(C_GROUPS):
        hs = slice(g * C_HPG, (g + 1) * C_HPG)
        o, l = dilated_window_attention(q[:, :, hs], k[:, :, hs], v[:, :, hs], dil, window // (2 * dil))
        outs.append(o)
        lses.append(l)
    alpha = jax.nn.softmax(jnp.stack(lses), axis=0)
    return jnp.concatenate([o * a[..., None].astype(o.dtype) for o, a in zip(outs, alpha)], axis=2)


def expert_dispatch(xf, eid, ew, w1, w3, w2):
    n, d = xf.shape
    a = n * TOP_K_INNER
    e_flat = eid.reshape(a).astype(jnp.int32)
    order = jnp.argsort(e_flat)
    e_sorted = e_flat[order]
    counts = jnp.bincount(e_flat, length=N_EXPERTS).astype(jnp.int32)
    padded = (counts + MOE_BLOCK - 1) // MOE_BLOCK * MOE_BLOCK
    pad_end = jnp.cumsum(padded)
    pad_start = pad_end - padded
    start = jnp.cumsum(counts) - counts
    dest = pad_start[e_sorted] + jnp.arange(a, dtype=jnp.int32) - start[e_sorted]
    n_blocks = -(-(a + N_EXPERTS * (MOE_BLOCK - 1)) // MOE_BLOCK)
    p_len = n_blocks * MOE_BLOCK
    slot_tok = jnp.full((p_len,), n, jnp.int32).at[dest].set((order // TOP_K_INNER).astype(jnp.int32))
    slot_w = jnp.zeros((p_len,), xf.dtype).at[dest].set(ew.reshape(a)[order].astype(xf.dtype))
    blk_start = jnp.arange(n_blocks, dtype=jnp.int32) * MOE_BLOCK
    blk_exp = jnp.minimum(jnp.searchsorted(pad_end, blk_start, side='right'), N_EXPERTS - 1)
    x_pad = jnp.concatenate([xf, jnp.zeros((1, d), xf.dtype)], axis=0)

    def run(args):
        tok, e = args
        xb = x_pad[tok]
        hid = jax.nn.silu(xb @ w1[e]) * (xb @ w3[e])
        return hid @ w2[e]

    yb = lax.map(run, (slot_tok.reshape(n_blocks, MOE_BLOCK), blk_exp)).reshape(p_len, d)
    y = jnp.zeros((n + 1, d), xf.dtype).at[slot_tok].add(yb * slot_w[:, None])
    return y[:n]


def hierarchical_moe(x, w_coarse, w_fine, w1, w3, w2):
    b, s, d = x.shape
    xf = x.reshape(b * s, d)
    cl = jnp.einsum('nd,dg->ng', xf, w_coarse).astype(jnp.float32)
    cp = jax.nn.softmax(cl, axis=-1)
    grp = jnp.argmax(cl, axis=-1)
    pg = jnp.take_along_axis(cp, grp[:, None], axis=1)[:, 0]
    fl = jnp.einsum('nd,gde->nge', xf, w_fine).astype(jnp.float32)
    fl = jnp.take_along_axis(fl, grp[:, None, None], axis=1)[:, 0]
    tv, ti = lax.top_k(fl, TOP_K_INNER)
    tw = jax.nn.softmax(tv, axis=-1) * pg[:, None]
    eid = grp[:, None] * EXPERTS_PER_GROUP + ti
    return expert_dispatch(xf, eid, tw, w1, w3, w2).reshape(b, s, d)


def encoder_layer(x, layer_idx, rope_a, rope_b, rope_c, w_in, lam_vecs, subln_g, q_norm_g, w_uq,
                  kv_norm_g, w_ukv, w_out, ln1_g, ln1_b, w_coarse, w_fine, w1, w3, w2, ln2_g, ln2_b):
    b, s, _ = x.shape
    h = x @ w_in
    h_a, h_b, h_c = jnp.split(h, [COL_A, COL_A + COL_B], axis=-1)

    qa, ka, va = jnp.split(h_a, 3, axis=-1)
    qa = apply_rope(qa.reshape(b, s, A_HEADS, 2, A_HALF), *rope_a)
    ka = apply_rope(ka.reshape(b, s, A_HEADS, 2, A_HALF), *rope_a)
    va = va.reshape(b, s, A_HEADS, HEAD_DIM)
    lam_init = 0.8 - 0.6 * math.exp(-0.3 * layer_idx)
    lv = lam_vecs.astype(jnp.float32)
    lam = jnp.exp(jnp.sum(lv[0] * lv[1])) - jnp.exp(jnp.sum(lv[2] * lv[3])) + lam_init
    oa = rms_norm(differential_attention(qa, ka, va, lam), subln_g) * (1.0 - lam_init)

    c_q, c_kv, k_rope = jnp.split(h_b, [B_Q_RANK, B_Q_RANK + B_KV_RANK], axis=-1)
    qb = (rms_norm(c_q, q_norm_g) @ w_uq).reshape(b, s, B_HEADS, B_NOPE + B_ROPE)
    q_nope, q_rope = jnp.split(qb, [B_NOPE], axis=-1)
    kvb = (rms_norm(c_kv, kv_norm_g) @ w_ukv).reshape(b, s, B_HEADS, B_NOPE + B_V)
    k_nope, vb = jnp.split(kvb, [B_NOPE], axis=-1)
    ob = latent_attention(q_nope, apply_rope(q_rope, *rope_b), k_nope, apply_rope(k_rope, *rope_b), vb)

    qc, kc, vc = [t.reshape(b, s, C_HEADS, HEAD_DIM) for t in jnp.split(h_c, 3, axis=-1)]
    oc = dilated_mixture_attention(apply_rope(qc, *rope_c), apply_rope(kc, *rope_c), vc)

    mix = jnp.concatenate([oa.reshape(b, s, A_W), ob.reshape(b, s, B_W), oc.reshape(b, s, C_W)], axis=-1) @ w_out
    x = layer_norm(DN_ALPHA * x + mix, ln1_g, ln1_b)
    x = layer_norm(DN_ALPHA * x + hierarchical_moe(x, w_coarse, w_fine, w1, w3, w2), ln2_g, ln2_b)
    return x


def setup_inputs(seed: int = 0) -> dict:
    key = jax.random.key(seed)
    ks = jax.random.split(key, 20)
    f32 = jnp.float32

    def nrm(k, shape, scale):
        return jax.random.normal(k, shape, f32) * scale

    in_scale = np.concatenate([
        np.ones(2 * A_W), np.full(A_W, DN_BETA),
        np.ones(COL_B),
        np.ones(2 * C_W), np.full(C_W, DN_BETA)]).astype(np.float32)
    ukv_scale = np.tile(np.concatenate([np.ones(B_NOPE), np.full(B_V, DN_BETA)]), B_HEADS).astype(np.float32)
    return {
        'x_prompt': nrm(ks[0], (BATCH, SEQ, D_MODEL), 1.0),
        'x_sample': nrm(ks[1], (DEC_BATCH, DEC_SEQ, D_MODEL), 1.0),
        'w_in': nrm(ks[2], (DEPTH, D_MODEL, IN_COLS), D_MODEL ** -0.5) * in_scale,
        'diff_lambda': nrm(ks[3], (DEPTH, 4, A_HALF), 0.1),
        'diff_subln': 1.0 + nrm(ks[4], (DEPTH, HEAD_DIM), 0.1),
        'mla_q_norm': 1.0 + nrm(ks[5], (DEPTH, B_Q_RANK), 0.1),
        'mla_w_uq': nrm(ks[6], (DEPTH, B_Q_RANK, B_HEADS * (B_NOPE + B_ROPE)), B_Q_RANK ** -0.5),
        'mla_kv_norm': 1.0 + nrm(ks[7], (DEPTH, B_KV_RANK), 0.1),
        'mla_w_ukv': nrm(ks[8], (DEPTH, B_KV_RANK, B_HEADS * (B_NOPE + B_V)), B_KV_RANK ** -0.5) * ukv_scale,
        'w_out': nrm(ks[9], (DEPTH, MIX_W, D_MODEL), MIX_W ** -0.5 * DN_BETA),
        'ln1_g': 1.0 + nrm(ks[10], (DEPTH, D_MODEL), 0.1),
        'ln1_b': nrm(ks[11], (DEPTH, D_MODEL), 0.02),
        'moe_w_coarse': nrm(ks[12], (DEPTH, D_MODEL, N_GROUPS), D_MODEL ** -0.5),
        'moe_w_fine': nrm(ks[13], (DEPTH, N_GROUPS, D_MODEL, EXPERTS_PER_GROUP), D_MODEL ** -0.5),
        'moe_w1': nrm(ks[14], (DEPTH, N_EXPERTS, D_MODEL, D_EXPERT), D_MODEL ** -0.5),
        'moe_w3': nrm(ks[15], (DEPTH, N_EXPERTS, D_MODEL, D_EXPERT), D_MODEL ** -0.5 * DN_BETA),
        'moe_w2': nrm(ks[16], (DEPTH, N_EXPERTS, D_EXPERT, D_MODEL), D_EXPERT ** -0.5 * DN_BETA),
        'ln2_g': 1.0 + nrm(ks[17], (DEPTH, D_MODEL), 0.1),
        'ln2_b': nrm(ks[18], (DEPTH, D_MODEL), 0.02),
    }


def reference(x_prompt, x_sample, w_in, diff_lambda, diff_subln, mla_q_norm, mla_w_uq, mla_kv_norm,
              mla_w_ukv, w_out, ln1_g, ln1_b, moe_w_coarse, moe_w_fine, moe_w1, moe_w3, moe_w2, ln2_g, ln2_b):
    def trunk(x):
        s = x.shape[1]
        rope_a = rope_tables(s, A_HALF)
        rope_b = rope_tables(s, B_ROPE)
        rope_c = rope_tables(s, HEAD_DIM)
        for l in range(DEPTH):
            x = encoder_layer(x, l, rope_a, rope_b, rope_c, w_in[l], diff_lambda[l], diff_subln[l],
                              mla_q_norm[l], mla_w_uq[l], mla_kv_norm[l], mla_w_ukv[l], w_out[l],
                              ln1_g[l], ln1_b[l], moe_w_coarse[l], moe_w_fine[l], moe_w1[l], moe_w3[l],
                              moe_w2[l], ln2_g[l], ln2_b[l])
        return x

    y_prompt = trunk(x_prompt)
    y_sample = trunk(x_sample)
    return (y_prompt, y_sample)
```

```python
import math
import os
from contextlib import ExitStack

import numpy as np
import concourse.bass as bass
import concourse.mybir as mybir
from concourse.bass_utils import run_bass_kernel_spmd

F32 = mybir.dt.float32
BF16 = mybir.dt.bfloat16
I32 = mybir.dt.int32
ALU = mybir.AluOpType
AF = mybir.ActivationFunctionType
AX = mybir.AxisListType

D = 1024
DEPTH = 4
IN_COLS = 2336
COLB = 768
COLKV = 1024
COLKR = 1152
COLC = 1184
NEXP = 32
DE = 512
LN_EPS = 1e-5
RMS_EPS = 1e-6
DN_ALPHA = (2 * DEPTH) ** 0.25
BLK = 512
C_R = (1, 4, 16)


class Res:
    __slots__ = ("w", "r", "ep")

    def __init__(self):
        self.w = None
        self.r = {}
        self.ep = -1


class Eng:
    def __init__(self, e, sem, is_pe=False):
        self.e = e
        self.sem = sem
        self.n = 0
        self.seen = {}
        self.is_pe = is_pe


class DQ:
    def __init__(self, eng, sems):
        self.eng = eng
        self.sems = sems
        self.cnt = [0] * len(sems)
        self.k = 0


class KB:
    def __init__(self, nc, es, nq=22):
        self.nc = nc
        self.ep = 0
        mk = lambda n: es.enter_context(nc.semaphore(n))
        self.pe = Eng(nc.tensor, mk("s_pe"), True)
        self.act = Eng(nc.scalar, mk("s_act"))
        self.dve = Eng(nc.vector, mk("s_dve"))
        self.pool = Eng(nc.gpsimd, mk("s_pool"))
        self.sp = Eng(nc.sync, None)
        self.engs = [self.pe, self.act, self.dve, self.pool]
        self.qs = DQ(self.sp, [mk("q_s%d" % i) for i in range(16)])
        self.qp = DQ(self.pool, [mk("q_p%d" % i) for i in range(6)])
        self.queues = [self.qs, self.qp]

    def _fresh(self, b):
        if b.ep != self.ep:
            b.w = None
            b.r = {}
            b.ep = self.ep

    def wait(self, eng, sem, val):
        if val > 0 and eng.seen.get(id(sem), 0) < val:
            eng.e.wait_ge(sem, val)
            eng.seen[id(sem)] = val

    def deps(self, eng, R, W, is_dma):
        for b in R:
            self._fresh(b)
            if b.w is not None:
                sem, val = b.w
                if sem is eng.sem and not is_dma and eng.is_pe:
                    continue
                self.wait(eng, sem, val)
        for b in W:
            self._fresh(b)
            if b.w is not None:
                sem, val = b.w
                if not (sem is eng.sem and not is_dma):
                    self.wait(eng, sem, val)
            for sem, val in b.r.values():
                if sem is eng.sem and not is_dma:
                    continue
                self.wait(eng, sem, val)

    def _mark(self, tok, R, W):
        sem, val = tok
        for b in R:
            old = b.r.get(id(sem))
            if old is None or old[1] < val:
                b.r[id(sem)] = (sem, val)
        for b in W:
            b.w = tok
            b.r = {}

    def op(self, eng, fn, R=(), W=(), sig=True):
        self.deps(eng, R, W, False)
        ins = fn()
        if sig:
            eng.n += 1
            ins.then_inc(eng.sem, 1)
            tok = (eng.sem, eng.n)
        else:
            tok = (eng.sem, eng.n + 1)
        self._mark(tok, R, W)
        return ins

    def dma(self, q, fn, R=(), W=()):
        eng = q.eng
        self.deps(eng, R, W, True)
        i = q.k % len(q.sems)
        q.k += 1
        sem = q.sems[i]
        self.wait(eng, sem, q.cnt[i])
        ins = fn()
        ins.then_inc(sem, 16)
        q.cnt[i] += 16
        self._mark((sem, q.cnt[i]), R, W)
        return ins

    def barrier(self):
        for E in self.engs + [self.sp]:
            for X in self.engs:
                if X is not E:
                    self.wait(E, X.sem, X.n)
            for q in self.queues:
                for i, sem in enumerate(q.sems):
                    self.wait(E, sem, q.cnt[i])
        self.ep += 1


class Tl:
    def __init__(self, t):
        self.t = t
        self.res = Res()

    def __getitem__(self, k):
        return self.t[k]


def host_consts(smax, nb):
    def rope(dim):
        inv = (1.0 / (np.float32(10000.0) ** (np.arange(0, dim, 2, dtype=np.float32) / np.float32(dim)))).astype(np.float32)
        ang = (np.arange(smax, dtype=np.float32)[:, None] * inv[None, :]).astype(np.float32)
        return np.cos(ang).astype(np.float32), np.sin(ang).astype(np.float32)

    ca, sa = rope(32)
    cc, sc = rope(64)
    p = np.arange(128)
    cosA = ca[:, p % 16].T.copy()
    sinA = sa[:, p % 16].T.copy()
    cosC = cc[:, p % 32].T.copy()
    sinC = sc[:, p % 32].T.copy()
    cosB = np.ones((128, smax), np.float32)
    sinB = np.zeros((128, smax), np.float32)
    cosB[64:96] = cosA[0:32]
    sinB[64:96] = sinA[0:32]
    k = np.arange(128)[:, None]
    q = np.arange(128)[None, :]
    band = np.stack([((k - 128 - q) >= -64), (np.abs(k - q) <= 64), ((k + 128 - q) <= 64)], axis=1).astype(np.float32)
    misc = np.zeros((128, 512), np.float32)
    misc[:, 0] = np.arange(128)
    misc[:, 1:1 + nb] = (np.arange(nb) * BLK)[None, :]
    tri = (np.arange(128)[:, None] < np.arange(128)[None, :]).astype(np.float32)
    return {
        "c_ident": np.eye(128, dtype=np.float32), "c_cosA": cosA, "c_sinA": sinA, "c_cosC": cosC, "c_sinC": sinC,
        "c_cosB": cosB, "c_sinB": sinB, "c_band": band.reshape(128, 384).copy(), "c_misc": misc, "c_tri": tri,
    }


def build(seqs, depth, wdepth=DEPTH, stop_after=None, dbg=()):
    T = sum(seqs)
    NT = T // 128
    NCH = T // 512
    NB = -(-(2 * T + NEXP * (BLK - 1)) // BLK)
    NSLOT = NB * BLK
    SMAX = max(seqs)
    seq_of_chunk = []
    t0 = 0
    seq_starts = []
    for S in seqs:
        seq_starts.append(t0)
        for c in range(S // 512):
            seq_of_chunk.append((t0, S, c * 512))
        t0 += S

    nc = bass.Bass("TRN2", target_bir_lowering=False)
    dt_in = lambda n, s, d=F32: nc.dram_tensor(n, list(s), d, kind="ExternalInput").ap()
    dt_sc = lambda n, s, d: nc.dram_tensor(n, list(s), d, kind=("ExternalOutput" if n in dbg else "Internal")).ap()
    x_in = dt_in("x", [T, D])
    w_in = dt_in("w_in", [wdepth, D, IN_COLS])
    diff_lambda = dt_in("diff_lambda", [wdepth, 4, 32])
    diff_subln = dt_in("diff_subln", [wdepth, 64])
    mla_q_norm = dt_in("mla_q_norm", [wdepth, 256])
    mla_w_uq = dt_in("mla_w_uq", [wdepth, 256, 576])
    mla_kv_norm = dt_in("mla_kv_norm", [wdepth, 128])
    mla_w_ukv = dt_in("mla_w_ukv", [wdepth, 128, 768])
    w_out = dt_in("w_out", [wdepth, D, D])
    ln1_g = dt_in("ln1_g", [wdepth, D])
    ln1_b = dt_in("ln1_b", [wdepth, D])
    w_coarse = dt_in("moe_w_coarse", [wdepth, D, 4])
    w_fine = dt_in("moe_w_fine", [wdepth, 4, D, 8])
    w1 = dt_in("moe_w1", [wdepth, NEXP, D, DE])
    w3 = dt_in("moe_w3", [wdepth, NEXP, D, DE])
    w2 = dt_in("moe_w2", [wdepth, NEXP, DE, D])
    ln2_g = dt_in("ln2_g", [wdepth, D])
    ln2_b = dt_in("ln2_b", [wdepth, D])
    c_ident = dt_in("c_ident", [128, 128])
    c_cos = {"A": dt_in("c_cosA", [128, SMAX]), "C": dt_in("c_cosC", [128, SMAX]), "B": dt_in("c_cosB", [128, SMAX])}
    c_sin = {"A": dt_in("c_sinA", [128, SMAX]), "C": dt_in("c_sinC", [128, SMAX]), "B": dt_in("c_sinB", [128, SMAX])}
    c_band = dt_in("c_band", [128, 384])
    c_misc = dt_in("c_misc", [128, 512])
    c_tri = dt_in("c_tri", [128, 128])
    y_out = nc.dram_tensor("y", [T, D], F32, kind="ExternalOutput").ap()

    xa = dt_sc("s_xa", [T, D], F32)
    x1d = dt_sc("s_x1", [T, D], F32)
    x1b = dt_sc("s_x1b", [T, D], BF16)
    xT = dt_sc("s_xT", [8, 128, T], BF16)
    mixT = dt_sc("s_mixT", [8, 128, T], BF16)
    QKA = dt_sc("s_qka", [4, 128, T], BF16)
    QKC = dt_sc("s_qkc", [6, 128, T], BF16)
    QB = dt_sc("s_qb", [6, 96, T], BF16)
    KBd = dt_sc("s_kb", [6, 96, T], BF16)
    VA = dt_sc("s_va", [T, 256], BF16)
    VB = dt_sc("s_vb", [T, 384], BF16)
    VC = dt_sc("s_vc", [T, 384], BF16)
    NCd = dt_sc("s_nc", [6, 64, T], BF16)
    DCd = dt_sc("s_dc", [6, 64, T], F32)
    xs = dt_sc("s_xs", [NSLOT, D], BF16)
    ys = dt_sc("s_ys", [NSLOT, D], F32)

    es = ExitStack()
    with es:
        kb = KB(nc, es)
        PE, ACT, DVE, POOL = kb.pe, kb.act, kb.dve, kb.pool
        QS, QP = kb.qs, kb.qp
        es.enter_context(nc.Block())
        dres = {}

        def DR(*key):
            r = dres.get(key)
            if r is None:
                r = dres[key] = Res()
            return r

        uid = [0]

        def sb(stack, name, shape, dtype):
            uid[0] += 1
            return Tl(stack.enter_context(nc.sbuf_tensor("%s_%d" % (name, uid[0]), list(shape), dtype)))

        def pst(stack, name, shape, dtype=F32):
            uid[0] += 1
            return Tl(stack.enter_context(nc.psum_tensor("%s_%d" % (name, uid[0]), list(shape), dtype)))

        def rs(*ts):
            return [t.res if isinstance(t, Tl) else t for t in ts]

        def mm(out, lhsT, rhs, start, stop, R, W, sig=None):
            kb.op(PE, lambda: nc.tensor.matmul(out, lhsT=lhsT, rhs=rhs, start=start, stop=stop), R=rs(*R), W=rs(*W),
                  sig=(stop if sig is None else sig))

        def tr(out, in_, ident, R, W, sig=True):
            kb.op(PE, lambda: nc.tensor.transpose(out, in_, ident), R=rs(*R), W=rs(*W), sig=sig)

        def vop(eng, fn, R, W):
            kb.op(eng, fn, R=rs(*R), W=rs(*W))

        def dma(q, out, in_, R, W):
            kb.dma(q, lambda: q.eng.e.dma_start(out=out, in_=in_), R=rs(*R), W=rs(*W))

        ident_f = sb(es, "ident_f", [128, 128], F32)
        ident_b = sb(es, "ident_b", [128, 128], BF16)
        ones_f = sb(es, "ones_f", [128, 128], F32)
        ones_b = sb(es, "ones_b", [128, 128], BF16)
        tri_f = sb(es, "tri_f", [128, 128], F32)
        misc = sb(es, "misc", [128, 512], F32)
        epsr = sb(es, "epsr", [128, 2], F32)
        dma(QS, ident_f[:], c_ident, [], [ident_f])
        dma(QP, ident_b[:], c_ident, [], [ident_b])
        dma(QS, tri_f[:], c_tri, [], [tri_f])
        dma(QS, misc[:], c_misc, [], [misc])
        vop(DVE, lambda: nc.vector.memset(ones_f[:], 1.0), [], [ones_f])
        vop(DVE, lambda: nc.vector.memset(ones_b[:], 1.0), [], [ones_b])
        vop(DVE, lambda: nc.vector.memset(epsr[:, 0:1], LN_EPS), [], [epsr])
        vop(DVE, lambda: nc.vector.memset(epsr[:, 1:2], RMS_EPS), [], [epsr])
        E0 = sb(es, "E0", [128, NT, 32], F32)
        E1 = sb(es, "E1", [128, NT, 32], F32)
        wts = sb(es, "wts", [128, NT, 2], F32)
        dest_i = sb(es, "dest_i", [128, NT, 2], I32)
        widx = sb(es, "widx", [128, NB], I32)
        kb.barrier()

        def emit_xT_tile(ps, src_bf, a, xTs):
            for j in range(8):
                tr(ps[:, j * 128:(j + 1) * 128], src_bf[:, j * 128:(j + 1) * 128], ident_b[:], [src_bf, ident_b], [ps], sig=(j == 7))
            vop(ACT, lambda: nc.scalar.copy(out=xTs[:, :, a * 128:(a + 1) * 128], in_=ps[:].rearrange("p (j t) -> p j t", j=8)),
                [ps], [xTs])

        def rstd_from(sums_sq_ap, out_tl, n, eps_col, tmp_tl, R):
            vop(ACT, lambda: nc.scalar.activation(out=tmp_tl, in_=sums_sq_ap, func=AF.Sqrt, bias=epsr[0:tmp_tl.shape[0], eps_col:eps_col + 1], scale=1.0 / n),
                R + [epsr], [])
            return None

        def phase_xprep(src):
            with ExitStack() as ph:
                xin = [sb(ph, "p0_x%d" % i, [128, 4, D], F32) for i in range(2)]
                xbf = [sb(ph, "p0_b%d" % i, [128, D], BF16) for i in range(2)]
                xTs = [sb(ph, "p0_t%d" % i, [128, 8, 512], BF16) for i in range(2)]
                pss = [pst(ph, "p0_ps%d" % i, [128, 1024], BF16) for i in range(2)]
                for c in range(NCH):
                    xi = xin[c % 2]
                    dma(QS, xi[:], src[c * 512:(c + 1) * 512, :].rearrange("(a p) d -> p a d", p=128), [DR("x", c)], [xi])
                    xt = xTs[c % 2]
                    for a in range(4):
                        xb = xbf[a % 2]
                        vop(POOL, lambda: nc.gpsimd.tensor_copy(out=xb[:], in_=xi[:, a, :]), [xi], [xb])
                        emit_xT_tile(pss[a % 2], xb, a, xt)
                    dma(QS, xT[:, :, c * 512:(c + 1) * 512].rearrange("j p t -> p j t"), xt[:], [xt], [DR("xT", c)])
                kb.barrier()

        def phase_proj(l):
            with ExitStack() as ph:
                win = sb(ph, "p1_win", [128, 8, IN_COLS], BF16)
                wrot = sb(ph, "p1_wrot", [128, 8, 1312], BF16)
                wuq = sb(ph, "p1_wuq", [128, 2, 576], BF16)
                wuqr = sb(ph, "p1_wuqr", [128, 2, 576], BF16)
                wukv = sb(ph, "p1_wukv", [128, 768], BF16)
                wukv_v = sb(ph, "p1_wukvv", [128, 384], BF16)
                stg = sb(ph, "p1_stg", [128, 2, 768], F32)
                gq = sb(ph, "p1_gq", [128, 3], F32)
                for j in range(8):
                    dma(QP, win[:, j, :], w_in[l, j * 128:(j + 1) * 128, :], [], [win])
                def rot(dst_lo, src_lo, ncols, half):
                    dv = wrot[:, :, dst_lo:dst_lo + ncols].rearrange("p j (b two h) -> p j b two h", two=2, h=half)
                    sv = win[:, :, src_lo:src_lo + ncols].rearrange("p j (b two h) -> p j b two h", two=2, h=half)
                    vop(DVE, lambda: nc.vector.tensor_scalar_mul(out=dv[:, :, :, 0, :], in0=sv[:, :, :, 1, :], scalar1=-1.0), [win], [wrot])
                    vop(DVE, lambda: nc.vector.tensor_copy(out=dv[:, :, :, 1, :], in_=sv[:, :, :, 0, :]), [win], [wrot])
                rot(0, 0, 512, 16)
                rot(512, COLKR, 32, 16)
                rot(544, COLC, 768, 32)
                for j in range(2):
                    dma(QS, gq[:, j:j + 1], mla_q_norm[l, j * 128:(j + 1) * 128].rearrange("(p o) -> p o", o=1), [], [gq])
                dma(QS, gq[:, 2:3], mla_kv_norm[l].rearrange("(p o) -> p o", o=1), [], [gq])
                dma(QS, stg[:, :, 0:576], mla_w_uq[l].rearrange("(j p) c -> p j c", p=128), [], [stg])
                for j in range(2):
                    vop(DVE, lambda: nc.vector.tensor_scalar_mul(out=wuq[:, j, :], in0=stg[:, j, 0:576], scalar1=gq[:, j:j + 1]), [stg, gq], [wuq])
                vop(DVE, lambda: nc.vector.memset(wuqr[:], 0.0), [], [wuqr])
                dv = wuqr[:].rearrange("p j (h c) -> p j h c", c=96)[:, :, :, 64:96].rearrange("p j h (two f) -> p j h two f", two=2)
                sv = wuq[:].rearrange("p j (h c) -> p j h c", c=96)[:, :, :, 64:96].rearrange("p j h (two f) -> p j h two f", two=2)
                vop(DVE, lambda: nc.vector.tensor_scalar_mul(out=dv[:, :, :, 0, :], in0=sv[:, :, :, 1, :], scalar1=-1.0), [wuq], [wuqr])
                vop(DVE, lambda: nc.vector.tensor_copy(out=dv[:, :, :, 1, :], in_=sv[:, :, :, 0, :]), [wuq], [wuqr])
                stg2 = sb(ph, "p1_stg2", [128, 768], F32)
                dma(QS, stg2[:], mla_w_ukv[l], [], [stg2])
                vop(DVE, lambda: nc.vector.tensor_scalar_mul(out=wukv[:], in0=stg2[:], scalar1=gq[:, 2:3]), [stg2, gq], [wukv])
                vop(DVE, lambda: nc.vector.tensor_copy(out=wukv_v[:].rearrange("p (h c) -> p h c", c=64),
                                                       in_=wukv[:].rearrange("p (h c) -> p h c", c=128)[:, :, 64:128]), [wukv], [wukv_v])

                xTc = [sb(ph, "p1_x%d" % i, [128, 8, 512], BF16) for i in range(2)]
                tabs = [{k: (sb(ph, "p1_c%s%d" % (k, i), [128, 512], F32), sb(ph, "p1_s%s%d" % (k, i), [128, 512], F32)) for k in "ACB"} for i in range(2)]
                pq = [pst(ph, "p1_pq%d" % i, [128, 512]) for i in range(2)]
                pr = [pst(ph, "p1_pr%d" % i, [128, 512]) for i in range(2)]
                pv = [pst(ph, "p1_pv%d" % i, [128, 512]) for i in range(2)]
                pm = [pst(ph, "p1_pm%d" % i, [128, 512]) for i in range(2)]
                t1 = [sb(ph, "p1_t1%d" % i, [128, 512], F32) for i in range(2)]
                t2 = [sb(ph, "p1_t2%d" % i, [128, 512], F32) for i in range(2)]
                t3 = [sb(ph, "p1_t3%d" % i, [128, 512], F32) for i in range(2)]
                ob = [sb(ph, "p1_ob%d" % i, [128, 512], BF16) for i in range(4)]
                vo = [sb(ph, "p1_vo%d" % i, [128, 640], BF16) for i in range(2)]
                vbo = [sb(ph, "p1_vbo%d" % i, [128, 384], BF16) for i in range(2)]
                cqT = sb(ph, "p1_cqT", [128, 3, 512], BF16)
                sq = sb(ph, "p1_sq", [128, 3, 512], F32)
                rq = sb(ph, "p1_rq", [128, 512], F32)
                rkv = sb(ph, "p1_rkv", [128, 512], F32)
                kr = sb(ph, "p1_kr", [128, 512], BF16)
                rtok = sb(ph, "p1_rtok", [128, 4], F32)
                cnt = {"g": 0, "o": 0}

                def proj_fm(c, xc, lo, m, rlo):
                    i = cnt["g"] % 2
                    cnt["g"] += 1
                    for j in range(8):
                        mm(pq[i][0:m, :], win[:, j, lo:lo + m], xc[:, j, :], j == 0, j == 7, [win, xc], [pq[i]])
                    if rlo is not None:
                        for j in range(8):
                            mm(pr[i][0:m, :], wrot[:, j, rlo:rlo + m], xc[:, j, :], j == 0, j == 7, [wrot, xc], [pr[i]])
                    return i

                def rope_evac(i, m, tab, out_ap, W, extra=None):
                    ct, st = tab
                    vop(DVE, lambda: nc.vector.tensor_tensor(out=t1[i][0:m, :], in0=pq[i][0:m, :], in1=ct[0:m, :], op=ALU.mult), [pq[i], ct], [t1[i]])
                    vop(DVE, lambda: nc.vector.tensor_tensor(out=t2[i][0:m, :], in0=pr[i][0:m, :], in1=st[0:m, :], op=ALU.mult), [pr[i], st], [t2[i]])
                    if extra is None:
                        vop(POOL, lambda: nc.gpsimd.tensor_tensor(out=out_ap, in0=t1[i][0:m, :], in1=t2[i][0:m, :], op=ALU.add), [t1[i], t2[i]], W)
                    else:
                        vop(POOL, lambda: nc.gpsimd.tensor_tensor(out=t3[i][0:m, :], in0=t1[i][0:m, :], in1=t2[i][0:m, :], op=ALU.add), [t1[i], t2[i]], [t3[i]])
                        vop(POOL, lambda: nc.gpsimd.tensor_tensor(out=out_ap, in0=t3[i][0:m, :], in1=extra[0:m, :], op=ALU.mult), [t3[i], extra], W)

                def next_ob():
                    o = ob[cnt["o"] % 4]
                    cnt["o"] += 1
                    return o

                def load_chunk(c):
                    tk = c * 512
                    s0, S, pos0 = seq_of_chunk[c]
                    xc = xTc[c % 2]
                    dma(QS, xc[:], xT[:, :, tk:tk + 512].rearrange("j p t -> p j t"), [DR("xT", c)], [xc])
                    tb = tabs[c % 2]
                    for k in "ACB":
                        dma(QS, tb[k][0][:], c_cos[k][:, pos0:pos0 + 512], [], [tb[k][0]])
                        dma(QS, tb[k][1][:], c_sin[k][:, pos0:pos0 + 512], [], [tb[k][1]])

                PSTOP = os.environ.get("PSTOP")
                if PSTOP == "w":
                    kb.barrier()
                    return
                load_chunk(0)
                for c in range(NCH):
                    if c + 1 < NCH:
                        load_chunk(c + 1)
                    tk = c * 512
                    s0, S, pos0 = seq_of_chunk[c]
                    xc = xTc[c % 2]
                    tb = tabs[c % 2]
                    for ga in range(4):
                        i = proj_fm(c, xc, ga * 128, 128, ga * 128)
                        o = next_ob()
                        rope_evac(i, 128, tb["A"], o[:], [o])
                        dma(QS, QKA[ga, :, tk:tk + 512], o[:], [o], [DR("qka", ga, c)])
                    if PSTOP == "A":
                        kb.barrier()
                        return
                    for gc in range(6):
                        r = C_R[gc % 3]
                        L = S // r
                        i = proj_fm(c, xc, COLC + gc * 128, 128, 544 + gc * 128)
                        o = next_ob()
                        ct, st = tb["C"]
                        vop(DVE, lambda: nc.vector.tensor_tensor(out=t1[i][:], in0=pq[i][:], in1=ct[:], op=ALU.mult), [pq[i], ct], [t1[i]])
                        vop(DVE, lambda: nc.vector.tensor_tensor(out=t2[i][:], in0=pr[i][:], in1=st[:], op=ALU.mult), [pr[i], st], [t2[i]])
                        vop(POOL, lambda: nc.gpsimd.tensor_tensor(out=o[:], in0=t1[i][:], in1=t2[i][:], op=ALU.add), [t1[i], t2[i]], [o])
                        dma(QS, QKC[gc, :, tk:tk + 512], o[:], [o], [DR("qkc", gc, c)])
                    if PSTOP == "C":
                        kb.barrier()
                        return
                    for g in range(3):
                        i = proj_fm(c, xc, COLB + g * 128, 128, None)
                        vop(ACT, lambda: nc.scalar.copy(out=cqT[:, g, :], in_=pq[i][:]), [pq[i]], [cqT])
                        vop(ACT, lambda: nc.scalar.activation(out=sq[:, g, :], in_=pq[i][:], func=AF.Square), [pq[i]], [sq])
                    i = proj_fm(c, xc, COLKR, 32, 512)
                    rope_evac(i, 32, tb["A"], kr[0:32, :], [kr])
                    for j in range(2):
                        mm(pm[0][:], ones_f[:], sq[:, j, :], j == 0, j == 1, [ones_f, sq], [pm[0]])
                    vop(ACT, lambda: nc.scalar.activation(out=t3[0][:], in_=pm[0][:], func=AF.Sqrt, bias=epsr[:, 1:2], scale=1.0 / 256), [pm[0], epsr], [t3[0]])
                    vop(DVE, lambda: nc.vector.reciprocal(out=rq[:], in_=t3[0][:]), [t3[0]], [rq])
                    mm(pm[1][:], ones_f[:], sq[:, 2, :], True, True, [ones_f, sq], [pm[1]])
                    vop(ACT, lambda: nc.scalar.activation(out=t3[1][:], in_=pm[1][:], func=AF.Sqrt, bias=epsr[:, 1:2], scale=1.0 / 128), [pm[1], epsr], [t3[1]])
                    vop(DVE, lambda: nc.vector.reciprocal(out=rkv[:], in_=t3[1][:]), [t3[1]], [rkv])
                    if PSTOP == "Bs":
                        kb.barrier()
                        return
                    for h in range(6):
                        i = cnt["g"] % 2
                        cnt["g"] += 1
                        for j in range(2):
                            mm(pq[i][0:96, :], wuq[:, j, h * 96:(h + 1) * 96], cqT[:, j, :], j == 0, j == 1, [wuq, cqT], [pq[i]])
                        for j in range(2):
                            mm(pr[i][0:96, :], wuqr[:, j, h * 96:(h + 1) * 96], cqT[:, j, :], j == 0, j == 1, [wuqr, cqT], [pr[i]])
                        o = next_ob()
                        rope_evac(i, 96, tb["B"], o[0:96, :], [o], extra=rq)
                        dma(QS, QB[h, :, tk:tk + 512], o[0:96, :], [o], [DR("qb", h, c)])
                        i = cnt["g"] % 2
                        cnt["g"] += 1
                        mm(pq[i][0:64, :], wukv[:, h * 128:h * 128 + 64], cqT[:, 2, :], True, True, [wukv, cqT], [pq[i]])
                        o = next_ob()
                        vop(DVE, lambda: nc.vector.tensor_tensor(out=o[0:64, :], in0=pq[i][0:64, :], in1=rkv[0:64, :], op=ALU.mult), [pq[i], rkv], [o])
                        vop(ACT, lambda: nc.scalar.copy(out=o[64:96, :], in_=kr[0:32, :]), [kr], [o])
                        dma(QS, KBd[h, :, tk:tk + 512], o[0:96, :], [o], [DR("kb", h, c)])
                    if PSTOP == "Bh":
                        kb.barrier()
                        return
                    for a in range(4):
                        i = a % 2
                        tsl = slice(a * 128, (a + 1) * 128)
                        for j in range(8):
                            mm(pv[i][:, 0:256], xc[:, j, tsl], win[:, j, 512:768], j == 0, j == 7, [xc, win], [pv[i]])
                        for j in range(8):
                            mm(pm[i][:, 0:384], xc[:, j, tsl], win[:, j, COLC + 768:COLC + 1152], j == 0, j == 7, [xc, win], [pm[i]])
                        vop(ACT, lambda: nc.scalar.copy(out=vo[i][:, 0:256], in_=pv[i][:, 0:256]), [pv[i]], [vo[i]])
                        vop(ACT, lambda: nc.scalar.copy(out=vo[i][:, 256:640], in_=pm[i][:, 0:384]), [pm[i]], [vo[i]])
                        dma(QS, VA[tk + a * 128:tk + (a + 1) * 128, :], vo[i][:, 0:256], [vo[i]], [DR("va", c)])
                        dma(QS, VC[tk + a * 128:tk + (a + 1) * 128, :], vo[i][:, 256:640], [vo[i]], [DR("vc", c)])
                        mm(pv[i][:, 256:272], sq[:, 2, tsl], ones_f[:, 0:16], True, True, [sq, ones_f], [pv[i]])
                        vop(ACT, lambda: nc.scalar.activation(out=rtok[:, 2 * i:2 * i + 1], in_=pv[i][:, 256:257], func=AF.Sqrt, bias=epsr[:, 1:2], scale=1.0 / 128),
                            [pv[i], epsr], [rtok])
                        vop(DVE, lambda: nc.vector.reciprocal(out=rtok[:, 2 * i + 1:2 * i + 2], in_=rtok[:, 2 * i:2 * i + 1]), [rtok], [rtok])
                        mm(pm[i][:, 0:384], cqT[:, 2, tsl], wukv_v[:], True, True, [cqT, wukv_v], [pm[i]])
                        vop(DVE, lambda: nc.vector.tensor_scalar_mul(out=vbo[i][:], in0=pm[i][:, 0:384], scalar1=rtok[:, 2 * i + 1:2 * i + 2]), [pm[i], rtok], [vbo[i]])
                        dma(QS, VB[tk + a * 128:tk + (a + 1) * 128, :], vbo[i][:], [vbo[i]], [DR("vb", c)])
                kb.barrier()

        def phase_full_attn(l):
            lam_init = 0.8 - 0.6 * math.exp(-0.3 * l)
            with ExitStack() as ph:
                lv = sb(ph, "p2_lv", [1, 128], F32)
                lw = sb(ph, "p2_lw", [1, 8], F32)
                lp = sb(ph, "p2_lp", [1, 64], F32)
                gs = sb(ph, "p2_gs", [64, 4], F32)
                pl = pst(ph, "p2_pl", [128, 512])
                dma(QS, lv[:], diff_lambda[l:l + 1].rearrange("o a b -> o (a b)"), [], [lv])
                dma(QS, gs[:, 0:1], diff_subln[l].rearrange("(p o) -> p o", o=1), [], [gs])
                lvv = lv[:].rearrange("o (a two b) -> o a two b", two=2, b=32)
                vop(DVE, lambda: nc.vector.tensor_tensor(out=lp[:].rearrange("o (a b) -> o a b", b=32), in0=lvv[:, :, 0, :], in1=lvv[:, :, 1, :], op=ALU.mult), [lv], [lp])
                vop(DVE, lambda: nc.vector.reduce_sum(out=lw[:, 0:2], in_=lp[:].rearrange("o (a b) -> o a b", b=32), axis=AX.X), [lp], [lw])
                vop(ACT, lambda: nc.scalar.activation(out=lw[:, 2:4], in_=lw[:, 0:2], func=AF.Exp), [lw], [lw])
                vop(DVE, lambda: nc.vector.tensor_tensor(out=lw[:, 4:5], in0=lw[:, 3:4], in1=lw[:, 2:3], op=ALU.subtract), [lw], [lw])
                vop(DVE, lambda: nc.vector.tensor_scalar_add(out=lw[:, 5:6], in0=lw[:, 4:5], scalar1=-lam_init), [lw], [lw])
                l16 = sb(ph, "p2_l16", [1, 16], F32)
                vop(DVE, lambda: nc.vector.memset(l16[:], 0.0), [], [l16])
                vop(DVE, lambda: nc.vector.tensor_scalar(out=l16[:], in0=l16[:], scalar1=lw[:, 5:6], scalar2=None, op0=ALU.add), [l16, lw], [l16])
                mm(pl[0:64, 0:16], ones_f[0:1, 0:64], l16[0:1, :], True, True, [ones_f, l16], [pl])
                vop(ACT, lambda: nc.scalar.copy(out=gs[:, 1:2], in_=pl[0:64, 0:1]), [pl], [gs])
                vop(DVE, lambda: nc.vector.tensor_scalar_mul(out=gs[:, 2:3], in0=gs[:, 0:1], scalar1=1.0 - lam_init), [gs], [gs])

                qT = [sb(ph, "p2_q%d" % i, [96, SMAX], BF16) for i in range(2)]
                kT = [sb(ph, "p2_k%d" % i, [96, SMAX], BF16) for i in range(2)]
                vt = [sb(ph, "p2_v%d" % i, [128, SMAX // 128, 128], BF16) for i in range(2)]
                pT = [sb(ph, "p2_p%d" % i, [128, 512], BF16) for i in range(4)]
                psc = [pst(ph, "p2_s%d" % i, [128, 512]) for i in range(3)]
                pac = [pst(ph, "p2_a%d" % i, [128, 512]) for i in range(2)]
                prm = pst(ph, "p2_rm", [128, 512])
                dsh = [sb(ph, "p2_d%d" % i, [64, 512], F32) for i in range(2)]
                oc = [sb(ph, "p2_o%d" % i, [64, 512], F32) for i in range(3)]
                tq = sb(ph, "p2_tq", [64, 512], F32)
                obf = [sb(ph, "p2_ob%d" % i, [64, 512], BF16) for i in range(2)]
                for v in vt:
                    vop(DVE, lambda: nc.vector.memset(v[:, :, 64:128], 1.0), [], [v])
                st = {"u": 0, "v": 0, "s": 0, "p": 0, "a": 0, "o": 0}

                def unit(qsrc, ksrc, dk, scale, vtile, s0, S, qc):
                    nkt = S // 128
                    acc = pac[st["a"] % 2]
                    st["a"] += 1
                    for kt in range(nkt):
                        ps = psc[st["s"] % 3]
                        st["s"] += 1
                        mm(ps[:], ksrc[0:dk, kt * 128:(kt + 1) * 128], qsrc[0:dk, qc * 512:(qc + 1) * 512], True, True, [ksrc, qsrc], [ps])
                        p = pT[st["p"] % 4]
                        st["p"] += 1
                        vop(ACT, lambda: nc.scalar.activation(out=p[:], in_=ps[:], func=AF.Exp, scale=scale), [ps], [p])
                        mm(acc[:], vtile[:, kt, :], p[:], kt == 0, kt == nkt - 1, [vtile, p], [acc])
                    d = dsh[st["o"] % 2]
                    o = oc[st["o"] % 3]
                    st["o"] += 1
                    vop(ACT, lambda: nc.scalar.copy(out=d[:], in_=acc[64:128, :]), [acc], [d])
                    vop(DVE, lambda: nc.vector.reciprocal(out=d[:], in_=d[:]), [d], [d])
                    vop(DVE, lambda: nc.vector.tensor_tensor(out=o[:], in0=acc[0:64, :], in1=d[:], op=ALU.mult), [acc, d], [o])
                    return o

                def load_v(src, ncols, h, s0, S):
                    v = vt[st["v"] % 2]
                    st["v"] += 1
                    for k0 in range(0, S // 128, 8):
                        dma(QS, v[:, k0:k0 + 8, 0:64], src[s0 + k0 * 128:s0 + (k0 + 8) * 128, h * 64:(h + 1) * 64].rearrange("(kt p) d -> p kt d", p=128), [DR("vsrc")], [v])
                    return v

                def load_qk(qd, kd, dk, s0, S):
                    i = st["u"] % 2
                    st["u"] += 1
                    dma(QS, qT[i][0:dk, 0:S], qd[:, s0:s0 + S], [DR("qsrc")], [qT[i]])
                    dma(QS, kT[i][0:dk, 0:S], kd[:, s0:s0 + S], [DR("ksrc")], [kT[i]])
                    return qT[i], kT[i]

                for (s0, S) in zip(seq_starts, seqs):
                    for h in range(4):
                        v = load_v(VA, 256, h, s0, S)
                        qk = []
                        for cpt in range(2):
                            u = h * 2 + cpt
                            qk.append(load_qk(QKA[u // 4, (u % 4) * 32:(u % 4) * 32 + 32, :], QKA[2 + u // 4, (u % 4) * 32:(u % 4) * 32 + 32, :], 32, s0, S))
                        for qc in range(S // 512):
                            o0 = unit(qk[0][0], qk[0][1], 32, 32.0 ** -0.5, v, s0, S, qc)
                            o1 = unit(qk[1][0], qk[1][1], 32, 32.0 ** -0.5, v, s0, S, qc)
                            vop(DVE, lambda: nc.vector.scalar_tensor_tensor(out=o0[:], in0=o1[:], scalar=gs[:, 1:2], in1=o0[:], op0=ALU.mult, op1=ALU.add), [o1, o0, gs], [o0])
                            vop(ACT, lambda: nc.scalar.activation(out=tq[:], in_=o0[:], func=AF.Square), [o0], [tq])
                            mm(prm[0:64, :], ones_f[0:64, 0:64], tq[:], True, True, [ones_f, tq], [prm])
                            vop(ACT, lambda: nc.scalar.activation(out=tq[:], in_=prm[0:64, :], func=AF.Sqrt, bias=epsr[0:64, 1:2], scale=1.0 / 64), [prm, epsr], [tq])
                            vop(DVE, lambda: nc.vector.reciprocal(out=tq[:], in_=tq[:]), [tq], [tq])
                            ob_ = obf[qc % 2]
                            vop(DVE, lambda: nc.vector.scalar_tensor_tensor(out=ob_[:], in0=o0[:], scalar=gs[:, 2:3], in1=tq[:], op0=ALU.mult, op1=ALU.mult), [o0, tq, gs], [ob_])
                            tk = s0 + qc * 512
                            dma(QS, mixT[h // 2, (h % 2) * 64:(h % 2) * 64 + 64, tk:tk + 512], ob_[:], [ob_], [DR("mixT", tk // 512)])
                    for h in range(6):
                        v = load_v(VB, 384, h, s0, S)
                        q_, k_ = load_qk(QB[h], KBd[h], 96, s0, S)
                        for qc in range(S // 512):
                            o0 = unit(q_, k_, 96, 96.0 ** -0.5, v, s0, S, qc)
                            ob_ = obf[qc % 2]
                            vop(POOL, lambda: nc.gpsimd.tensor_copy(out=ob_[:], in_=o0[:]), [o0], [ob_])
                            tk = s0 + qc * 512
                            f0 = 256 + h * 64
                            dma(QS, mixT[f0 // 128, f0 % 128:f0 % 128 + 64, tk:tk + 512], ob_[:], [ob_], [DR("mixT", tk // 512)])
                kb.barrier()

        def phase_dilated(l):
            with ExitStack() as ph:
                band = sb(ph, "p4_band", [128, 384], F32)
                dma(QS, band[:], c_band, [], [band])
                qT = [sb(ph, "p4_q%d" % i, [128, SMAX], BF16) for i in range(2)]
                kT = [sb(ph, "p4_k%d" % i, [128, SMAX], BF16) for i in range(2)]
                stq = sb(ph, "p4_stq", [128, SMAX], BF16)
                stk = sb(ph, "p4_stk", [128, SMAX], BF16)
                vt = [sb(ph, "p4_v%d" % i, [128, SMAX // 128, 2, 128], BF16) for i in range(2)]
                nsb = [sb(ph, "p4_n%d" % i, [64, SMAX], BF16) for i in range(2)]
                dsb = [sb(ph, "p4_d%d" % i, [64, SMAX], F32) for i in range(2)]
                psc = [pst(ph, "p4_s%d" % i, [128, 512]) for i in range(3)]
                pac = [pst(ph, "p4_a%d" % i, [128, 512]) for i in range(3)]
                pe_ = [sb(ph, "p4_e%d" % i, [128, 384], F32) for i in range(3)]
                pT = [sb(ph, "p4_p%d" % i, [128, 384], BF16) for i in range(3)]
                for v in vt:
                    vop(DVE, lambda: nc.vector.memset(v[:, :, :, 64:128], 1.0), [], [v])
                st = {"s": 0, "a": 0, "b": 0}
                scale = 64.0 ** -0.5
                for si, (s0, S) in enumerate(zip(seq_starts, seqs)):
                    for g in range(3):
                        r = C_R[g]
                        L = S // r
                        nlt = L // 128
                        bi = (si * 3 + g) % 2
                        q_, k_, v_ = qT[bi], kT[bi], vt[bi]
                        if r == 1:
                            dma(QS, q_[:, 0:S], QKC[g, :, s0:s0 + S], [DR("qkc")], [q_])
                            dma(QS, k_[:, 0:S], QKC[3 + g, :, s0:s0 + S], [DR("qkc")], [k_])
                        else:
                            dma(QS, stq[:, 0:S], QKC[g, :, s0:s0 + S], [DR("qkc")], [stq])
                            dma(QS, stk[:, 0:S], QKC[3 + g, :, s0:s0 + S], [DR("qkc")], [stk])
                            vop(DVE, lambda: nc.vector.tensor_copy(out=q_[:, 0:S].rearrange("p (r m) -> p m r", r=r), in_=stq[:, 0:S].rearrange("p (m r) -> p m r", r=r)), [stq], [q_])
                            vop(POOL, lambda: nc.gpsimd.tensor_copy(out=k_[:, 0:S].rearrange("p (r m) -> p m r", r=r), in_=stk[:, 0:S].rearrange("p (m r) -> p m r", r=r)), [stk], [k_])
                        for rho in range(r):
                            for j in range(2):
                                src = VC[s0:s0 + S, g * 128 + j * 64:g * 128 + (j + 1) * 64].rearrange("(kt p r) d -> r p kt d", p=128, r=r)[rho]
                                for k0 in range(0, nlt, 8):
                                    k1 = min(nlt, k0 + 8)
                                    dma(QS, v_[:, rho * nlt + k0:rho * nlt + k1, j, 0:64], src[:, k0:k1, :], [DR("vc")], [v_])
                        for j in range(2):
                            nb_, db_ = nsb[j], dsb[j]
                            for rho in range(r):
                                for t in range(nlt):
                                    base = rho * L + t * 128
                                    tiles = [tt for tt in (t - 1, t, t + 1) if 0 <= tt < nlt]
                                    ps = psc[st["s"] % 3]
                                    e_ = pe_[st["s"] % 3]
                                    p_ = pT[st["s"] % 3]
                                    st["s"] += 1
                                    for tt in tiles:
                                        mi = tt - t + 1
                                        kb0 = rho * L + tt * 128
                                        mm(ps[:, mi * 128:(mi + 1) * 128], k_[j * 64:(j + 1) * 64, kb0:kb0 + 128], q_[j * 64:(j + 1) * 64, base:base + 128],
                                           True, True, [k_, q_], [ps], sig=(tt == tiles[-1]))
                                    lo, hi = (tiles[0] - t + 1) * 128, (tiles[-1] - t + 2) * 128
                                    vop(ACT, lambda: nc.scalar.activation(out=e_[:, lo:hi], in_=ps[:, lo:hi], func=AF.Exp, scale=scale), [ps], [e_])
                                    vop(POOL, lambda: nc.gpsimd.tensor_tensor(out=p_[:, lo:hi], in0=e_[:, lo:hi], in1=band[:, lo:hi], op=ALU.mult), [e_, band], [p_])
                                    acc = pac[st["a"] % 3]
                                    st["a"] += 1
                                    for n_, tt in enumerate(tiles):
                                        mi = tt - t + 1
                                        mm(acc[:, 0:128], v_[:, rho * nlt + tt, j, :], p_[:, mi * 128:(mi + 1) * 128], n_ == 0, n_ == len(tiles) - 1, [v_, p_], [acc])
                                    if r == 1:
                                        no = nb_[:, t * 128:(t + 1) * 128]
                                        do = db_[:, t * 128:(t + 1) * 128]
                                    else:
                                        no = nb_[:, 0:S].rearrange("p (m r) -> p r m", r=r)[:, rho, t * 128:(t + 1) * 128]
                                        do = db_[:, 0:S].rearrange("p (m r) -> p r m", r=r)[:, rho, t * 128:(t + 1) * 128]
                                    vop(DVE, lambda: nc.vector.tensor_copy(out=no, in_=acc[0:64, 0:128]), [acc], [nb_])
                                    vop(ACT, lambda: nc.scalar.copy(out=do, in_=acc[64:128, 0:128]), [acc], [db_])
                            dma(QS, NCd[g * 2 + j, :, s0:s0 + S], nb_[:, 0:S], [nb_], [DR("ncd")])
                            dma(QS, DCd[g * 2 + j, :, s0:s0 + S], db_[:, 0:S], [db_], [DR("dcd")])
                kb.barrier()
            with ExitStack() as ph:
                nn = [sb(ph, "p4_nn%d" % i, [64, 6, 512], BF16) for i in range(2)]
                dd = [sb(ph, "p4_dd%d" % i, [64, 6, 512], F32) for i in range(2)]
                tt_ = [sb(ph, "p4_tt%d" % i, [64, 2, 512], F32) for i in range(2)]
                oo = [sb(ph, "p4_oo%d" % i, [64, 6, 512], BF16) for i in range(2)]
                for c in range(NCH):
                    tk = c * 512
                    n_, d_, t_, o_ = nn[c % 2], dd[c % 2], tt_[c % 2], oo[c % 2]
                    dma(QS, n_[:], NCd[:, :, tk:tk + 512].rearrange("h p t -> p h t"), [], [n_])
                    dma(QS, d_[:], DCd[:, :, tk:tk + 512].rearrange("h p t -> p h t"), [], [d_])
                    vop(DVE, lambda: nc.vector.tensor_tensor(out=t_[:], in0=d_[:, 0:2, :], in1=d_[:, 2:4, :], op=ALU.add), [d_], [t_])
                    vop(DVE, lambda: nc.vector.tensor_tensor(out=t_[:], in0=t_[:], in1=d_[:, 4:6, :], op=ALU.add), [d_, t_], [t_])
                    vop(DVE, lambda: nc.vector.reciprocal(out=t_[:], in_=t_[:]), [t_], [t_])
                    for g in range(3):
                        vop(POOL, lambda: nc.gpsimd.tensor_tensor(out=o_[:, 2 * g:2 * g + 2, :], in0=n_[:, 2 * g:2 * g + 2, :], in1=t_[:], op=ALU.mult), [n_, t_], [o_])
                    for hh in range(6):
                        f0 = 640 + hh * 64
                        dma(QS, mixT[f0 // 128, f0 % 128:f0 % 128 + 64, tk:tk + 512], o_[:, hh, :], [o_], [DR("mixT", c)])
                kb.barrier()

        def layer_norm(z, gb, bb, st4, junk):
            vop(DVE, lambda: nc.vector.memset(st4[:, 0:2], 0.0), [], [st4])
            vop(ACT, lambda: nc.scalar.activation(out=junk[:], in_=z[:], func=AF.Identity, accum_out=st4[:, 0:1]), [z, st4], [junk, st4])
            vop(ACT, lambda: nc.scalar.activation(out=junk[:], in_=z[:], func=AF.Square, accum_out=st4[:, 1:2]), [z, st4], [junk, st4])
            vop(DVE, lambda: nc.vector.tensor_scalar_mul(out=st4[:, 2:3], in0=st4[:, 0:1], scalar1=1.0 / D), [st4], [st4])
            vop(DVE, lambda: nc.vector.tensor_tensor(out=st4[:, 3:4], in0=st4[:, 2:3], in1=st4[:, 2:3], op=ALU.mult), [st4], [st4])
            vop(DVE, lambda: nc.vector.scalar_tensor_tensor(out=st4[:, 4:5], in0=st4[:, 1:2], scalar=1.0 / D, in1=st4[:, 3:4], op0=ALU.mult, op1=ALU.subtract), [st4], [st4])
            vop(ACT, lambda: nc.scalar.activation(out=st4[:, 5:6], in_=st4[:, 4:5], func=AF.Sqrt, bias=epsr[:, 0:1], scale=1.0), [st4, epsr], [st4])
            vop(DVE, lambda: nc.vector.reciprocal(out=st4[:, 6:7], in_=st4[:, 5:6]), [st4], [st4])
            vop(DVE, lambda: nc.vector.tensor_scalar(out=z[:], in0=z[:], scalar1=st4[:, 2:3], scalar2=st4[:, 6:7], op0=ALU.subtract, op1=ALU.mult), [z, st4], [z])
            vop(POOL, lambda: nc.gpsimd.tensor_tensor(out=z[:], in0=z[:], in1=gb[:], op=ALU.mult), [z, gb], [z])
            vop(POOL, lambda: nc.gpsimd.tensor_tensor(out=z[:], in0=z[:], in1=bb[:], op=ALU.add), [z, bb], [z])

        def phase_out_ln1(l, xsrc):
            with ExitStack() as ph:
                wout = sb(ph, "p5_wout", [128, 8, D], BF16)
                wr = sb(ph, "p5_wr", [128, 8, 36], F32)
                gb = sb(ph, "p5_g", [128, D], F32)
                bb = sb(ph, "p5_b", [128, D], F32)
                for j in range(8):
                    dma(QP, wout[:, j, :], w_out[l, j * 128:(j + 1) * 128, :], [], [wout])
                dma(QS, wr[:, :, 0:4], w_coarse[l].rearrange("(j p) g -> p j g", p=128), [], [wr])
                for g in range(4):
                    dma(QS, wr[:, :, 4 + g * 8:12 + g * 8], w_fine[l, g].rearrange("(j p) e -> p j e", p=128), [], [wr])
                dma(QS, gb[:], ln1_g[l:l + 1, :].partition_broadcast(128), [], [gb])
                dma(QS, bb[:], ln1_b[l:l + 1, :].partition_broadcast(128), [], [bb])
                mx = [sb(ph, "p5_m%d" % i, [128, 8, 512], BF16) for i in range(2)]
                xr = [sb(ph, "p5_x%d" % i, [128, 4, D], F32) for i in range(2)]
                z = [sb(ph, "p5_z%d" % i, [128, D], F32) for i in range(3)]
                zb = [sb(ph, "p5_zb%d" % i, [128, D], BF16) for i in range(2)]
                junk = sb(ph, "p5_junk", [128, D], F32)
                st4 = [sb(ph, "p5_st%d" % i, [128, 8], F32) for i in range(2)]
                x1T = [sb(ph, "p5_xT%d" % i, [128, 8, 128], F32) for i in range(2)]
                po = [pst(ph, "p5_po%d" % i, [128, 1024]) for i in range(2)]
                pt = [pst(ph, "p5_pt%d" % i, [128, 512]) for i in range(2)]
                plg = pst(ph, "p5_plg", [128, 512])
                lg = sb(ph, "p5_lg", [128, 36], F32)
                sm = sb(ph, "p5_sm", [128, 64], F32)

                def load(c):
                    dma(QS, mx[c % 2][:], mixT[:, :, c * 512:(c + 1) * 512].rearrange("j p t -> p j t"), [DR("mixT", c)], [mx[c % 2]])
                    dma(QS, xr[c % 2][:], xsrc[c * 512:(c + 1) * 512, :].rearrange("(a p) d -> p a d", p=128), [DR("x", c)], [xr[c % 2]])

                load(0)
                for c in range(NCH):
                    if c + 1 < NCH:
                        load(c + 1)
                    m_, x_ = mx[c % 2], xr[c % 2]
                    for a in range(4):
                        it = c * 4 + a
                        tok0 = it * 128
                        p_ = po[it % 2]
                        for hf in range(2):
                            for j in range(8):
                                mm(p_[:, hf * 512:(hf + 1) * 512], m_[:, j, a * 128:(a + 1) * 128], wout[:, j, hf * 512:(hf + 1) * 512], j == 0, j == 7, [m_, wout], [p_])
                        z_ = z[it % 3]
                        vop(DVE, lambda: nc.vector.scalar_tensor_tensor(out=z_[:], in0=x_[:, a, :], scalar=DN_ALPHA, in1=p_[:], op0=ALU.mult, op1=ALU.add), [x_, p_], [z_])
                        layer_norm(z_, gb, bb, st4[it % 2], junk)
                        dma(QS, x1d[tok0:tok0 + 128, :], z_[:], [z_], [DR("x1", it)])
                        zb_ = zb[it % 2]
                        vop(ACT, lambda: nc.scalar.copy(out=zb_[:], in_=z_[:]), [z_], [zb_])
                        dma(QS, x1b[tok0:tok0 + 128, :], zb_[:], [zb_], [DR("x1b", it)])
                        xT_ = x1T[it % 2]
                        for hf in range(2):
                            for jj in range(4):
                                j = hf * 4 + jj
                                tr(pt[hf][:, jj * 128:(jj + 1) * 128], z_[:, j * 128:(j + 1) * 128], ident_f[:], [z_, ident_f], [pt[hf]], sig=(jj == 3))
                            vop(ACT, lambda: nc.scalar.copy(out=xT_[:, hf * 4:(hf + 1) * 4, :], in_=pt[hf][:].rearrange("p (j t) -> p j t", j=4)), [pt[hf]], [xT_])
                        for j in range(8):
                            mm(plg[:, 0:36], xT_[:, j, :], wr[:, j, :], j == 0, j == 7, [xT_, wr], [plg])
                        vop(DVE, lambda: nc.vector.tensor_copy(out=lg[:], in_=plg[:, 0:36]), [plg], [lg])
                        vop(DVE, lambda: nc.vector.reduce_max(out=sm[:, 0:1], in_=lg[:, 0:4], axis=AX.X), [lg], [sm])
                        vop(DVE, lambda: nc.vector.tensor_scalar_mul(out=sm[:, 1:2], in0=sm[:, 0:1], scalar1=-1.0), [sm], [sm])
                        vop(DVE, lambda: nc.vector.memset(sm[:, 2:3], 0.0), [], [sm])
                        vop(ACT, lambda: nc.scalar.activation(out=sm[:, 44:48], in_=lg[:, 0:4], func=AF.Exp, bias=sm[:, 1:2], scale=1.0, accum_out=sm[:, 2:3]), [lg, sm], [sm])
                        vop(DVE, lambda: nc.vector.reciprocal(out=sm[:, 3:4], in_=sm[:, 2:3]), [sm], [sm])
                        vop(DVE, lambda: nc.vector.tensor_scalar(out=sm[:, 4:8], in0=lg[:, 0:4], scalar1=sm[:, 0:1], scalar2=None, op0=ALU.is_equal), [lg, sm], [sm])
                        vop(DVE, lambda: nc.vector.tensor_scalar_mul(out=sm[:, 8:16], in0=lg[:, 4:12], scalar1=sm[:, 4:5]), [lg, sm], [sm])
                        for g in range(1, 4):
                            vop(DVE, lambda: nc.vector.scalar_tensor_tensor(out=sm[:, 8:16], in0=lg[:, 4 + g * 8:12 + g * 8], scalar=sm[:, 4 + g:5 + g], in1=sm[:, 8:16],
                                                                            op0=ALU.mult, op1=ALU.add), [lg, sm], [sm])
                        vop(DVE, lambda: nc.vector.max(out=sm[:, 16:24], in_=sm[:, 8:16]), [sm], [sm])
                        vop(DVE, lambda: nc.vector.tensor_scalar(out=sm[:, 24:32], in0=sm[:, 8:16], scalar1=sm[:, 16:17], scalar2=None, op0=ALU.is_equal), [sm], [sm])
                        vop(DVE, lambda: nc.vector.tensor_scalar(out=sm[:, 32:40], in0=sm[:, 8:16], scalar1=sm[:, 17:18], scalar2=None, op0=ALU.is_equal), [sm], [sm])
                        vop(DVE, lambda: nc.vector.tensor_tensor(out=sm[:, 40:41], in0=sm[:, 17:18], in1=sm[:, 16:17], op=ALU.subtract), [sm], [sm])
                        vop(ACT, lambda: nc.scalar.activation(out=sm[:, 41:42], in_=sm[:, 40:41], func=AF.Exp), [sm], [sm])
                        vop(DVE, lambda: nc.vector.tensor_scalar_add(out=sm[:, 42:43], in0=sm[:, 41:42], scalar1=1.0), [sm], [sm])
                        vop(DVE, lambda: nc.vector.reciprocal(out=sm[:, 43:44], in_=sm[:, 42:43]), [sm], [sm])
                        vop(DVE, lambda: nc.vector.tensor_tensor(out=wts[:, it, 0:1], in0=sm[:, 43:44], in1=sm[:, 3:4], op=ALU.mult), [sm], [wts])
                        vop(DVE, lambda: nc.vector.tensor_tensor(out=wts[:, it, 1:2], in0=wts[:, it, 0:1], in1=sm[:, 41:42], op=ALU.mult), [sm, wts], [wts])
                        for g in range(4):
                            vop(DVE, lambda: nc.vector.tensor_scalar_mul(out=E0[:, it, g * 8:(g + 1) * 8], in0=sm[:, 24:32], scalar1=sm[:, 4 + g:5 + g]), [sm], [E0])
                            vop(DVE, lambda: nc.vector.tensor_scalar_mul(out=E1[:, it, g * 8:(g + 1) * 8], in0=sm[:, 32:40], scalar1=sm[:, 4 + g:5 + g]), [sm], [E1])
                kb.barrier()

        def phase_route(l):
            with ExitStack() as ph:
                e01 = sb(ph, "p6_e01", [128, NT, 32], F32)
                run = sb(ph, "p6_run", [128, NT + 1, 32], F32)
                cb = sb(ph, "p6_cb", [128, 32], F32)
                ci = sb(ph, "p6_ci", [128, 32], I32)
                inc = [sb(ph, "p6_inc%d" % i, [128, 32], F32) for i in range(2)]
                pstart = sb(ph, "p6_ps", [128, 32], F32)
                pend = sb(ph, "p6_pe", [128, 32], F32)
                destf = sb(ph, "p6_df", [128, NT, 2], F32)
                tmp = [sb(ph, "p6_t%d" % i, [128, 32], F32) for i in range(2)]
                tmp2 = [sb(ph, "p6_u%d" % i, [128, 32], F32) for i in range(2)]
                bex = sb(ph, "p6_bex", [128, NB], F32)
                pc = [pst(ph, "p6_pc%d" % i, [128, 512]) for i in range(2)]
                vop(POOL, lambda: nc.gpsimd.tensor_tensor(out=e01[:], in0=E0[:], in1=E1[:], op=ALU.add), [E0, E1], [e01])
                vop(DVE, lambda: nc.vector.memset(run[:, 0, :], 0.0), [], [run])
                for i in range(NT):
                    vop(DVE, lambda: nc.vector.tensor_tensor(out=run[:, i + 1, :], in0=run[:, i, :], in1=e01[:, i, :], op=ALU.add), [run, e01], [run])
                mm(pc[0][:, 0:32], ones_f[:], run[:, NT, :], True, True, [ones_f, run], [pc[0]])
                vop(DVE, lambda: nc.vector.tensor_scalar_add(out=cb[:], in0=pc[0][:, 0:32], scalar1=float(BLK - 1)), [pc[0]], [cb])
                vop(DVE, lambda: nc.vector.tensor_copy(out=ci[:], in_=cb[:]), [cb], [ci])
                sh = int(math.log2(BLK))
                vop(DVE, lambda: nc.vector.tensor_single_scalar(out=ci[:], in_=ci[:], scalar=sh, op=ALU.arith_shift_right), [ci], [ci])
                vop(DVE, lambda: nc.vector.tensor_single_scalar(out=ci[:], in_=ci[:], scalar=sh, op=ALU.logical_shift_left), [ci], [ci])
                vop(DVE, lambda: nc.vector.tensor_copy(out=cb[:], in_=ci[:]), [ci], [cb])
                vop(DVE, lambda: nc.vector.tensor_copy(out=inc[0][:], in_=cb[:]), [cb], [inc[0]])
                cur = 0
                s = 1
                while s < 32:
                    a_, b_ = inc[cur], inc[1 - cur]
                    vop(DVE, lambda: nc.vector.tensor_copy(out=b_[:, 0:s], in_=a_[:, 0:s]), [a_], [b_])
                    vop(DVE, lambda: nc.vector.tensor_tensor(out=b_[:, s:32], in0=a_[:, s:32], in1=a_[:, 0:32 - s], op=ALU.add), [a_], [b_])
                    cur = 1 - cur
                    s *= 2
                vop(DVE, lambda: nc.vector.tensor_copy(out=pend[:], in_=inc[cur][:]), [inc[cur]], [pend])
                vop(DVE, lambda: nc.vector.tensor_tensor(out=pstart[:], in0=pend[:], in1=cb[:], op=ALU.subtract), [pend, cb], [pstart])
                for i in range(NT):
                    p_ = pc[i % 2]
                    mm(p_[:, 0:32], tri_f[:], e01[:, i, :], True, False, [tri_f, e01], [p_], sig=False)
                    mm(p_[:, 0:32], ones_f[:], run[:, i, :], False, True, [ones_f, run], [p_])
                    t_ = tmp[i % 2]
                    u_ = tmp2[i % 2]
                    vop(DVE, lambda: nc.vector.tensor_tensor(out=t_[:], in0=p_[:, 0:32], in1=pstart[:], op=ALU.add), [p_, pstart], [t_])
                    for k, EE in enumerate((E0, E1)):
                        vop(POOL, lambda: nc.gpsimd.tensor_tensor(out=u_[:], in0=t_[:], in1=EE[:, i, :], op=ALU.mult), [t_, EE], [u_])
                        vop(DVE, lambda: nc.vector.reduce_sum(out=destf[:, i, k:k + 1], in_=u_[:], axis=AX.X), [u_], [destf])
                vop(DVE, lambda: nc.vector.tensor_copy(out=dest_i[:], in_=destf[:]), [destf], [dest_i])
                vop(DVE, lambda: nc.vector.memset(bex[:], 0.0), [], [bex])
                for e in range(NEXP):
                    vop(DVE, lambda: nc.vector.scalar_tensor_tensor(out=bex[:], in0=misc[:, 1:1 + NB], scalar=pend[:, e:e + 1], in1=bex[:], op0=ALU.is_ge, op1=ALU.add),
                        [misc, pend, bex], [bex])
                vop(DVE, lambda: nc.vector.tensor_scalar(out=bex[:], in0=bex[:], scalar1=float(NEXP - 1), scalar2=float(l * NEXP), op0=ALU.min, op1=ALU.add), [bex], [bex])
                vop(DVE, lambda: nc.vector.tensor_scalar(out=bex[:], in0=bex[:], scalar1=128.0, scalar2=misc[:, 0:1], op0=ALU.mult, op1=ALU.add), [bex, misc], [bex])
                vop(DVE, lambda: nc.vector.tensor_copy(out=widx[:], in_=bex[:]), [bex], [widx])
                kb.barrier()
                xb = [sb(ph, "p7_x%d" % i, [128, D], BF16) for i in range(4)]
                for i in range(NT):
                    b_ = xb[i % 4]
                    dma(QS, b_[:], x1b[i * 128:(i + 1) * 128, :], [], [b_])
                    for k in range(2):
                        kb.dma(QP, lambda: nc.gpsimd.indirect_dma_start(out=xs, out_offset=bass.IndirectOffsetOnAxis(ap=dest_i[:, i, k:k + 1], axis=0),
                                                                         in_=b_[:], in_offset=None), R=rs(b_, dest_i), W=[DR("xs")])
                kb.barrier()

        def phase_experts(l):
            w1v = w1.rearrange("l e (p j) c -> (l e p) (j c)", j=8)
            w3v = w3.rearrange("l e (p j) c -> (l e p) (j c)", j=8)
            w2v = w2.rearrange("l e (p j) c -> (l e p) (j c)", j=4)
            with ExitStack() as ph:
                W1 = [sb(ph, "p8_w1%d" % i, [128, 8, DE], BF16) for i in range(2)]
                W3 = [sb(ph, "p8_w3%d" % i, [128, 8, DE], BF16) for i in range(2)]
                W2 = [sb(ph, "p8_w2%d" % i, [128, 4, D], BF16) for i in range(2)]
                xt = [sb(ph, "p8_x%d" % i, [128, 4, D], BF16) for i in range(2)]
                xsT = [sb(ph, "p8_xT%d" % i, [128, 8, 512], BF16) for i in range(2)]
                hT = [sb(ph, "p8_h%d" % i, [128, 4, 512], BF16) for i in range(2)]
                sl = [sb(ph, "p8_sl%d" % i, [128, 512], F32) for i in range(2)]
                yo = [sb(ph, "p8_y%d" % i, [128, D], F32) for i in range(2)]
                ptr = [pst(ph, "p8_pt%d" % i, [128, 1024], BF16) for i in range(2)]
                p1 = [pst(ph, "p8_p1%d" % i, [128, 512]) for i in range(1)]
                p3 = [pst(ph, "p8_p3%d" % i, [128, 512]) for i in range(1)]
                py = [pst(ph, "p8_py%d" % i, [128, 1024]) for i in range(2)]

                def load(b):
                    i = b % 2
                    for (Wt, src) in ((W1[i], w1v), (W3[i], w3v), (W2[i], w2v)):
                        kb.dma(QP, lambda: nc.gpsimd.indirect_dma_start(out=Wt[:].rearrange("p a c -> p (a c)"), out_offset=None, in_=src,
                                                                         in_offset=bass.IndirectOffsetOnAxis(ap=widx[:, b:b + 1], axis=0)), R=rs(widx), W=rs(Wt))
                    dma(QS, xt[i][:], xs[b * BLK:(b + 1) * BLK, :].rearrange("(a p) d -> p a d", p=128), [DR("xs")], [xt[i]])

                load(0)
                for b in range(NB):
                    if b + 1 < NB:
                        load(b + 1)
                    i = b % 2
                    x_, xT_, h_ = xt[i], xsT[i], hT[i]
                    for a in range(4):
                        ps = ptr[a % 2]
                        for j in range(8):
                            tr(ps[:, j * 128:(j + 1) * 128], x_[:, a, :].rearrange("p (q j) -> p j q", j=8)[:, j, :], ident_b[:], [x_, ident_b], [ps], sig=(j == 7))
                        vop(DVE, lambda: nc.vector.tensor_copy(out=xT_[:, :, a * 128:(a + 1) * 128], in_=ps[:].rearrange("p (j t) -> p j t", j=8)), [ps], [xT_])
                    for jp in range(4):
                        w1s = W1[i][:].rearrange("p j (q f) -> p j f q", f=4)
                        w3s = W3[i][:].rearrange("p j (q f) -> p j f q", f=4)
                        for j in range(8):
                            mm(p1[0][:], w1s[:, j, jp, :], xT_[:, j, :], j == 0, j == 7, [W1[i], xT_], [p1[0]])
                        for j in range(8):
                            mm(p3[0][:], w3s[:, j, jp, :], xT_[:, j, :], j == 0, j == 7, [W3[i], xT_], [p3[0]])
                        s_ = sl[jp % 2]
                        vop(ACT, lambda: nc.scalar.activation(out=s_[:], in_=p1[0][:], func=AF.Silu), [p1[0]], [s_])
                        vop(DVE, lambda: nc.vector.tensor_tensor(out=h_[:, jp, :], in0=p3[0][:], in1=s_[:], op=ALU.mult), [p3[0], s_], [h_])
                    for a in range(4):
                        p_ = py[a % 2]
                        for hf in range(2):
                            for jp in range(4):
                                mm(p_[:, hf * 512:(hf + 1) * 512], h_[:, jp, a * 128:(a + 1) * 128], W2[i][:, jp, hf * 512:(hf + 1) * 512], jp == 0, jp == 3, [h_, W2[i]], [p_])
                        y_ = yo[a % 2]
                        vop(ACT, lambda: nc.scalar.copy(out=y_[:], in_=p_[:]), [p_], [y_])
                        r0 = b * BLK + a * 128
                        dma(QS, ys[r0:r0 + 128, :], y_[:], [y_], [DR("ys")])
                kb.barrier()

        def phase_ln2(l, dst, make_xT):
            with ExitStack() as ph:
                gb = sb(ph, "p9_g", [128, D], F32)
                bb = sb(ph, "p9_b", [128, D], F32)
                dma(QS, gb[:], ln2_g[l:l + 1, :].partition_broadcast(128), [], [gb])
                dma(QS, bb[:], ln2_b[l:l + 1, :].partition_broadcast(128), [], [bb])
                x1t = [sb(ph, "p9_x%d" % i, [128, D], F32) for i in range(2)]
                y0 = [sb(ph, "p9_y0%d" % i, [128, D], F32) for i in range(2)]
                y1 = [sb(ph, "p9_y1%d" % i, [128, D], F32) for i in range(2)]
                z = [sb(ph, "p9_z%d" % i, [128, D], F32) for i in range(3)]
                zb = [sb(ph, "p9_zb%d" % i, [128, D], BF16) for i in range(2)]
                junk = sb(ph, "p9_junk", [128, D], F32)
                st4 = [sb(ph, "p9_st%d" % i, [128, 8], F32) for i in range(2)]
                xTs = [sb(ph, "p9_t%d" % i, [128, 8, 512], BF16) for i in range(2)]
                pss = [pst(ph, "p9_ps%d" % i, [128, 1024], BF16) for i in range(2)]

                def load(i):
                    dma(QS, x1t[i % 2][:], x1d[i * 128:(i + 1) * 128, :], [], [x1t[i % 2]])
                    for k, yy in enumerate((y0, y1)):
                        kb.dma(QP, lambda: nc.gpsimd.indirect_dma_start(out=yy[i % 2][:], out_offset=None, in_=ys,
                                                                         in_offset=bass.IndirectOffsetOnAxis(ap=dest_i[:, i, k:k + 1], axis=0)), R=rs(dest_i), W=rs(yy[i % 2]))

                load(0)
                for i in range(NT):
                    if i + 1 < NT:
                        load(i + 1)
                    c, a = divmod(i, 4)
                    z_ = z[i % 3]
                    vop(DVE, lambda: nc.vector.tensor_scalar_mul(out=z_[:], in0=y0[i % 2][:], scalar1=wts[:, i, 0:1]), [y0[i % 2], wts], [z_])
                    vop(DVE, lambda: nc.vector.scalar_tensor_tensor(out=z_[:], in0=y1[i % 2][:], scalar=wts[:, i, 1:2], in1=z_[:], op0=ALU.mult, op1=ALU.add), [y1[i % 2], wts, z_], [z_])
                    vop(DVE, lambda: nc.vector.scalar_tensor_tensor(out=z_[:], in0=x1t[i % 2][:], scalar=DN_ALPHA, in1=z_[:], op0=ALU.mult, op1=ALU.add), [x1t[i % 2], z_], [z_])
                    layer_norm(z_, gb, bb, st4[i % 2], junk)
                    dma(QS, dst[i * 128:(i + 1) * 128, :], z_[:], [z_], [DR("x", c)])
                    if make_xT:
                        zb_ = zb[i % 2]
                        vop(ACT, lambda: nc.scalar.copy(out=zb_[:], in_=z_[:]), [z_], [zb_])
                        emit_xT_tile(pss[i % 2], zb_, a, xTs[c % 2])
                        if a == 3:
                            dma(QS, xT[:, :, c * 512:(c + 1) * 512].rearrange("j p t -> p j t"), xTs[c % 2][:], [xTs[c % 2]], [DR("xT", c)])
                kb.barrier()

        phase_xprep(x_in)
        for l in range(depth):
            xsrc = x_in if l == 0 else xa
            last = (l == depth - 1)
            if stop_after == "xprep":
                break
            phase_proj(l)
            if stop_after == "proj":
                break
            phase_full_attn(l)
            phase_dilated(l)
            if stop_after == "attn":
                break
            phase_out_ln1(l, xsrc)
            if stop_after == "ln1":
                break
            phase_route(l)
            phase_experts(l)
            phase_ln2(l, y_out if last else xa, not last)
        kb.barrier()
        stats = dict(pe=kb.pe.n, act=kb.act.n, dve=kb.dve.n, pool=kb.pool.n, qs=kb.qs.k, qp=kb.qp.k, qs_max=max(kb.qs.cnt), qp_max=max(kb.qp.cnt))
    build.stats = stats
    return nc


WNAMES = ["w_in", "diff_lambda", "diff_subln", "mla_q_norm", "mla_w_uq", "mla_kv_norm", "mla_w_ukv", "w_out", "ln1_g", "ln1_b",
          "moe_w_coarse", "moe_w_fine", "moe_w1", "moe_w3", "moe_w2", "ln2_g", "ln2_b"]


def kernel(x_prompt, x_sample, **w):
    x_prompt = np.asarray(x_prompt, dtype=np.float32)
    x_sample = np.asarray(x_sample, dtype=np.float32)
    n = 8
    seqs = [2048, 2048, 4096]
    nc = build(seqs, DEPTH)
    T = sum(seqs)
    NB = -(-(2 * T + NEXP * (BLK - 1)) // BLK)
    consts = host_consts(max(seqs), NB)
    wd = {k: np.ascontiguousarray(np.asarray(w[k], dtype=np.float32)) for k in WNAMES}
    in_maps = []
    for c in range(n):
        xc = np.concatenate([x_prompt[2 * c], x_prompt[2 * c + 1], x_sample[c]], axis=0)
        m = {"x": np.ascontiguousarray(xc)}
        m.update(wd)
        m.update(consts)
        in_maps.append(m)
    res = run_bass_kernel_spmd(nc, in_maps, core_ids=list(range(n)))
    yp = np.empty_like(x_prompt)
    ysm = np.empty_like(x_sample)
    for c in range(n):
        y = res.results[c]["y"]
        yp[2 * c] = y[0:2048]
        yp[2 * c + 1] = y[2048:4096]
        ysm[c] = y[4096:8192]
    return (yp, ysm)
```

```python
import math
import os
from contextlib import ExitStack

import numpy as np
import concourse.bass as bass
import concourse.mybir as mybir
from concourse.bass_utils import run_bass_kernel_spmd

F32 = mybir.dt.float32
BF16 = mybir.dt.bfloat16
I32 = mybir.dt.int32
ALU = mybir.AluOpType
AF = mybir.ActivationFunctionType
AX = mybir.AxisListType

D = 1024
DEPTH = 4
IN_COLS = 2336
COLB = 768
COLKV = 1024
COLKR = 1152
COLC = 1184
NEXP = 32
DE = 512
LN_EPS = 1e-5
RMS_EPS = 1e-6
DN_ALPHA = (2 * DEPTH) ** 0.25
BLK = 512
C_R = (1, 4, 16)


class Res:
    __slots__ = ("w", "r", "ep")

    def __init__(self):
        self.w = None
        self.r = {}
        self.ep = -1


class Eng:
    def __init__(self, e, sem, is_pe=False):
        self.e = e
        self.sem = sem
        self.n = 0
        self.seen = {}
        self.is_pe = is_pe


class DQ:
    def __init__(self, eng, sems):
        self.eng = eng
        self.sems = sems
        self.cnt = [0] * len(sems)
        self.k = 0


class KB:
    def __init__(self, nc, es, nq=22):
        self.nc = nc
        self.ep = 0
        mk = lambda n: es.enter_context(nc.semaphore(n))
        self.pe = Eng(nc.tensor, mk("s_pe"), True)
        self.act = Eng(nc.scalar, mk("s_act"))
        self.dve = Eng(nc.vector, mk("s_dve"))
        self.pool = Eng(nc.gpsimd, mk("s_pool"))
        self.sp = Eng(nc.sync, None)
        self.engs = [self.pe, self.act, self.dve, self.pool]
        self.qs = DQ(self.sp, [mk("q_s%d" % i) for i in range(16)])
        self.qp = DQ(self.pool, [mk("q_p%d" % i) for i in range(6)])
        self.queues = [self.qs, self.qp]

    def _fresh(self, b):
        if b.ep != self.ep:
            b.w = None
            b.r = {}
            b.ep = self.ep

    def wait(self, eng, sem, val):
        if val > 0 and eng.seen.get(id(sem), 0) < val:
            eng.e.wait_ge(sem, val)
            eng.seen[id(sem)] = val

    def deps(self, eng, R, W, is_dma):
        for b in R:
            self._fresh(b)
            if b.w is not None:
                sem, val = b.w
                if sem is eng.sem and not is_dma and eng.is_pe:
                    continue
                self.wait(eng, sem, val)
        for b in W:
            self._fresh(b)
            if b.w is not None:
                sem, val = b.w
                if not (sem is eng.sem and not is_dma):
                    self.wait(eng, sem, val)
            for sem, val in b.r.values():
                if sem is eng.sem and not is_dma:
                    continue
                self.wait(eng, sem, val)

    def _mark(self, tok, R, W):
        sem, val = tok
        for b in R:
            old = b.r.get(id(sem))
            if old is None or old[1] < val:
                b.r[id(sem)] = (sem, val)
        for b in W:
            b.w = tok
            b.r = {}

    def op(self, eng, fn, R=(), W=(), sig=True):
        self.deps(eng, R, W, False)
        ins = fn()
        if sig:
            eng.n += 1
            ins.then_inc(eng.sem, 1)
            tok = (eng.sem, eng.n)
        else:
            tok = (eng.sem, eng.n + 1)
        self._mark(tok, R, W)
        return ins

    def dma(self, q, fn, R=(), W=()):
        eng = q.eng
        self.deps(eng, R, W, True)
        i = q.k % len(q.sems)
        q.k += 1
        sem = q.sems[i]
        self.wait(eng, sem, q.cnt[i])
        ins = fn()
        ins.then_inc(sem, 16)
        q.cnt[i] += 16
        self._mark((sem, q.cnt[i]), R, W)
        return ins

    def barrier(self):
        for E in self.engs + [self.sp]:
            for X in self.engs:
                if X is not E:
                    self.wait(E, X.sem, X.n)
            for q in self.queues:
                for i, sem in enumerate(q.sems):
                    self.wait(E, sem, q.cnt[i])
        self.ep += 1


class Tl:
    def __init__(self, t):
        self.t = t
        self.res = Res()

    def __getitem__(self, k):
        return self.t[k]


def host_consts(smax, nb):
    def rope(dim):
        inv = (1.0 / (np.float32(10000.0) ** (np.arange(0, dim, 2, dtype=np.float32) / np.float32(dim)))).astype(np.float32)
        ang = (np.arange(smax, dtype=np.float32)[:, None] * inv[None, :]).astype(np.float32)
        return np.cos(ang).astype(np.float32), np.sin(ang).astype(np.float32)

    ca, sa = rope(32)
    cc, sc = rope(64)
    p = np.arange(128)
    cosA = ca[:, p % 16].T.copy()
    sinA = sa[:, p % 16].T.copy()
    cosC = cc[:, p % 32].T.copy()
    sinC = sc[:, p % 32].T.copy()
    cosB = np.ones((128, smax), np.float32)
    sinB = np.zeros((128, smax), np.float32)
    cosB[64:96] = cosA[0:32]
    sinB[64:96] = sinA[0:32]
    k = np.arange(128)[:, None]
    q = np.arange(128)[None, :]
    band = np.stack([((k - 128 - q) >= -64), (np.abs(k - q) <= 64), ((k + 128 - q) <= 64)], axis=1).astype(np.float32)
    misc = np.zeros((128, 512), np.float32)
    misc[:, 0] = np.arange(128)
    misc[:, 1:1 + nb] = (np.arange(nb) * BLK)[None, :]
    tri = (np.arange(128)[:, None] < np.arange(128)[None, :]).astype(np.float32)
    return {
        "c_ident": np.eye(128, dtype=np.float32), "c_cosA": cosA, "c_sinA": sinA, "c_cosC": cosC, "c_sinC": sinC,
        "c_cosB": cosB, "c_sinB": sinB, "c_band": band.reshape(128, 384).copy(), "c_misc": misc, "c_tri": tri,
    }


def build(seqs, depth, wdepth=DEPTH, stop_after=None, dbg=()):
    T = sum(seqs)
    NT = T // 128
    NCH = T // 512
    NB = -(-(2 * T + NEXP * (BLK - 1)) // BLK)
    NSLOT = NB * BLK
    SMAX = max(seqs)
    seq_of_chunk = []
    t0 = 0
    seq_starts = []
    for S in seqs:
        seq_starts.append(t0)
        for c in range(S // 512):
            seq_of_chunk.append((t0, S, c * 512))
        t0 += S

    nc = bass.Bass("TRN2", target_bir_lowering=False)
    dt_in = lambda n, s, d=F32: nc.dram_tensor(n, list(s), d, kind="ExternalInput").ap()
    dt_sc = lambda n, s, d: nc.dram_tensor(n, list(s), d, kind=("ExternalOutput" if n in dbg else "Internal")).ap()
    x_in = dt_in("x", [T, D])
    w_in = dt_in("w_in", [wdepth, D, IN_COLS])
    diff_lambda = dt_in("diff_lambda", [wdepth, 4, 32])
    diff_subln = dt_in("diff_subln", [wdepth, 64])
    mla_q_norm = dt_in("mla_q_norm", [wdepth, 256])
    mla_w_uq = dt_in("mla_w_uq", [wdepth, 256, 576])
    mla_kv_norm = dt_in("mla_kv_norm", [wdepth, 128])
    mla_w_ukv = dt_in("mla_w_ukv", [wdepth, 128, 768])
    w_out = dt_in("w_out", [wdepth, D, D])
    ln1_g = dt_in("ln1_g", [wdepth, D])
    ln1_b = dt_in("ln1_b", [wdepth, D])
    w_coarse = dt_in("moe_w_coarse", [wdepth, D, 4])
    w_fine = dt_in("moe_w_fine", [wdepth, 4, D, 8])
    w1 = dt_in("moe_w1", [wdepth, NEXP, D, DE])
    w3 = dt_in("moe_w3", [wdepth, NEXP, D, DE])
    w2 = dt_in("moe_w2", [wdepth, NEXP, DE, D])
    ln2_g = dt_in("ln2_g", [wdepth, D])
    ln2_b = dt_in("ln2_b", [wdepth, D])
    c_ident = dt_in("c_ident", [128, 128])
    c_cos = {"A": dt_in("c_cosA", [128, SMAX]), "C": dt_in("c_cosC", [128, SMAX]), "B": dt_in("c_cosB", [128, SMAX])}
    c_sin = {"A": dt_in("c_sinA", [128, SMAX]), "C": dt_in("c_sinC", [128, SMAX]), "B": dt_in("c_sinB", [128, SMAX])}
    c_band = dt_in("c_band", [128, 384])
    c_misc = dt_in("c_misc", [128, 512])
    c_tri = dt_in("c_tri", [128, 128])
    y_out = nc.dram_tensor("y", [T, D], F32, kind="ExternalOutput").ap()

    xa = dt_sc("s_xa", [T, D], F32)
    x1d = dt_sc("s_x1", [T, D], F32)
    x1b = dt_sc("s_x1b", [T, D], BF16)
    xT = dt_sc("s_xT", [8, 128, T], BF16)
    mixT = dt_sc("s_mixT", [8, 128, T], BF16)
    QKA = dt_sc("s_qka", [4, 128, T], BF16)
    QKC = dt_sc("s_qkc", [6, 128, T], BF16)
    QB = dt_sc("s_qb", [6, 96, T], BF16)
    KBd = dt_sc("s_kb", [6, 96, T], BF16)
    VA = dt_sc("s_va", [T, 256], BF16)
    VB = dt_sc("s_vb", [T, 384], BF16)
    VC = dt_sc("s_vc", [T, 384], BF16)
    NCd = dt_sc("s_nc", [6, 64, T], BF16)
    DCd = dt_sc("s_dc", [6, 64, T], F32)
    xs = dt_sc("s_xs", [NSLOT, D], BF16)
    ys = dt_sc("s_ys", [NSLOT, D], F32)

    es = ExitStack()
    with es:
        kb = KB(nc, es)
        PE, ACT, DVE, POOL = kb.pe, kb.act, kb.dve, kb.pool
        QS, QP = kb.qs, kb.qp
        es.enter_context(nc.Block())
        dres = {}

        def DR(*key):
            r = dres.get(key)
            if r is None:
                r = dres[key] = Res()
            return r

        uid = [0]

        def sb(stack, name, shape, dtype):
            uid[0] += 1
            return Tl(stack.enter_context(nc.sbuf_tensor("%s_%d" % (name, uid[0]), list(shape), dtype)))

        def pst(stack, name, shape, dtype=F32):
            uid[0] += 1
            return Tl(stack.enter_context(nc.psum_tensor("%s_%d" % (name, uid[0]), list(shape), dtype)))

        def rs(*ts):
            return [t.res if isinstance(t, Tl) else t for t in ts]

        def mm(out, lhsT, rhs, start, stop, R, W, sig=None):
            kb.op(PE, lambda: nc.tensor.matmul(out, lhsT=lhsT, rhs=rhs, start=start, stop=stop), R=rs(*R), W=rs(*W),
                  sig=(stop if sig is None else sig))

        def tr(out, in_, ident, R, W, sig=True):
            kb.op(PE, lambda: nc.tensor.transpose(out, in_, ident), R=rs(*R), W=rs(*W), sig=sig)

        def vop(eng, fn, R, W):
            kb.op(eng, fn, R=rs(*R), W=rs(*W))

        def dma(q, out, in_, R, W):
            kb.dma(q, lambda: q.eng.e.dma_start(out=out, in_=in_), R=rs(*R), W=rs(*W))

        ident_f = sb(es, "ident_f", [128, 128], F32)
        ident_b = sb(es, "ident_b", [128, 128], BF16)
        ones_f = sb(es, "ones_f", [128, 128], F32)
        ones_b = sb(es, "ones_b", [128, 128], BF16)
        tri_f = sb(es, "tri_f", [128, 128], F32)
        misc = sb(es, "misc", [128, 512], F32)
        epsr = sb(es, "epsr", [128, 2], F32)
        dma(QS, ident_f[:], c_ident, [], [ident_f])
        dma(QP, ident_b[:], c_ident, [], [ident_b])
        dma(QS, tri_f[:], c_tri, [], [tri_f])
        dma(QS, misc[:], c_misc, [], [misc])
        vop(DVE, lambda: nc.vector.memset(ones_f[:], 1.0), [], [ones_f])
        vop(DVE, lambda: nc.vector.memset(ones_b[:], 1.0), [], [ones_b])
        vop(DVE, lambda: nc.vector.memset(epsr[:, 0:1], LN_EPS), [], [epsr])
        vop(DVE, lambda: nc.vector.memset(epsr[:, 1:2], RMS_EPS), [], [epsr])
        E0 = sb(es, "E0", [128, NT, 32], F32)
        E1 = sb(es, "E1", [128, NT, 32], F32)
        wts = sb(es, "wts", [128, NT, 2], F32)
        dest_i = sb(es, "dest_i", [128, NT, 2], I32)
        widx = sb(es, "widx", [128, NB], I32)
        kb.barrier()

        def emit_xT_tile(ps, src_bf, a, xTs):
            for j in range(8):
                tr(ps[:, j * 128:(j + 1) * 128], src_bf[:, j * 128:(j + 1) * 128], ident_b[:], [src_bf, ident_b], [ps], sig=(j == 7))
            vop(ACT, lambda: nc.scalar.copy(out=xTs[:, :, a * 128:(a + 1) * 128], in_=ps[:].rearrange("p (j t) -> p j t", j=8)),
                [ps], [xTs])

        def rstd_from(sums_sq_ap, out_tl, n, eps_col, tmp_tl, R):
            vop(ACT, lambda: nc.scalar.activation(out=tmp_tl, in_=sums_sq_ap, func=AF.Sqrt, bias=epsr[0:tmp_tl.shape[0], eps_col:eps_col + 1], scale=1.0 / n),
                R + [epsr], [])
            return None

        def phase_xprep(src):
            with ExitStack() as ph:
                xin = [sb(ph, "p0_x%d" % i, [128, 4, D], F32) for i in range(2)]
                xbf = [sb(ph, "p0_b%d" % i, [128, D], BF16) for i in range(2)]
                xTs = [sb(ph, "p0_t%d" % i, [128, 8, 512], BF16) for i in range(2)]
                pss = [pst(ph, "p0_ps%d" % i, [128, 1024], BF16) for i in range(2)]
                for c in range(NCH):
                    xi = xin[c % 2]
                    dma(QS, xi[:], src[c * 512:(c + 1) * 512, :].rearrange("(a p) d -> p a d", p=128), [DR("x", c)], [xi])
                    xt = xTs[c % 2]
                    for a in range(4):
                        xb = xbf[a % 2]
                        vop(POOL, lambda: nc.gpsimd.tensor_copy(out=xb[:], in_=xi[:, a, :]), [xi], [xb])
                        emit_xT_tile(pss[a % 2], xb, a, xt)
                    dma(QS, xT[:, :, c * 512:(c + 1) * 512].rearrange("j p t -> p j t"), xt[:], [xt], [DR("xT", c)])
                kb.barrier()

        def phase_proj(l):
            with ExitStack() as ph:
                win = sb(ph, "p1_win", [128, 8, IN_COLS], BF16)
                wrot = sb(ph, "p1_wrot", [128, 8, 1312], BF16)
                wuq = sb(ph, "p1_wuq", [128, 2, 576], BF16)
                wuqr = sb(ph, "p1_wuqr", [128, 2, 576], BF16)
                wukv = sb(ph, "p1_wukv", [128, 768], BF16)
                wukv_v = sb(ph, "p1_wukvv", [128, 384], BF16)
                stg = sb(ph, "p1_stg", [128, 2, 768], F32)
                gq = sb(ph, "p1_gq", [128, 3], F32)
                for j in range(8):
                    dma(QP, win[:, j, :], w_in[l, j * 128:(j + 1) * 128, :], [], [win])
                def rot(dst_lo, src_lo, ncols, half):
                    dv = wrot[:, :, dst_lo:dst_lo + ncols].rearrange("p j (b two h) -> p j b two h", two=2, h=half)
                    sv = win[:, :, src_lo:src_lo + ncols].rearrange("p j (b two h) -> p j b two h", two=2, h=half)
                    vop(DVE, lambda: nc.vector.tensor_scalar_mul(out=dv[:, :, :, 0, :], in0=sv[:, :, :, 1, :], scalar1=-1.0), [win], [wrot])
                    vop(DVE, lambda: nc.vector.tensor_copy(out=dv[:, :, :, 1, :], in_=sv[:, :, :, 0, :]), [win], [wrot])
                rot(0, 0, 512, 16)
                rot(512, COLKR, 32, 16)
                rot(544, COLC, 768, 32)
                for j in range(2):
                    dma(QS, gq[:, j:j + 1], mla_q_norm[l, j * 128:(j + 1) * 128].rearrange("(p o) -> p o", o=1), [], [gq])
                dma(QS, gq[:, 2:3], mla_kv_norm[l].rearrange("(p o) -> p o", o=1), [], [gq])
                dma(QS, stg[:, :, 0:576], mla_w_uq[l].rearrange("(j p) c -> p j c", p=128), [], [stg])
                for j in range(2):
                    vop(DVE, lambda: nc.vector.tensor_scalar_mul(out=wuq[:, j, :], in0=stg[:, j, 0:576], scalar1=gq[:, j:j + 1]), [stg, gq], [wuq])
                vop(DVE, lambda: nc.vector.memset(wuqr[:], 0.0), [], [wuqr])
                dv = wuqr[:].rearrange("p j (h c) -> p j h c", c=96)[:, :, :, 64:96].rearrange("p j h (two f) -> p j h two f", two=2)
                sv = wuq[:].rearrange("p j (h c) -> p j h c", c=96)[:, :, :, 64:96].rearrange("p j h (two f) -> p j h two f", two=2)
                vop(DVE, lambda: nc.vector.tensor_scalar_mul(out=dv[:, :, :, 0, :], in0=sv[:, :, :, 1, :], scalar1=-1.0), [wuq], [wuqr])
                vop(DVE, lambda: nc.vector.tensor_copy(out=dv[:, :, :, 1, :], in_=sv[:, :, :, 0, :]), [wuq], [wuqr])
                stg2 = sb(ph, "p1_stg2", [128, 768], F32)
                dma(QS, stg2[:], mla_w_ukv[l], [], [stg2])
                vop(DVE, lambda: nc.vector.tensor_scalar_mul(out=wukv[:], in0=stg2[:], scalar1=gq[:, 2:3]), [stg2, gq], [wukv])
                vop(DVE, lambda: nc.vector.tensor_copy(out=wukv_v[:].rearrange("p (h c) -> p h c", c=64),
                                                       in_=wukv[:].rearrange("p (h c) -> p h c", c=128)[:, :, 64:128]), [wukv], [wukv_v])

                xTc = [sb(ph, "p1_x%d" % i, [128, 8, 512], BF16) for i in range(2)]
                tabs = [{k: (sb(ph, "p1_c%s%d" % (k, i), [128, 512], F32), sb(ph, "p1_s%s%d" % (k, i), [128, 512], F32)) for k in "ACB"} for i in range(2)]
                pq = [pst(ph, "p1_pq%d" % i, [128, 512]) for i in range(2)]
                pr = [pst(ph, "p1_pr%d" % i, [128, 512]) for i in range(2)]
                pv = [pst(ph, "p1_pv%d" % i, [128, 512]) for i in range(2)]
                pm = [pst(ph, "p1_pm%d" % i, [128, 512]) for i in range(2)]
                t1 = [sb(ph, "p1_t1%d" % i, [128, 512], F32) for i in range(2)]
                t2 = [sb(ph, "p1_t2%d" % i, [128, 512], F32) for i in range(2)]
                t3 = [sb(ph, "p1_t3%d" % i, [128, 512], F32) for i in range(2)]
                ob = [sb(ph, "p1_ob%d" % i, [128, 512], BF16) for i in range(4)]
                vo = [sb(ph, "p1_vo%d" % i, [128, 640], BF16) for i in range(2)]
                vbo = [sb(ph, "p1_vbo%d" % i, [128, 384], BF16) for i in range(2)]
                cqT = sb(ph, "p1_cqT", [128, 3, 512], BF16)
                sq = sb(ph, "p1_sq", [128, 3, 512], F32)
                rq = sb(ph, "p1_rq", [128, 512], F32)
                rkv = sb(ph, "p1_rkv", [128, 512], F32)
                kr = sb(ph, "p1_kr", [128, 512], BF16)
                rtok = sb(ph, "p1_rtok", [128, 4], F32)
                cnt = {"g": 0, "o": 0}

                def proj_fm(c, xc, lo, m, rlo):
                    i = cnt["g"] % 2
                    cnt["g"] += 1
                    for j in range(8):
                        mm(pq[i][0:m, :], win[:, j, lo:lo + m], xc[:, j, :], j == 0, j == 7, [win, xc], [pq[i]])
                    if rlo is not None:
                        for j in range(8):
                            mm(pr[i][0:m, :], wrot[:, j, rlo:rlo + m], xc[:, j, :], j == 0, j == 7, [wrot, xc], [pr[i]])
                    return i

                def rope_evac(i, m, tab, out_ap, W, extra=None):
                    ct, st = tab
                    vop(DVE, lambda: nc.vector.tensor_tensor(out=t1[i][0:m, :], in0=pq[i][0:m, :], in1=ct[0:m, :], op=ALU.mult), [pq[i], ct], [t1[i]])
                    vop(DVE, lambda: nc.vector.tensor_tensor(out=t2[i][0:m, :], in0=pr[i][0:m, :], in1=st[0:m, :], op=ALU.mult), [pr[i], st], [t2[i]])
                    if extra is None:
                        vop(POOL, lambda: nc.gpsimd.tensor_tensor(out=out_ap, in0=t1[i][0:m, :], in1=t2[i][0:m, :], op=ALU.add), [t1[i], t2[i]], W)
                    else:
                        vop(POOL, lambda: nc.gpsimd.tensor_tensor(out=t3[i][0:m, :], in0=t1[i][0:m, :], in1=t2[i][0:m, :], op=ALU.add), [t1[i], t2[i]], [t3[i]])
                        vop(POOL, lambda: nc.gpsimd.tensor_tensor(out=out_ap, in0=t3[i][0:m, :], in1=extra[0:m, :], op=ALU.mult), [t3[i], extra], W)

                def next_ob():
                    o = ob[cnt["o"] % 4]
                    cnt["o"] += 1
                    return o

                def load_chunk(c):
                    tk = c * 512
                    s0, S, pos0 = seq_of_chunk[c]
                    xc = xTc[c % 2]
                    dma(QS, xc[:], xT[:, :, tk:tk + 512].rearrange("j p t -> p j t"), [DR("xT", c)], [xc])
                    tb = tabs[c % 2]
                    for k in "ACB":
                        dma(QS, tb[k][0][:], c_cos[k][:, pos0:pos0 + 512], [], [tb[k][0]])
                        dma(QS, tb[k][1][:], c_sin[k][:, pos0:pos0 + 512], [], [tb[k][1]])

                PSTOP = os.environ.get("PSTOP")
                if PSTOP == "w":
                    kb.barrier()
                    return
                load_chunk(0)
                for c in range(NCH):
                    if c + 1 < NCH:
                        load_chunk(c + 1)
                    tk = c * 512
                    s0, S, pos0 = seq_of_chunk[c]
                    xc = xTc[c % 2]
                    tb = tabs[c % 2]
                    for ga in range(4):
                        i = proj_fm(c, xc, ga * 128, 128, ga * 128)
                        o = next_ob()
                        rope_evac(i, 128, tb["A"], o[:], [o])
                        dma(QS, QKA[ga, :, tk:tk + 512], o[:], [o], [DR("qka", ga, c)])
                    if PSTOP == "A":
                        kb.barrier()
                        return
                    for gc in range(6):
                        r = C_R[gc % 3]
                        L = S // r
                        i = proj_fm(c, xc, COLC + gc * 128, 128, 544 + gc * 128)
                        o = next_ob()
                        ct, st = tb["C"]
                        vop(DVE, lambda: nc.vector.tensor_tensor(out=t1[i][:], in0=pq[i][:], in1=ct[:], op=ALU.mult), [pq[i], ct], [t1[i]])
                        vop(DVE, lambda: nc.vector.tensor_tensor(out=t2[i][:], in0=pr[i][:], in1=st[:], op=ALU.mult), [pr[i], st], [t2[i]])
                        vop(POOL, lambda: nc.gpsimd.tensor_tensor(out=o[:], in0=t1[i][:], in1=t2[i][:], op=ALU.add), [t1[i], t2[i]], [o])
                        dma(QS, QKC[gc, :, tk:tk + 512], o[:], [o], [DR("qkc", gc, c)])
                    if PSTOP == "C":
                        kb.barrier()
                        return
                    for g in range(3):
                        i = proj_fm(c, xc, COLB + g * 128, 128, None)
                        vop(ACT, lambda: nc.scalar.copy(out=cqT[:, g, :], in_=pq[i][:]), [pq[i]], [cqT])
                        vop(ACT, lambda: nc.scalar.activation(out=sq[:, g, :], in_=pq[i][:], func=AF.Square), [pq[i]], [sq])
                    i = proj_fm(c, xc, COLKR, 32, 512)
                    rope_evac(i, 32, tb["A"], kr[0:32, :], [kr])
                    for j in range(2):
                        mm(pm[0][:], ones_f[:], sq[:, j, :], j == 0, j == 1, [ones_f, sq], [pm[0]])
                    vop(ACT, lambda: nc.scalar.activation(out=t3[0][:], in_=pm[0][:], func=AF.Sqrt, bias=epsr[:, 1:2], scale=1.0 / 256), [pm[0], epsr], [t3[0]])
                    vop(DVE, lambda: nc.vector.reciprocal(out=rq[:], in_=t3[0][:]), [t3[0]], [rq])
                    mm(pm[1][:], ones_f[:], sq[:, 2, :], True, True, [ones_f, sq], [pm[1]])
                    vop(ACT, lambda: nc.scalar.activation(out=t3[1][:], in_=pm[1][:], func=AF.Sqrt, bias=epsr[:, 1:2], scale=1.0 / 128), [pm[1], epsr], [t3[1]])
                    vop(DVE, lambda: nc.vector.reciprocal(out=rkv[:], in_=t3[1][:]), [t3[1]], [rkv])
                    if PSTOP == "Bs":
                        kb.barrier()
                        return
                    for h in range(6):
                        i = cnt["g"] % 2
                        cnt["g"] += 1
                        for j in range(2):
                            mm(pq[i][0:96, :], wuq[:, j, h * 96:(h + 1) * 96], cqT[:, j, :], j == 0, j == 1, [wuq, cqT], [pq[i]])
                        for j in range(2):
                            mm(pr[i][0:96, :], wuqr[:, j, h * 96:(h + 1) * 96], cqT[:, j, :], j == 0, j == 1, [wuqr, cqT], [pr[i]])
                        o = next_ob()
                        rope_evac(i, 96, tb["B"], o[0:96, :], [o], extra=rq)
                        dma(QS, QB[h, :, tk:tk + 512], o[0:96, :], [o], [DR("qb", h, c)])
                        i = cnt["g"] % 2
                        cnt["g"] += 1
                        mm(pq[i][0:64, :], wukv[:, h * 128:h * 128 + 64], cqT[:, 2, :], True, True, [wukv, cqT], [pq[i]])
                        o = next_ob()
                        vop(DVE, lambda: nc.vector.tensor_tensor(out=o[0:64, :], in0=pq[i][0:64, :], in1=rkv[0:64, :], op=ALU.mult), [pq[i], rkv], [o])
                        vop(ACT, lambda: nc.scalar.copy(out=o[64:96, :], in_=kr[0:32, :]), [kr], [o])
                        dma(QS, KBd[h, :, tk:tk + 512], o[0:96, :], [o], [DR("kb", h, c)])
                    if PSTOP == "Bh":
                        kb.barrier()
                        return
                    for a in range(4):
                        i = a % 2
                        tsl = slice(a * 128, (a + 1) * 128)
                        for j in range(8):
                            mm(pv[i][:, 0:256], xc[:, j, tsl], win[:, j, 512:768], j == 0, j == 7, [xc, win], [pv[i]])
                        for j in range(8):
                            mm(pm[i][:, 0:384], xc[:, j, tsl], win[:, j, COLC + 768:COLC + 1152], j == 0, j == 7, [xc, win], [pm[i]])
                        vop(ACT, lambda: nc.scalar.copy(out=vo[i][:, 0:256], in_=pv[i][:, 0:256]), [pv[i]], [vo[i]])
                        vop(ACT, lambda: nc.scalar.copy(out=vo[i][:, 256:640], in_=pm[i][:, 0:384]), [pm[i]], [vo[i]])
                        dma(QS, VA[tk + a * 128:tk + (a + 1) * 128, :], vo[i][:, 0:256], [vo[i]], [DR("va", c)])
                        dma(QS, VC[tk + a * 128:tk + (a + 1) * 128, :], vo[i][:, 256:640], [vo[i]], [DR("vc", c)])
                        mm(pv[i][:, 256:272], sq[:, 2, tsl], ones_f[:, 0:16], True, True, [sq, ones_f], [pv[i]])
                        vop(ACT, lambda: nc.scalar.activation(out=rtok[:, 2 * i:2 * i + 1], in_=pv[i][:, 256:257], func=AF.Sqrt, bias=epsr[:, 1:2], scale=1.0 / 128),
                            [pv[i], epsr], [rtok])
                        vop(DVE, lambda: nc.vector.reciprocal(out=rtok[:, 2 * i + 1:2 * i + 2], in_=rtok[:, 2 * i:2 * i + 1]), [rtok], [rtok])
                        mm(pm[i][:, 0:384], cqT[:, 2, tsl], wukv_v[:], True, True, [cqT, wukv_v], [pm[i]])
                        vop(DVE, lambda: nc.vector.tensor_scalar_mul(out=vbo[i][:], in0=pm[i][:, 0:384], scalar1=rtok[:, 2 * i + 1:2 * i + 2]), [pm[i], rtok], [vbo[i]])
                        dma(QS, VB[tk + a * 128:tk + (a + 1) * 128, :], vbo[i][:], [vbo[i]], [DR("vb", c)])
                kb.barrier()

        def phase_full_attn(l):
            lam_init = 0.8 - 0.6 * math.exp(-0.3 * l)
            with ExitStack() as ph:
                lv = sb(ph, "p2_lv", [1, 128], F32)
                lw = sb(ph, "p2_lw", [1, 8], F32)
                lp = sb(ph, "p2_lp", [1, 64], F32)
                gs = sb(ph, "p2_gs", [64, 4], F32)
                pl = pst(ph, "p2_pl", [128, 512])
                dma(QS, lv[:], diff_lambda[l:l + 1].rearrange("o a b -> o (a b)"), [], [lv])
                dma(QS, gs[:, 0:1], diff_subln[l].rearrange("(p o) -> p o", o=1), [], [gs])
                lvv = lv[:].rearrange("o (a two b) -> o a two b", two=2, b=32)
                vop(DVE, lambda: nc.vector.tensor_tensor(out=lp[:].rearrange("o (a b) -> o a b", b=32), in0=lvv[:, :, 0, :], in1=lvv[:, :, 1, :], op=ALU.mult), [lv], [lp])
                vop(DVE, lambda: nc.vector.reduce_sum(out=lw[:, 0:2], in_=lp[:].rearrange("o (a b) -> o a b", b=32), axis=AX.X), [lp], [lw])
                vop(ACT, lambda: nc.scalar.activation(out=lw[:, 2:4], in_=lw[:, 0:2], func=AF.Exp), [lw], [lw])
                vop(DVE, lambda: nc.vector.tensor_tensor(out=lw[:, 4:5], in0=lw[:, 3:4], in1=lw[:, 2:3], op=ALU.subtract), [lw], [lw])
                vop(DVE, lambda: nc.vector.tensor_scalar_add(out=lw[:, 5:6], in0=lw[:, 4:5], scalar1=-lam_init), [lw], [lw])
                l16 = sb(ph, "p2_l16", [1, 16], F32)
                vop(DVE, lambda: nc.vector.memset(l16[:], 0.0), [], [l16])
                vop(DVE, lambda: nc.vector.tensor_scalar(out=l16[:], in0=l16[:], scalar1=lw[:, 5:6], scalar2=None, op0=ALU.add), [l16, lw], [l16])
                mm(pl[0:64, 0:16], ones_f[0:1, 0:64], l16[0:1, :], True, True, [ones_f, l16], [pl])
                vop(ACT, lambda: nc.scalar.copy(out=gs[:, 1:2], in_=pl[0:64, 0:1]), [pl], [gs])
                vop(DVE, lambda: nc.vector.tensor_scalar_mul(out=gs[:, 2:3], in0=gs[:, 0:1], scalar1=1.0 - lam_init), [gs], [gs])

                qT = [sb(ph, "p2_q%d" % i, [96, SMAX], BF16) for i in range(4)]
                kT = [sb(ph, "p2_k%d" % i, [96, SMAX], BF16) for i in range(4)]
                vt = [sb(ph, "p2_v%d" % i, [128, SMAX // 128, 128], BF16) for i in range(2)]
                pT = [sb(ph, "p2_p%d" % i, [128, 512], BF16) for i in range(4)]
                psc = [pst(ph, "p2_s%d" % i, [128, 512]) for i in range(3)]
                pac = [pst(ph, "p2_a%d" % i, [128, 512]) for i in range(2)]
                prm = pst(ph, "p2_rm", [128, 512])
                dsh = [sb(ph, "p2_d%d" % i, [64, 512], F32) for i in range(2)]
                oc = [sb(ph, "p2_o%d" % i, [64, 512], F32) for i in range(3)]
                tq = sb(ph, "p2_tq", [64, 512], F32)
                obf = [sb(ph, "p2_ob%d" % i, [64, 512], BF16) for i in range(2)]
                for v in vt:
                    vop(DVE, lambda: nc.vector.memset(v[:, :, 64:128], 1.0), [], [v])
                st = {"s": 0, "p": 0, "a": 0, "o": 0, "qk": 0, "v": 0}
                LA = 2
                groups = []

                def norm_post(acc):
                    d = dsh[st["o"] % 2]
                    o = oc[st["o"] % 3]
                    st["o"] += 1
                    vop(ACT, lambda: nc.scalar.copy(out=d[:], in_=acc[64:128, :]), [acc], [d])
                    vop(DVE, lambda: nc.vector.reciprocal(out=d[:], in_=d[:]), [d], [d])
                    vop(DVE, lambda: nc.vector.tensor_tensor(out=o[:], in0=acc[0:64, :], in1=d[:], op=ALU.mult), [acc, d], [o])
                    return o

                def make_its(q_, k_, dk, scale, v, S, qc, post):
                    nkt = S // 128
                    acc = pac[st["a"] % 2]
                    st["a"] += 1
                    return [dict(q=q_, k=k_, dk=dk, scale=scale, v=v, kt=kt, qc=qc, acc=acc, first=(kt == 0), last=(kt == nkt - 1),
                                 post=(post if kt == nkt - 1 else None)) for kt in range(nkt)]

                def load_v(src, h, s0, S, v):
                    for k0 in range(0, S // 128, 8):
                        dma(QS, v[:, k0:k0 + 8, 0:64], src[s0 + k0 * 128:s0 + (k0 + 8) * 128, h * 64:(h + 1) * 64].rearrange("(kt p) d -> p kt d", p=128), [DR("vsrc")], [v])

                def load_qk(qd, kd, dk, s0, S, slot):
                    dma(QS, slot[0][0:dk, 0:S], qd[:, s0:s0 + S], [DR("qsrc")], [slot[0]])
                    dma(QS, slot[1][0:dk, 0:S], kd[:, s0:s0 + S], [DR("ksrc")], [slot[1]])

                def take_slots(n):
                    r_ = [(qT[(st["qk"] + c) % 4], kT[(st["qk"] + c) % 4]) for c in range(n)]
                    st["qk"] += n
                    v = vt[st["v"] % 2]
                    st["v"] += 1
                    return r_, v

                sa = 32.0 ** -0.5
                sbq = 96.0 ** -0.5
                for (s0, S) in zip(seq_starts, seqs):
                    for h in range(4):
                        slots, v = take_slots(2)

                        def load(h=h, s0=s0, S=S, slots=slots, v=v):
                            load_v(VA, h, s0, S, v)
                            for cpt in range(2):
                                u = h * 2 + cpt
                                load_qk(QKA[u // 4, (u % 4) * 32:(u % 4) * 32 + 32, :], QKA[2 + u // 4, (u % 4) * 32:(u % 4) * 32 + 32, :], 32, s0, S, slots[cpt])

                        its = []
                        for qc in range(S // 512):
                            hold = {}

                            def post0(acc, hold=hold):
                                hold["o0"] = norm_post(acc)

                            def post1(acc, hold=hold, qc=qc, h=h, s0=s0):
                                o1 = norm_post(acc)
                                o0 = hold["o0"]
                                vop(DVE, lambda: nc.vector.scalar_tensor_tensor(out=o0[:], in0=o1[:], scalar=gs[:, 1:2], in1=o0[:], op0=ALU.mult, op1=ALU.add), [o1, o0, gs], [o0])
                                vop(ACT, lambda: nc.scalar.activation(out=tq[:], in_=o0[:], func=AF.Square), [o0], [tq])
                                mm(prm[0:64, :], ones_f[0:64, 0:64], tq[:], True, True, [ones_f, tq], [prm])
                                vop(ACT, lambda: nc.scalar.activation(out=tq[:], in_=prm[0:64, :], func=AF.Sqrt, bias=epsr[0:64, 1:2], scale=1.0 / 64), [prm, epsr], [tq])
                                vop(DVE, lambda: nc.vector.reciprocal(out=tq[:], in_=tq[:]), [tq], [tq])
                                ob_ = obf[qc % 2]
                                vop(DVE, lambda: nc.vector.scalar_tensor_tensor(out=ob_[:], in0=o0[:], scalar=gs[:, 2:3], in1=tq[:], op0=ALU.mult, op1=ALU.mult), [o0, tq, gs], [ob_])
                                tk = s0 + qc * 512
                                dma(QS, mixT[h // 2, (h % 2) * 64:(h % 2) * 64 + 64, tk:tk + 512], ob_[:], [ob_], [DR("mixT", tk // 512)])

                            its += make_its(slots[0][0], slots[0][1], 32, sa, v, S, qc, post0)
                            its += make_its(slots[1][0], slots[1][1], 32, sa, v, S, qc, post1)
                        groups.append(dict(load=load, its=its))
                    for h in range(6):
                        slots, v = take_slots(1)

                        def load(h=h, s0=s0, S=S, slots=slots, v=v):
                            load_v(VB, h, s0, S, v)
                            load_qk(QB[h], KBd[h], 96, s0, S, slots[0])

                        its = []
                        for qc in range(S // 512):
                            def post(acc, qc=qc, h=h, s0=s0):
                                o0 = norm_post(acc)
                                ob_ = obf[qc % 2]
                                vop(POOL, lambda: nc.gpsimd.tensor_copy(out=ob_[:], in_=o0[:]), [o0], [ob_])
                                tk = s0 + qc * 512
                                f0 = 256 + h * 64
                                dma(QS, mixT[f0 // 128, f0 % 128:f0 % 128 + 64, tk:tk + 512], ob_[:], [ob_], [DR("mixT", tk // 512)])

                            its += make_its(slots[0][0], slots[0][1], 96, sbq, v, S, qc, post)
                        groups.append(dict(load=load, its=its))

                flat = []
                for gi, g in enumerate(groups):
                    for ii, it in enumerate(g["its"]):
                        it["g"] = gi
                        it["ii"] = ii
                        flat.append(it)

                def emit_qk(it):
                    ps = psc[st["s"] % 3]
                    st["s"] += 1
                    it["ps"] = ps
                    dk, kt, qc = it["dk"], it["kt"], it["qc"]
                    mm(ps[:], it["k"][0:dk, kt * 128:(kt + 1) * 128], it["q"][0:dk, qc * 512:(qc + 1) * 512], True, True, [it["k"], it["q"]], [ps])

                def flush(it):
                    ps = it["ps"]
                    p = pT[st["p"] % 4]
                    st["p"] += 1
                    vop(ACT, lambda: nc.scalar.activation(out=p[:], in_=ps[:], func=AF.Exp, scale=it["scale"]), [ps], [p])
                    mm(it["acc"][:], it["v"][:, it["kt"], :], p[:], it["first"], it["last"], [it["v"], p], [it["acc"]])
                    if it["post"] is not None:
                        it["post"](it["acc"])

                pending = []
                groups[0]["load"]()
                for it in flat:
                    if it["ii"] == LA + 1 and it["g"] + 1 < len(groups):
                        groups[it["g"] + 1]["load"]()
                    emit_qk(it)
                    pending.append(it)
                    if len(pending) > LA:
                        flush(pending.pop(0))
                while pending:
                    flush(pending.pop(0))
                kb.barrier()

        def phase_dilated(l):
            with ExitStack() as ph:
                band = sb(ph, "p4_band", [128, 384], F32)
                dma(QS, band[:], c_band, [], [band])
                qT = [sb(ph, "p4_q%d" % i, [128, SMAX], BF16) for i in range(2)]
                kT = [sb(ph, "p4_k%d" % i, [128, SMAX], BF16) for i in range(2)]
                stq = sb(ph, "p4_stq", [128, SMAX], BF16)
                stk = sb(ph, "p4_stk", [128, SMAX], BF16)
                vt = [sb(ph, "p4_v%d" % i, [128, SMAX // 128, 2, 128], BF16) for i in range(2)]
                nsb = [sb(ph, "p4_n%d" % i, [64, SMAX], BF16) for i in range(2)]
                dsb = [sb(ph, "p4_d%d" % i, [64, SMAX], F32) for i in range(2)]
                psc = [pst(ph, "p4_s%d" % i, [128, 512]) for i in range(3)]
                pac = [pst(ph, "p4_a%d" % i, [128, 512]) for i in range(3)]
                pe_ = [sb(ph, "p4_e%d" % i, [128, 384], F32) for i in range(3)]
                pT = [sb(ph, "p4_p%d" % i, [128, 384], BF16) for i in range(3)]
                for v in vt:
                    vop(DVE, lambda: nc.vector.memset(v[:, :, :, 64:128], 1.0), [], [v])
                st = {"s": 0, "a": 0, "b": 0}
                scale = 64.0 ** -0.5
                for si, (s0, S) in enumerate(zip(seq_starts, seqs)):
                    for g in range(3):
                        r = C_R[g]
                        L = S // r
                        nlt = L // 128
                        bi = (si * 3 + g) % 2
                        q_, k_, v_ = qT[bi], kT[bi], vt[bi]
                        if r == 1:
                            dma(QS, q_[:, 0:S], QKC[g, :, s0:s0 + S], [DR("qkc")], [q_])
                            dma(QS, k_[:, 0:S], QKC[3 + g, :, s0:s0 + S], [DR("qkc")], [k_])
                        else:
                            dma(QS, stq[:, 0:S], QKC[g, :, s0:s0 + S], [DR("qkc")], [stq])
                            dma(QS, stk[:, 0:S], QKC[3 + g, :, s0:s0 + S], [DR("qkc")], [stk])
                            vop(DVE, lambda: nc.vector.tensor_copy(out=q_[:, 0:S].rearrange("p (r m) -> p m r", r=r), in_=stq[:, 0:S].rearrange("p (m r) -> p m r", r=r)), [stq], [q_])
                            vop(POOL, lambda: nc.gpsimd.tensor_copy(out=k_[:, 0:S].rearrange("p (r m) -> p m r", r=r), in_=stk[:, 0:S].rearrange("p (m r) -> p m r", r=r)), [stk], [k_])
                        for rho in range(r):
                            for j in range(2):
                                src = VC[s0:s0 + S, g * 128 + j * 64:g * 128 + (j + 1) * 64].rearrange("(kt p r) d -> r p kt d", p=128, r=r)[rho]
                                for k0 in range(0, nlt, 8):
                                    k1 = min(nlt, k0 + 8)
                                    dma(QS, v_[:, rho * nlt + k0:rho * nlt + k1, j, 0:64], src[:, k0:k1, :], [DR("vc")], [v_])
                        for j in range(2):
                            nb_, db_ = nsb[j], dsb[j]
                            def emit_qk4(rho, t, j=j, q_=q_, k_=k_, L=L, nlt=nlt):
                                base = rho * L + t * 128
                                tiles = [tt for tt in (t - 1, t, t + 1) if 0 <= tt < nlt]
                                i3 = st["s"] % 3
                                st["s"] += 1
                                ps = psc[i3]
                                for tt in tiles:
                                    mi = tt - t + 1
                                    kb0 = rho * L + tt * 128
                                    mm(ps[:, mi * 128:(mi + 1) * 128], k_[j * 64:(j + 1) * 64, kb0:kb0 + 128], q_[j * 64:(j + 1) * 64, base:base + 128],
                                       True, True, [k_, q_], [ps], sig=(tt == tiles[-1]))
                                return dict(rho=rho, t=t, tiles=tiles, i3=i3)

                            def flush4(cx, j=j, v_=v_, nb_=nb_, db_=db_, r=r, S=S, nlt=nlt):
                                rho, t, tiles, i3 = cx["rho"], cx["t"], cx["tiles"], cx["i3"]
                                ps, e_, p_ = psc[i3], pe_[i3], pT[i3]
                                lo, hi = (tiles[0] - t + 1) * 128, (tiles[-1] - t + 2) * 128
                                vop(ACT, lambda: nc.scalar.activation(out=e_[:, lo:hi], in_=ps[:, lo:hi], func=AF.Exp, scale=scale), [ps], [e_])
                                vop(POOL, lambda: nc.gpsimd.tensor_tensor(out=p_[:, lo:hi], in0=e_[:, lo:hi], in1=band[:, lo:hi], op=ALU.mult), [e_, band], [p_])
                                acc = pac[st["a"] % 3]
                                st["a"] += 1
                                for n_, tt in enumerate(tiles):
                                    mi = tt - t + 1
                                    mm(acc[:, 0:128], v_[:, rho * nlt + tt, j, :], p_[:, mi * 128:(mi + 1) * 128], n_ == 0, n_ == len(tiles) - 1, [v_, p_], [acc])
                                if r == 1:
                                    no = nb_[:, t * 128:(t + 1) * 128]
                                    do = db_[:, t * 128:(t + 1) * 128]
                                else:
                                    no = nb_[:, 0:S].rearrange("p (m r) -> p r m", r=r)[:, rho, t * 128:(t + 1) * 128]
                                    do = db_[:, 0:S].rearrange("p (m r) -> p r m", r=r)[:, rho, t * 128:(t + 1) * 128]
                                vop(DVE, lambda: nc.vector.tensor_copy(out=no, in_=acc[0:64, 0:128]), [acc], [nb_])
                                vop(ACT, lambda: nc.scalar.copy(out=do, in_=acc[64:128, 0:128]), [acc], [db_])

                            pend = []
                            for rho in range(r):
                                for t in range(nlt):
                                    pend.append(emit_qk4(rho, t))
                                    if len(pend) > 2:
                                        flush4(pend.pop(0))
                            while pend:
                                flush4(pend.pop(0))
                            dma(QS, NCd[g * 2 + j, :, s0:s0 + S], nb_[:, 0:S], [nb_], [DR("ncd")])
                            dma(QS, DCd[g * 2 + j, :, s0:s0 + S], db_[:, 0:S], [db_], [DR("dcd")])
                kb.barrier()
            with ExitStack() as ph:
                nn = [sb(ph, "p4_nn%d" % i, [64, 6, 512], BF16) for i in range(2)]
                dd = [sb(ph, "p4_dd%d" % i, [64, 6, 512], F32) for i in range(2)]
                tt_ = [sb(ph, "p4_tt%d" % i, [64, 2, 512], F32) for i in range(2)]
                oo = [sb(ph, "p4_oo%d" % i, [64, 6, 512], BF16) for i in range(2)]
                for c in range(NCH):
                    tk = c * 512
                    n_, d_, t_, o_ = nn[c % 2], dd[c % 2], tt_[c % 2], oo[c % 2]
                    dma(QS, n_[:], NCd[:, :, tk:tk + 512].rearrange("h p t -> p h t"), [], [n_])
                    dma(QS, d_[:], DCd[:, :, tk:tk + 512].rearrange("h p t -> p h t"), [], [d_])
                    vop(DVE, lambda: nc.vector.tensor_tensor(out=t_[:], in0=d_[:, 0:2, :], in1=d_[:, 2:4, :], op=ALU.add), [d_], [t_])
                    vop(DVE, lambda: nc.vector.tensor_tensor(out=t_[:], in0=t_[:], in1=d_[:, 4:6, :], op=ALU.add), [d_, t_], [t_])
                    vop(DVE, lambda: nc.vector.reciprocal(out=t_[:], in_=t_[:]), [t_], [t_])
                    for g in range(3):
                        vop(POOL, lambda: nc.gpsimd.tensor_tensor(out=o_[:, 2 * g:2 * g + 2, :], in0=n_[:, 2 * g:2 * g + 2, :], in1=t_[:], op=ALU.mult), [n_, t_], [o_])
                    for hh in range(6):
                        f0 = 640 + hh * 64
                        dma(QS, mixT[f0 // 128, f0 % 128:f0 % 128 + 64, tk:tk + 512], o_[:, hh, :], [o_], [DR("mixT", c)])
                kb.barrier()

        def layer_norm(z, gb, bb, st4, junk):
            vop(DVE, lambda: nc.vector.memset(st4[:, 0:2], 0.0), [], [st4])
            vop(ACT, lambda: nc.scalar.activation(out=junk[:], in_=z[:], func=AF.Identity, accum_out=st4[:, 0:1]), [z, st4], [junk, st4])
            vop(ACT, lambda: nc.scalar.activation(out=junk[:], in_=z[:], func=AF.Square, accum_out=st4[:, 1:2]), [z, st4], [junk, st4])
            vop(DVE, lambda: nc.vector.tensor_scalar_mul(out=st4[:, 2:3], in0=st4[:, 0:1], scalar1=1.0 / D), [st4], [st4])
            vop(DVE, lambda: nc.vector.tensor_tensor(out=st4[:, 3:4], in0=st4[:, 2:3], in1=st4[:, 2:3], op=ALU.mult), [st4], [st4])
            vop(DVE, lambda: nc.vector.scalar_tensor_tensor(out=st4[:, 4:5], in0=st4[:, 1:2], scalar=1.0 / D, in1=st4[:, 3:4], op0=ALU.mult, op1=ALU.subtract), [st4], [st4])
            vop(ACT, lambda: nc.scalar.activation(out=st4[:, 5:6], in_=st4[:, 4:5], func=AF.Sqrt, bias=epsr[:, 0:1], scale=1.0), [st4, epsr], [st4])
            vop(DVE, lambda: nc.vector.reciprocal(out=st4[:, 6:7], in_=st4[:, 5:6]), [st4], [st4])
            vop(DVE, lambda: nc.vector.tensor_scalar(out=z[:], in0=z[:], scalar1=st4[:, 2:3], scalar2=st4[:, 6:7], op0=ALU.subtract, op1=ALU.mult), [z, st4], [z])
            vop(POOL, lambda: nc.gpsimd.tensor_tensor(out=z[:], in0=z[:], in1=gb[:], op=ALU.mult), [z, gb], [z])
            vop(POOL, lambda: nc.gpsimd.tensor_tensor(out=z[:], in0=z[:], in1=bb[:], op=ALU.add), [z, bb], [z])

        def phase_out_ln1(l, xsrc):
            with ExitStack() as ph:
                wout = sb(ph, "p5_wout", [128, 8, D], BF16)
                wr = sb(ph, "p5_wr", [128, 8, 36], F32)
                gb = sb(ph, "p5_g", [128, D], F32)
                bb = sb(ph, "p5_b", [128, D], F32)
                for j in range(8):
                    dma(QP, wout[:, j, :], w_out[l, j * 128:(j + 1) * 128, :], [], [wout])
                dma(QS, wr[:, :, 0:4], w_coarse[l].rearrange("(j p) g -> p j g", p=128), [], [wr])
                for g in range(4):
                    dma(QS, wr[:, :, 4 + g * 8:12 + g * 8], w_fine[l, g].rearrange("(j p) e -> p j e", p=128), [], [wr])
                dma(QS, gb[:], ln1_g[l:l + 1, :].partition_broadcast(128), [], [gb])
                dma(QS, bb[:], ln1_b[l:l + 1, :].partition_broadcast(128), [], [bb])
                mx = [sb(ph, "p5_m%d" % i, [128, 8, 512], BF16) for i in range(2)]
                xr = [sb(ph, "p5_x%d" % i, [128, 4, D], F32) for i in range(2)]
                z = [sb(ph, "p5_z%d" % i, [128, D], F32) for i in range(3)]
                zb = [sb(ph, "p5_zb%d" % i, [128, D], BF16) for i in range(2)]
                junk = sb(ph, "p5_junk", [128, D], F32)
                st4 = [sb(ph, "p5_st%d" % i, [128, 8], F32) for i in range(2)]
                x1T = [sb(ph, "p5_xT%d" % i, [128, 8, 128], F32) for i in range(2)]
                po = [pst(ph, "p5_po%d" % i, [128, 1024]) for i in range(2)]
                pt = [pst(ph, "p5_pt%d" % i, [128, 512]) for i in range(2)]
                plg = pst(ph, "p5_plg", [128, 512])
                lg = sb(ph, "p5_lg", [128, 36], F32)
                sm = sb(ph, "p5_sm", [128, 64], F32)

                def load(c):
                    dma(QS, mx[c % 2][:], mixT[:, :, c * 512:(c + 1) * 512].rearrange("j p t -> p j t"), [DR("mixT", c)], [mx[c % 2]])
                    dma(QS, xr[c % 2][:], xsrc[c * 512:(c + 1) * 512, :].rearrange("(a p) d -> p a d", p=128), [DR("x", c)], [xr[c % 2]])

                load(0)
                for c in range(NCH):
                    if c + 1 < NCH:
                        load(c + 1)
                    m_, x_ = mx[c % 2], xr[c % 2]
                    for a in range(4):
                        it = c * 4 + a
                        tok0 = it * 128
                        p_ = po[it % 2]
                        for hf in range(2):
                            for j in range(8):
                                mm(p_[:, hf * 512:(hf + 1) * 512], m_[:, j, a * 128:(a + 1) * 128], wout[:, j, hf * 512:(hf + 1) * 512], j == 0, j == 7, [m_, wout], [p_])
                        z_ = z[it % 3]
                        vop(DVE, lambda: nc.vector.scalar_tensor_tensor(out=z_[:], in0=x_[:, a, :], scalar=DN_ALPHA, in1=p_[:], op0=ALU.mult, op1=ALU.add), [x_, p_], [z_])
                        layer_norm(z_, gb, bb, st4[it % 2], junk)
                        dma(QS, x1d[tok0:tok0 + 128, :], z_[:], [z_], [DR("x1", it)])
                        zb_ = zb[it % 2]
                        vop(ACT, lambda: nc.scalar.copy(out=zb_[:], in_=z_[:]), [z_], [zb_])
                        dma(QS, x1b[tok0:tok0 + 128, :], zb_[:], [zb_], [DR("x1b", it)])
                        xT_ = x1T[it % 2]
                        for hf in range(2):
                            for jj in range(4):
                                j = hf * 4 + jj
                                tr(pt[hf][:, jj * 128:(jj + 1) * 128], z_[:, j * 128:(j + 1) * 128], ident_f[:], [z_, ident_f], [pt[hf]], sig=(jj == 3))
                            vop(ACT, lambda: nc.scalar.copy(out=xT_[:, hf * 4:(hf + 1) * 4, :], in_=pt[hf][:].rearrange("p (j t) -> p j t", j=4)), [pt[hf]], [xT_])
                        for j in range(8):
                            mm(plg[:, 0:36], xT_[:, j, :], wr[:, j, :], j == 0, j == 7, [xT_, wr], [plg])
                        vop(DVE, lambda: nc.vector.tensor_copy(out=lg[:], in_=plg[:, 0:36]), [plg], [lg])
                        vop(DVE, lambda: nc.vector.reduce_max(out=sm[:, 0:1], in_=lg[:, 0:4], axis=AX.X), [lg], [sm])
                        vop(DVE, lambda: nc.vector.tensor_scalar_mul(out=sm[:, 1:2], in0=sm[:, 0:1], scalar1=-1.0), [sm], [sm])
                        vop(DVE, lambda: nc.vector.memset(sm[:, 2:3], 0.0), [], [sm])
                        vop(ACT, lambda: nc.scalar.activation(out=sm[:, 44:48], in_=lg[:, 0:4], func=AF.Exp, bias=sm[:, 1:2], scale=1.0, accum_out=sm[:, 2:3]), [lg, sm], [sm])
                        vop(DVE, lambda: nc.vector.reciprocal(out=sm[:, 3:4], in_=sm[:, 2:3]), [sm], [sm])
                        vop(DVE, lambda: nc.vector.tensor_scalar(out=sm[:, 4:8], in0=lg[:, 0:4], scalar1=sm[:, 0:1], scalar2=None, op0=ALU.is_equal), [lg, sm], [sm])
                        vop(DVE, lambda: nc.vector.tensor_scalar_mul(out=sm[:, 8:16], in0=lg[:, 4:12], scalar1=sm[:, 4:5]), [lg, sm], [sm])
                        for g in range(1, 4):
                            vop(DVE, lambda: nc.vector.scalar_tensor_tensor(out=sm[:, 8:16], in0=lg[:, 4 + g * 8:12 + g * 8], scalar=sm[:, 4 + g:5 + g], in1=sm[:, 8:16],
                                                                            op0=ALU.mult, op1=ALU.add), [lg, sm], [sm])
                        vop(DVE, lambda: nc.vector.max(out=sm[:, 16:24], in_=sm[:, 8:16]), [sm], [sm])
                        vop(DVE, lambda: nc.vector.tensor_scalar(out=sm[:, 24:32], in0=sm[:, 8:16], scalar1=sm[:, 16:17], scalar2=None, op0=ALU.is_equal), [sm], [sm])
                        vop(DVE, lambda: nc.vector.tensor_scalar(out=sm[:, 32:40], in0=sm[:, 8:16], scalar1=sm[:, 17:18], scalar2=None, op0=ALU.is_equal), [sm], [sm])
                        vop(DVE, lambda: nc.vector.tensor_tensor(out=sm[:, 40:41], in0=sm[:, 17:18], in1=sm[:, 16:17], op=ALU.subtract), [sm], [sm])
                        vop(ACT, lambda: nc.scalar.activation(out=sm[:, 41:42], in_=sm[:, 40:41], func=AF.Exp), [sm], [sm])
                        vop(DVE, lambda: nc.vector.tensor_scalar_add(out=sm[:, 42:43], in0=sm[:, 41:42], scalar1=1.0), [sm], [sm])
                        vop(DVE, lambda: nc.vector.reciprocal(out=sm[:, 43:44], in_=sm[:, 42:43]), [sm], [sm])
                        vop(DVE, lambda: nc.vector.tensor_tensor(out=wts[:, it, 0:1], in0=sm[:, 43:44], in1=sm[:, 3:4], op=ALU.mult), [sm], [wts])
                        vop(DVE, lambda: nc.vector.tensor_tensor(out=wts[:, it, 1:2], in0=wts[:, it, 0:1], in1=sm[:, 41:42], op=ALU.mult), [sm, wts], [wts])
                        for g in range(4):
                            vop(DVE, lambda: nc.vector.tensor_scalar_mul(out=E0[:, it, g * 8:(g + 1) * 8], in0=sm[:, 24:32], scalar1=sm[:, 4 + g:5 + g]), [sm], [E0])
                            vop(DVE, lambda: nc.vector.tensor_scalar_mul(out=E1[:, it, g * 8:(g + 1) * 8], in0=sm[:, 32:40], scalar1=sm[:, 4 + g:5 + g]), [sm], [E1])
                kb.barrier()

        def phase_route(l):
            with ExitStack() as ph:
                e01 = sb(ph, "p6_e01", [128, NT, 32], F32)
                run = sb(ph, "p6_run", [128, NT + 1, 32], F32)
                cb = sb(ph, "p6_cb", [128, 32], F32)
                ci = sb(ph, "p6_ci", [128, 32], I32)
                inc = [sb(ph, "p6_inc%d" % i, [128, 32], F32) for i in range(2)]
                pstart = sb(ph, "p6_ps", [128, 32], F32)
                pend = sb(ph, "p6_pe", [128, 32], F32)
                destf = sb(ph, "p6_df", [128, NT, 2], F32)
                tmp = [sb(ph, "p6_t%d" % i, [128, 32], F32) for i in range(2)]
                tmp2 = [sb(ph, "p6_u%d" % i, [128, 32], F32) for i in range(2)]
                bex = sb(ph, "p6_bex", [128, NB], F32)
                pc = [pst(ph, "p6_pc%d" % i, [128, 512]) for i in range(2)]
                vop(POOL, lambda: nc.gpsimd.tensor_tensor(out=e01[:], in0=E0[:], in1=E1[:], op=ALU.add), [E0, E1], [e01])
                vop(DVE, lambda: nc.vector.memset(run[:, 0, :], 0.0), [], [run])
                for i in range(NT):
                    vop(DVE, lambda: nc.vector.tensor_tensor(out=run[:, i + 1, :], in0=run[:, i, :], in1=e01[:, i, :], op=ALU.add), [run, e01], [run])
                mm(pc[0][:, 0:32], ones_f[:], run[:, NT, :], True, True, [ones_f, run], [pc[0]])
                vop(DVE, lambda: nc.vector.tensor_scalar_add(out=cb[:], in0=pc[0][:, 0:32], scalar1=float(BLK - 1)), [pc[0]], [cb])
                vop(DVE, lambda: nc.vector.tensor_copy(out=ci[:], in_=cb[:]), [cb], [ci])
                sh = int(math.log2(BLK))
                vop(DVE, lambda: nc.vector.tensor_single_scalar(out=ci[:], in_=ci[:], scalar=sh, op=ALU.arith_shift_right), [ci], [ci])
                vop(DVE, lambda: nc.vector.tensor_single_scalar(out=ci[:], in_=ci[:], scalar=sh, op=ALU.logical_shift_left), [ci], [ci])
                vop(DVE, lambda: nc.vector.tensor_copy(out=cb[:], in_=ci[:]), [ci], [cb])
                vop(DVE, lambda: nc.vector.tensor_copy(out=inc[0][:], in_=cb[:]), [cb], [inc[0]])
                cur = 0
                s = 1
                while s < 32:
                    a_, b_ = inc[cur], inc[1 - cur]
                    vop(DVE, lambda: nc.vector.tensor_copy(out=b_[:, 0:s], in_=a_[:, 0:s]), [a_], [b_])
                    vop(DVE, lambda: nc.vector.tensor_tensor(out=b_[:, s:32], in0=a_[:, s:32], in1=a_[:, 0:32 - s], op=ALU.add), [a_], [b_])
                    cur = 1 - cur
                    s *= 2
                vop(DVE, lambda: nc.vector.tensor_copy(out=pend[:], in_=inc[cur][:]), [inc[cur]], [pend])
                vop(DVE, lambda: nc.vector.tensor_tensor(out=pstart[:], in0=pend[:], in1=cb[:], op=ALU.subtract), [pend, cb], [pstart])
                for i in range(NT):
                    p_ = pc[i % 2]
                    mm(p_[:, 0:32], tri_f[:], e01[:, i, :], True, False, [tri_f, e01], [p_], sig=False)
                    mm(p_[:, 0:32], ones_f[:], run[:, i, :], False, True, [ones_f, run], [p_])
                    t_ = tmp[i % 2]
                    u_ = tmp2[i % 2]
                    vop(DVE, lambda: nc.vector.tensor_tensor(out=t_[:], in0=p_[:, 0:32], in1=pstart[:], op=ALU.add), [p_, pstart], [t_])
                    for k, EE in enumerate((E0, E1)):
                        vop(POOL, lambda: nc.gpsimd.tensor_tensor(out=u_[:], in0=t_[:], in1=EE[:, i, :], op=ALU.mult), [t_, EE], [u_])
                        vop(DVE, lambda: nc.vector.reduce_sum(out=destf[:, i, k:k + 1], in_=u_[:], axis=AX.X), [u_], [destf])
                vop(DVE, lambda: nc.vector.tensor_copy(out=dest_i[:], in_=destf[:]), [destf], [dest_i])
                vop(DVE, lambda: nc.vector.memset(bex[:], 0.0), [], [bex])
                for e in range(NEXP):
                    vop(DVE, lambda: nc.vector.scalar_tensor_tensor(out=bex[:], in0=misc[:, 1:1 + NB], scalar=pend[:, e:e + 1], in1=bex[:], op0=ALU.is_ge, op1=ALU.add),
                        [misc, pend, bex], [bex])
                vop(DVE, lambda: nc.vector.tensor_scalar(out=bex[:], in0=bex[:], scalar1=float(NEXP - 1), scalar2=float(l * NEXP), op0=ALU.min, op1=ALU.add), [bex], [bex])
                vop(DVE, lambda: nc.vector.tensor_scalar(out=bex[:], in0=bex[:], scalar1=128.0, scalar2=misc[:, 0:1], op0=ALU.mult, op1=ALU.add), [bex, misc], [bex])
                vop(DVE, lambda: nc.vector.tensor_copy(out=widx[:], in_=bex[:]), [bex], [widx])
                kb.barrier()
                xb = [sb(ph, "p7_x%d" % i, [128, D], BF16) for i in range(4)]
                for i in range(NT):
                    b_ = xb[i % 4]
                    dma(QS, b_[:], x1b[i * 128:(i + 1) * 128, :], [], [b_])
                    for k in range(2):
                        kb.dma(QP, lambda: nc.gpsimd.indirect_dma_start(out=xs, out_offset=bass.IndirectOffsetOnAxis(ap=dest_i[:, i, k:k + 1], axis=0),
                                                                         in_=b_[:], in_offset=None), R=rs(b_, dest_i), W=[DR("xs")])
                kb.barrier()

        def phase_experts(l):
            w1v = w1.rearrange("l e (p j) c -> (l e p) (j c)", j=8)
            w3v = w3.rearrange("l e (p j) c -> (l e p) (j c)", j=8)
            w2v = w2.rearrange("l e (p j) c -> (l e p) (j c)", j=4)
            with ExitStack() as ph:
                W1 = [sb(ph, "p8_w1%d" % i, [128, 8, DE], BF16) for i in range(2)]
                W3 = [sb(ph, "p8_w3%d" % i, [128, 8, DE], BF16) for i in range(2)]
                W2 = [sb(ph, "p8_w2%d" % i, [128, 4, D], BF16) for i in range(2)]
                xt = [sb(ph, "p8_x%d" % i, [128, 4, D], BF16) for i in range(2)]
                xsT = [sb(ph, "p8_xT%d" % i, [128, 8, 512], BF16) for i in range(2)]
                hT = [sb(ph, "p8_h%d" % i, [128, 4, 512], BF16) for i in range(2)]
                sl = [sb(ph, "p8_sl%d" % i, [128, 512], F32) for i in range(2)]
                yo = [sb(ph, "p8_y%d" % i, [128, D], F32) for i in range(2)]
                ptr = [pst(ph, "p8_pt%d" % i, [128, 1024], BF16) for i in range(2)]
                p1 = [pst(ph, "p8_p1%d" % i, [128, 512]) for i in range(1)]
                p3 = [pst(ph, "p8_p3%d" % i, [128, 512]) for i in range(1)]
                py = [pst(ph, "p8_py%d" % i, [128, 1024]) for i in range(2)]

                def load(b):
                    i = b % 2
                    for (Wt, src) in ((W1[i], w1v), (W3[i], w3v), (W2[i], w2v)):
                        kb.dma(QP, lambda: nc.gpsimd.indirect_dma_start(out=Wt[:].rearrange("p a c -> p (a c)"), out_offset=None, in_=src,
                                                                         in_offset=bass.IndirectOffsetOnAxis(ap=widx[:, b:b + 1], axis=0)), R=rs(widx), W=rs(Wt))
                    dma(QS, xt[i][:], xs[b * BLK:(b + 1) * BLK, :].rearrange("(a p) d -> p a d", p=128), [DR("xs")], [xt[i]])

                load(0)
                for b in range(NB):
                    if b + 1 < NB:
                        load(b + 1)
                    i = b % 2
                    x_, xT_, h_ = xt[i], xsT[i], hT[i]
                    for a in range(4):
                        ps = ptr[a % 2]
                        for j in range(8):
                            tr(ps[:, j * 128:(j + 1) * 128], x_[:, a, :].rearrange("p (q j) -> p j q", j=8)[:, j, :], ident_b[:], [x_, ident_b], [ps], sig=(j == 7))
                        vop(DVE, lambda: nc.vector.tensor_copy(out=xT_[:, :, a * 128:(a + 1) * 128], in_=ps[:].rearrange("p (j t) -> p j t", j=8)), [ps], [xT_])
                    for jp in range(4):
                        w1s = W1[i][:].rearrange("p j (q f) -> p j f q", f=4)
                        w3s = W3[i][:].rearrange("p j (q f) -> p j f q", f=4)
                        for j in range(8):
                            mm(p1[0][:], w1s[:, j, jp, :], xT_[:, j, :], j == 0, j == 7, [W1[i], xT_], [p1[0]])
                        for j in range(8):
                            mm(p3[0][:], w3s[:, j, jp, :], xT_[:, j, :], j == 0, j == 7, [W3[i], xT_], [p3[0]])
                        s_ = sl[jp % 2]
                        vop(ACT, lambda: nc.scalar.activation(out=s_[:], in_=p1[0][:], func=AF.Silu), [p1[0]], [s_])
                        vop(DVE, lambda: nc.vector.tensor_tensor(out=h_[:, jp, :], in0=p3[0][:], in1=s_[:], op=ALU.mult), [p3[0], s_], [h_])
                    for a in range(4):
                        p_ = py[a % 2]
                        for hf in range(2):
                            for jp in range(4):
                                mm(p_[:, hf * 512:(hf + 1) * 512], h_[:, jp, a * 128:(a + 1) * 128], W2[i][:, jp, hf * 512:(hf + 1) * 512], jp == 0, jp == 3, [h_, W2[i]], [p_])
                        y_ = yo[a % 2]
                        vop(ACT, lambda: nc.scalar.copy(out=y_[:], in_=p_[:]), [p_], [y_])
                        r0 = b * BLK + a * 128
                        dma(QS, ys[r0:r0 + 128, :], y_[:], [y_], [DR("ys")])
                kb.barrier()

        def phase_ln2(l, dst, make_xT):
            with ExitStack() as ph:
                gb = sb(ph, "p9_g", [128, D], F32)
                bb = sb(ph, "p9_b", [128, D], F32)
                dma(QS, gb[:], ln2_g[l:l + 1, :].partition_broadcast(128), [], [gb])
                dma(QS, bb[:], ln2_b[l:l + 1, :].partition_broadcast(128), [], [bb])
                x1t = [sb(ph, "p9_x%d" % i, [128, D], F32) for i in range(2)]
                y0 = [sb(ph, "p9_y0%d" % i, [128, D], F32) for i in range(2)]
                y1 = [sb(ph, "p9_y1%d" % i, [128, D], F32) for i in range(2)]
                z = [sb(ph, "p9_z%d" % i, [128, D], F32) for i in range(3)]
                zb = [sb(ph, "p9_zb%d" % i, [128, D], BF16) for i in range(2)]
                junk = sb(ph, "p9_junk", [128, D], F32)
                st4 = [sb(ph, "p9_st%d" % i, [128, 8], F32) for i in range(2)]
                xTs = [sb(ph, "p9_t%d" % i, [128, 8, 512], BF16) for i in range(2)]
                pss = [pst(ph, "p9_ps%d" % i, [128, 1024], BF16) for i in range(2)]

                def load(i):
                    dma(QS, x1t[i % 2][:], x1d[i * 128:(i + 1) * 128, :], [], [x1t[i % 2]])
                    for k, yy in enumerate((y0, y1)):
                        kb.dma(QP, lambda: nc.gpsimd.indirect_dma_start(out=yy[i % 2][:], out_offset=None, in_=ys,
                                                                         in_offset=bass.IndirectOffsetOnAxis(ap=dest_i[:, i, k:k + 1], axis=0)), R=rs(dest_i), W=rs(yy[i % 2]))

                load(0)
                for i in range(NT):
                    if i + 1 < NT:
                        load(i + 1)
                    c, a = divmod(i, 4)
                    z_ = z[i % 3]
                    vop(DVE, lambda: nc.vector.tensor_scalar_mul(out=z_[:], in0=y0[i % 2][:], scalar1=wts[:, i, 0:1]), [y0[i % 2], wts], [z_])
                    vop(DVE, lambda: nc.vector.scalar_tensor_tensor(out=z_[:], in0=y1[i % 2][:], scalar=wts[:, i, 1:2], in1=z_[:], op0=ALU.mult, op1=ALU.add), [y1[i % 2], wts, z_], [z_])
                    vop(DVE, lambda: nc.vector.scalar_tensor_tensor(out=z_[:], in0=x1t[i % 2][:], scalar=DN_ALPHA, in1=z_[:], op0=ALU.mult, op1=ALU.add), [x1t[i % 2], z_], [z_])
                    layer_norm(z_, gb, bb, st4[i % 2], junk)
                    dma(QS, dst[i * 128:(i + 1) * 128, :], z_[:], [z_], [DR("x", c)])
                    if make_xT:
                        zb_ = zb[i % 2]
                        vop(ACT, lambda: nc.scalar.copy(out=zb_[:], in_=z_[:]), [z_], [zb_])
                        emit_xT_tile(pss[i % 2], zb_, a, xTs[c % 2])
                        if a == 3:
                            dma(QS, xT[:, :, c * 512:(c + 1) * 512].rearrange("j p t -> p j t"), xTs[c % 2][:], [xTs[c % 2]], [DR("xT", c)])
                kb.barrier()

        phase_xprep(x_in)
        for l in range(depth):
            xsrc = x_in if l == 0 else xa
            last = (l == depth - 1)
            if stop_after == "xprep":
                break
            phase_proj(l)
            if stop_after == "proj":
                break
            phase_full_attn(l)
            phase_dilated(l)
            if stop_after == "attn":
                break
            phase_out_ln1(l, xsrc)
            if stop_after == "ln1":
                break
            phase_route(l)
            phase_experts(l)
            phase_ln2(l, y_out if last else xa, not last)
        kb.barrier()
        stats = dict(pe=kb.pe.n, act=kb.act.n, dve=kb.dve.n, pool=kb.pool.n, qs=kb.qs.k, qp=kb.qp.k, qs_max=max(kb.qs.cnt), qp_max=max(kb.qp.cnt))
    build.stats = stats
    return nc


WNAMES = ["w_in", "diff_lambda", "diff_subln", "mla_q_norm", "mla_w_uq", "mla_kv_norm", "mla_w_ukv", "w_out", "ln1_g", "ln1_b",
          "moe_w_coarse", "moe_w_fine", "moe_w1", "moe_w3", "moe_w2", "ln2_g", "ln2_b"]


def kernel(x_prompt, x_sample, **w):
    x_prompt = np.asarray(x_prompt, dtype=np.float32)
    x_sample = np.asarray(x_sample, dtype=np.float32)
    n = 8
    seqs = [2048, 2048, 4096]
    nc = build(seqs, DEPTH)
    T = sum(seqs)
    NB = -(-(2 * T + NEXP * (BLK - 1)) // BLK)
    consts = host_consts(max(seqs), NB)
    wd = {k: np.ascontiguousarray(np.asarray(w[k], dtype=np.float32)) for k in WNAMES}
    in_maps = []
    for c in range(n):
        xc = np.concatenate([x_prompt[2 * c], x_prompt[2 * c + 1], x_sample[c]], axis=0)
        m = {"x": np.ascontiguousarray(xc)}
        m.update(wd)
        m.update(consts)
        in_maps.append(m)
    res = run_bass_kernel_spmd(nc, in_maps, core_ids=list(range(n)))
    yp = np.empty_like(x_prompt)
    ysm = np.empty_like(x_sample)
    for c in range(n):
        y = res.results[c]["y"]
        yp[2 * c] = y[0:2048]
        yp[2 * c + 1] = y[2048:4096]
        ysm[c] = y[4096:8192]
    return (yp, ysm)
```

```python
import math
import os
from contextlib import ExitStack

import numpy as np
import concourse.bass as bass
import concourse.mybir as mybir
from concourse.bass_utils import run_bass_kernel_spmd

F32 = mybir.dt.float32
BF16 = mybir.dt.bfloat16
I32 = mybir.dt.int32
ALU = mybir.AluOpType
AF = mybir.ActivationFunctionType
AX = mybir.AxisListType

D = 1024
DEPTH = 4
IN_COLS = 2336
COLB = 768
COLKV = 1024
COLKR = 1152
COLC = 1184
NEXP = 32
DE = 512
LN_EPS = 1e-5
RMS_EPS = 1e-6
DN_ALPHA = (2 * DEPTH) ** 0.25
BLK = 512
C_R = (1, 4, 16)


class Res:
    __slots__ = ("w", "r", "ep")

    def __init__(self):
        self.w = None
        self.r = {}
        self.ep = -1


class Eng:
    def __init__(self, e, sem, is_pe=False):
        self.e = e
        self.sem = sem
        self.n = 0
        self.seen = {}
        self.is_pe = is_pe


class DQ:
    def __init__(self, eng, sems):
        self.eng = eng
        self.sems = sems
        self.cnt = [0] * len(sems)
        self.k = 0


class KB:
    def __init__(self, nc, es, nq=22):
        self.nc = nc
        self.ep = 0
        mk = lambda n: es.enter_context(nc.semaphore(n))
        self.pe = Eng(nc.tensor, mk("s_pe"), True)
        self.act = Eng(nc.scalar, mk("s_act"))
        self.dve = Eng(nc.vector, mk("s_dve"))
        self.pool = Eng(nc.gpsimd, mk("s_pool"))
        self.sp = Eng(nc.sync, None)
        self.engs = [self.pe, self.act, self.dve, self.pool]
        self.qs = DQ(self.sp, [mk("q_s%d" % i) for i in range(16)])
        self.qp = DQ(self.pool, [mk("q_p%d" % i) for i in range(6)])
        self.queues = [self.qs, self.qp]

    def _fresh(self, b):
        if b.ep != self.ep:
            b.w = None
            b.r = {}
            b.ep = self.ep

    def wait(self, eng, sem, val):
        if val > 0 and eng.seen.get(id(sem), 0) < val:
            eng.e.wait_ge(sem, val)
            eng.seen[id(sem)] = val

    def deps(self, eng, R, W, is_dma):
        for b in R:
            self._fresh(b)
            if b.w is not None:
                sem, val = b.w
                if sem is eng.sem and not is_dma and eng.is_pe:
                    continue
                self.wait(eng, sem, val)
        for b in W:
            self._fresh(b)
            if b.w is not None:
                sem, val = b.w
                if not (sem is eng.sem and not is_dma):
                    self.wait(eng, sem, val)
            for sem, val in b.r.values():
                if sem is eng.sem and not is_dma:
                    continue
                self.wait(eng, sem, val)

    def _mark(self, tok, R, W):
        sem, val = tok
        for b in R:
            old = b.r.get(id(sem))
            if old is None or old[1] < val:
                b.r[id(sem)] = (sem, val)
        for b in W:
            b.w = tok
            b.r = {}

    def op(self, eng, fn, R=(), W=(), sig=True):
        self.deps(eng, R, W, False)
        ins = fn()
        if sig:
            eng.n += 1
            ins.then_inc(eng.sem, 1)
            tok = (eng.sem, eng.n)
        else:
            tok = (eng.sem, eng.n + 1)
        self._mark(tok, R, W)
        return ins

    def dma(self, q, fn, R=(), W=()):
        eng = q.eng
        self.deps(eng, R, W, True)
        i = q.k % len(q.sems)
        q.k += 1
        sem = q.sems[i]
        self.wait(eng, sem, q.cnt[i])
        ins = fn()
        ins.then_inc(sem, 16)
        q.cnt[i] += 16
        self._mark((sem, q.cnt[i]), R, W)
        return ins

    def barrier(self):
        for E in self.engs + [self.sp]:
            for X in self.engs:
                if X is not E:
                    self.wait(E, X.sem, X.n)
            for q in self.queues:
                for i, sem in enumerate(q.sems):
                    self.wait(E, sem, q.cnt[i])
        self.ep += 1


class Tl:
    def __init__(self, t):
        self.t = t
        self.res = Res()

    def __getitem__(self, k):
        return self.t[k]


def host_consts(smax, nb):
    def rope(dim):
        inv = (1.0 / (np.float32(10000.0) ** (np.arange(0, dim, 2, dtype=np.float32) / np.float32(dim)))).astype(np.float32)
        ang = (np.arange(smax, dtype=np.float32)[:, None] * inv[None, :]).astype(np.float32)
        return np.cos(ang).astype(np.float32), np.sin(ang).astype(np.float32)

    ca, sa = rope(32)
    cc, sc = rope(64)
    p = np.arange(128)
    cosA = ca[:, p % 16].T.copy()
    sinA = sa[:, p % 16].T.copy()
    cosC = cc[:, p % 32].T.copy()
    sinC = sc[:, p % 32].T.copy()
    cosB = np.ones((128, smax), np.float32)
    sinB = np.zeros((128, smax), np.float32)
    cosB[64:96] = cosA[0:32]
    sinB[64:96] = sinA[0:32]
    k = np.arange(128)[:, None]
    q = np.arange(128)[None, :]
    band = np.stack([((k - 128 - q) >= -64), (np.abs(k - q) <= 64), ((k + 128 - q) <= 64)], axis=1).astype(np.float32)
    misc = np.zeros((128, 512), np.float32)
    misc[:, 0] = np.arange(128)
    misc[:, 1:1 + nb] = (np.arange(nb) * BLK)[None, :]
    tri = (np.arange(128)[:, None] < np.arange(128)[None, :]).astype(np.float32)
    return {
        "c_ident": np.eye(128, dtype=np.float32), "c_cosA": cosA, "c_sinA": sinA, "c_cosC": cosC, "c_sinC": sinC,
        "c_cosB": cosB, "c_sinB": sinB, "c_band": band.reshape(128, 384).copy(), "c_misc": misc, "c_tri": tri,
    }


def build(seqs, depth, wdepth=DEPTH, stop_after=None, dbg=()):
    T = sum(seqs)
    NT = T // 128
    NCH = T // 512
    NB = -(-(2 * T + NEXP * (BLK - 1)) // BLK)
    NSLOT = NB * BLK
    SMAX = max(seqs)
    seq_of_chunk = []
    t0 = 0
    seq_starts = []
    for S in seqs:
        seq_starts.append(t0)
        for c in range(S // 512):
            seq_of_chunk.append((t0, S, c * 512))
        t0 += S

    nc = bass.Bass("TRN2", target_bir_lowering=False)
    dt_in = lambda n, s, d=F32: nc.dram_tensor(n, list(s), d, kind="ExternalInput").ap()
    dt_sc = lambda n, s, d: nc.dram_tensor(n, list(s), d, kind=("ExternalOutput" if n in dbg else "Internal")).ap()
    x_in = dt_in("x", [T, D])
    w_in = dt_in("w_in", [wdepth, D, IN_COLS])
    diff_lambda = dt_in("diff_lambda", [wdepth, 4, 32])
    diff_subln = dt_in("diff_subln", [wdepth, 64])
    mla_q_norm = dt_in("mla_q_norm", [wdepth, 256])
    mla_w_uq = dt_in("mla_w_uq", [wdepth, 256, 576])
    mla_kv_norm = dt_in("mla_kv_norm", [wdepth, 128])
    mla_w_ukv = dt_in("mla_w_ukv", [wdepth, 128, 768])
    w_out = dt_in("w_out", [wdepth, D, D])
    ln1_g = dt_in("ln1_g", [wdepth, D])
    ln1_b = dt_in("ln1_b", [wdepth, D])
    w_coarse = dt_in("moe_w_coarse", [wdepth, D, 4])
    w_fine = dt_in("moe_w_fine", [wdepth, 4, D, 8])
    w1 = dt_in("moe_w1", [wdepth, NEXP, D, DE])
    w3 = dt_in("moe_w3", [wdepth, NEXP, D, DE])
    w2 = dt_in("moe_w2", [wdepth, NEXP, DE, D])
    ln2_g = dt_in("ln2_g", [wdepth, D])
    ln2_b = dt_in("ln2_b", [wdepth, D])
    c_ident = dt_in("c_ident", [128, 128])
    c_cos = {"A": dt_in("c_cosA", [128, SMAX]), "C": dt_in("c_cosC", [128, SMAX]), "B": dt_in("c_cosB", [128, SMAX])}
    c_sin = {"A": dt_in("c_sinA", [128, SMAX]), "C": dt_in("c_sinC", [128, SMAX]), "B": dt_in("c_sinB", [128, SMAX])}
    c_band = dt_in("c_band", [128, 384])
    c_misc = dt_in("c_misc", [128, 512])
    c_tri = dt_in("c_tri", [128, 128])
    y_out = nc.dram_tensor("y", [T, D], F32, kind="ExternalOutput").ap()

    xa = dt_sc("s_xa", [T, D], F32)
    x1d = dt_sc("s_x1", [T, D], F32)
    x1b = dt_sc("s_x1b", [T, D], BF16)
    xT = dt_sc("s_xT", [8, 128, T], BF16)
    mixT = dt_sc("s_mixT", [8, 128, T], BF16)
    QKA = dt_sc("s_qka", [4, 128, T], BF16)
    QKC = dt_sc("s_qkc", [6, 128, T], BF16)
    QB = dt_sc("s_qb", [6, 96, T], BF16)
    KBd = dt_sc("s_kb", [6, 96, T], BF16)
    VA = dt_sc("s_va", [T, 256], BF16)
    VB = dt_sc("s_vb", [T, 384], BF16)
    VC = dt_sc("s_vc", [T, 384], BF16)
    NCd = dt_sc("s_nc", [6, 64, T], BF16)
    DCd = dt_sc("s_dc", [6, 64, T], F32)
    xs = dt_sc("s_xs", [NSLOT, D], BF16)
    ys = dt_sc("s_ys", [NSLOT, D], F32)

    es = ExitStack()
    with es:
        kb = KB(nc, es)
        PE, ACT, DVE, POOL = kb.pe, kb.act, kb.dve, kb.pool
        QS, QP = kb.qs, kb.qp
        es.enter_context(nc.Block())
        dres = {}

        def DR(*key):
            r = dres.get(key)
            if r is None:
                r = dres[key] = Res()
            return r

        uid = [0]

        def sb(stack, name, shape, dtype):
            uid[0] += 1
            return Tl(stack.enter_context(nc.sbuf_tensor("%s_%d" % (name, uid[0]), list(shape), dtype)))

        def pst(stack, name, shape, dtype=F32):
            uid[0] += 1
            return Tl(stack.enter_context(nc.psum_tensor("%s_%d" % (name, uid[0]), list(shape), dtype)))

        def rs(*ts):
            return [t.res if isinstance(t, Tl) else t for t in ts]

        def mm(out, lhsT, rhs, start, stop, R, W, sig=None):
            kb.op(PE, lambda: nc.tensor.matmul(out, lhsT=lhsT, rhs=rhs, start=start, stop=stop), R=rs(*R), W=rs(*W),
                  sig=(stop if sig is None else sig))

        def tr(out, in_, ident, R, W, sig=True):
            kb.op(PE, lambda: nc.tensor.transpose(out, in_, ident), R=rs(*R), W=rs(*W), sig=sig)

        def vop(eng, fn, R, W):
            kb.op(eng, fn, R=rs(*R), W=rs(*W))

        def dma(q, out, in_, R, W):
            kb.dma(q, lambda: q.eng.e.dma_start(out=out, in_=in_), R=rs(*R), W=rs(*W))

        ident_f = sb(es, "ident_f", [128, 128], F32)
        ident_b = sb(es, "ident_b", [128, 128], BF16)
        ones_f = sb(es, "ones_f", [128, 128], F32)
        ones_b = sb(es, "ones_b", [128, 128], BF16)
        tri_f = sb(es, "tri_f", [128, 128], F32)
        misc = sb(es, "misc", [128, 512], F32)
        epsr = sb(es, "epsr", [128, 2], F32)
        dma(QS, ident_f[:], c_ident, [], [ident_f])
        dma(QP, ident_b[:], c_ident, [], [ident_b])
        dma(QS, tri_f[:], c_tri, [], [tri_f])
        dma(QS, misc[:], c_misc, [], [misc])
        vop(DVE, lambda: nc.vector.memset(ones_f[:], 1.0), [], [ones_f])
        vop(DVE, lambda: nc.vector.memset(ones_b[:], 1.0), [], [ones_b])
        vop(DVE, lambda: nc.vector.memset(epsr[:, 0:1], LN_EPS), [], [epsr])
        vop(DVE, lambda: nc.vector.memset(epsr[:, 1:2], RMS_EPS), [], [epsr])
        E0 = sb(es, "E0", [128, NT, 32], F32)
        E1 = sb(es, "E1", [128, NT, 32], F32)
        wts = sb(es, "wts", [128, NT, 2], F32)
        dest_i = sb(es, "dest_i", [128, NT, 2], I32)
        widx = sb(es, "widx", [128, NB], I32)
        kb.barrier()

        def emit_xT_tile(ps, src_bf, a, xTs):
            for j in range(8):
                tr(ps[:, j * 128:(j + 1) * 128], src_bf[:, j * 128:(j + 1) * 128], ident_b[:], [src_bf, ident_b], [ps], sig=(j == 7))
            vop(ACT, lambda: nc.scalar.copy(out=xTs[:, :, a * 128:(a + 1) * 128], in_=ps[:].rearrange("p (j t) -> p j t", j=8)),
                [ps], [xTs])

        def rstd_from(sums_sq_ap, out_tl, n, eps_col, tmp_tl, R):
            vop(ACT, lambda: nc.scalar.activation(out=tmp_tl, in_=sums_sq_ap, func=AF.Sqrt, bias=epsr[0:tmp_tl.shape[0], eps_col:eps_col + 1], scale=1.0 / n),
                R + [epsr], [])
            return None

        def phase_xprep(src):
            with ExitStack() as ph:
                xin = [sb(ph, "p0_x%d" % i, [128, 4, D], F32) for i in range(2)]
                xbf = [sb(ph, "p0_b%d" % i, [128, D], BF16) for i in range(2)]
                xTs = [sb(ph, "p0_t%d" % i, [128, 8, 512], BF16) for i in range(2)]
                pss = [pst(ph, "p0_ps%d" % i, [128, 1024], BF16) for i in range(2)]
                for c in range(NCH):
                    xi = xin[c % 2]
                    dma(QS, xi[:], src[c * 512:(c + 1) * 512, :].rearrange("(a p) d -> p a d", p=128), [DR("x", c)], [xi])
                    xt = xTs[c % 2]
                    for a in range(4):
                        xb = xbf[a % 2]
                        vop(POOL, lambda: nc.gpsimd.tensor_copy(out=xb[:], in_=xi[:, a, :]), [xi], [xb])
                        emit_xT_tile(pss[a % 2], xb, a, xt)
                    dma(QS, xT[:, :, c * 512:(c + 1) * 512].rearrange("j p t -> p j t"), xt[:], [xt], [DR("xT", c)])
                kb.barrier()

        def phase_proj(l):
            with ExitStack() as ph:
                win = sb(ph, "p1_win", [128, 8, IN_COLS], BF16)
                wrot = sb(ph, "p1_wrot", [128, 8, 1312], BF16)
                wuq = sb(ph, "p1_wuq", [128, 2, 576], BF16)
                wuqr = sb(ph, "p1_wuqr", [128, 2, 576], BF16)
                wukv = sb(ph, "p1_wukv", [128, 768], BF16)
                wukv_v = sb(ph, "p1_wukvv", [128, 384], BF16)
                stg = sb(ph, "p1_stg", [128, 2, 768], F32)
                gq = sb(ph, "p1_gq", [128, 3], F32)
                for j in range(8):
                    dma(QP, win[:, j, :], w_in[l, j * 128:(j + 1) * 128, :], [], [win])
                def rot(dst_lo, src_lo, ncols, half):
                    dv = wrot[:, :, dst_lo:dst_lo + ncols].rearrange("p j (b two h) -> p j b two h", two=2, h=half)
                    sv = win[:, :, src_lo:src_lo + ncols].rearrange("p j (b two h) -> p j b two h", two=2, h=half)
                    vop(DVE, lambda: nc.vector.tensor_scalar_mul(out=dv[:, :, :, 0, :], in0=sv[:, :, :, 1, :], scalar1=-1.0), [win], [wrot])
                    vop(DVE, lambda: nc.vector.tensor_copy(out=dv[:, :, :, 1, :], in_=sv[:, :, :, 0, :]), [win], [wrot])
                rot(0, 0, 512, 16)
                rot(512, COLKR, 32, 16)
                rot(544, COLC, 768, 32)
                for j in range(2):
                    dma(QS, gq[:, j:j + 1], mla_q_norm[l, j * 128:(j + 1) * 128].rearrange("(p o) -> p o", o=1), [], [gq])
                dma(QS, gq[:, 2:3], mla_kv_norm[l].rearrange("(p o) -> p o", o=1), [], [gq])
                dma(QS, stg[:, :, 0:576], mla_w_uq[l].rearrange("(j p) c -> p j c", p=128), [], [stg])
                for j in range(2):
                    vop(DVE, lambda: nc.vector.tensor_scalar_mul(out=wuq[:, j, :], in0=stg[:, j, 0:576], scalar1=gq[:, j:j + 1]), [stg, gq], [wuq])
                vop(DVE, lambda: nc.vector.memset(wuqr[:], 0.0), [], [wuqr])
                dv = wuqr[:].rearrange("p j (h c) -> p j h c", c=96)[:, :, :, 64:96].rearrange("p j h (two f) -> p j h two f", two=2)
                sv = wuq[:].rearrange("p j (h c) -> p j h c", c=96)[:, :, :, 64:96].rearrange("p j h (two f) -> p j h two f", two=2)
                vop(DVE, lambda: nc.vector.tensor_scalar_mul(out=dv[:, :, :, 0, :], in0=sv[:, :, :, 1, :], scalar1=-1.0), [wuq], [wuqr])
                vop(DVE, lambda: nc.vector.tensor_copy(out=dv[:, :, :, 1, :], in_=sv[:, :, :, 0, :]), [wuq], [wuqr])
                stg2 = sb(ph, "p1_stg2", [128, 768], F32)
                dma(QS, stg2[:], mla_w_ukv[l], [], [stg2])
                vop(DVE, lambda: nc.vector.tensor_scalar_mul(out=wukv[:], in0=stg2[:], scalar1=gq[:, 2:3]), [stg2, gq], [wukv])
                vop(DVE, lambda: nc.vector.tensor_copy(out=wukv_v[:].rearrange("p (h c) -> p h c", c=64),
                                                       in_=wukv[:].rearrange("p (h c) -> p h c", c=128)[:, :, 64:128]), [wukv], [wukv_v])

                xTc = [sb(ph, "p1_x%d" % i, [128, 8, 512], BF16) for i in range(2)]
                tabs = [{k: (sb(ph, "p1_c%s%d" % (k, i), [128, 512], F32), sb(ph, "p1_s%s%d" % (k, i), [128, 512], F32)) for k in "ACB"} for i in range(2)]
                pq = [pst(ph, "p1_pq%d" % i, [128, 512]) for i in range(2)]
                pr = [pst(ph, "p1_pr%d" % i, [128, 512]) for i in range(2)]
                pv = [pst(ph, "p1_pv%d" % i, [128, 512]) for i in range(2)]
                pm = [pst(ph, "p1_pm%d" % i, [128, 512]) for i in range(2)]
                t1 = [sb(ph, "p1_t1%d" % i, [128, 512], F32) for i in range(2)]
                t2 = [sb(ph, "p1_t2%d" % i, [128, 512], F32) for i in range(2)]
                t3 = [sb(ph, "p1_t3%d" % i, [128, 512], F32) for i in range(2)]
                ob = [sb(ph, "p1_ob%d" % i, [128, 512], BF16) for i in range(4)]
                vo = [sb(ph, "p1_vo%d" % i, [128, 640], BF16) for i in range(2)]
                vbo = [sb(ph, "p1_vbo%d" % i, [128, 384], BF16) for i in range(2)]
                cqT = sb(ph, "p1_cqT", [128, 3, 512], BF16)
                sq = sb(ph, "p1_sq", [128, 3, 512], F32)
                rq = sb(ph, "p1_rq", [128, 512], F32)
                rkv = sb(ph, "p1_rkv", [128, 512], F32)
                kr = sb(ph, "p1_kr", [128, 512], BF16)
                rtok = sb(ph, "p1_rtok", [128, 4], F32)
                cnt = {"g": 0, "o": 0}

                def proj_fm(c, xc, lo, m, rlo):
                    i = cnt["g"] % 2
                    cnt["g"] += 1
                    for j in range(8):
                        mm(pq[i][0:m, :], win[:, j, lo:lo + m], xc[:, j, :], j == 0, j == 7, [win, xc], [pq[i]])
                    if rlo is not None:
                        for j in range(8):
                            mm(pr[i][0:m, :], wrot[:, j, rlo:rlo + m], xc[:, j, :], j == 0, j == 7, [wrot, xc], [pr[i]])
                    return i

                def rope_evac(i, m, tab, out_ap, W, extra=None):
                    ct, st = tab
                    vop(DVE, lambda: nc.vector.tensor_tensor(out=t1[i][0:m, :], in0=pq[i][0:m, :], in1=ct[0:m, :], op=ALU.mult), [pq[i], ct], [t1[i]])
                    vop(DVE, lambda: nc.vector.tensor_tensor(out=t2[i][0:m, :], in0=pr[i][0:m, :], in1=st[0:m, :], op=ALU.mult), [pr[i], st], [t2[i]])
                    if extra is None:
                        vop(POOL, lambda: nc.gpsimd.tensor_tensor(out=out_ap, in0=t1[i][0:m, :], in1=t2[i][0:m, :], op=ALU.add), [t1[i], t2[i]], W)
                    else:
                        vop(POOL, lambda: nc.gpsimd.tensor_tensor(out=t3[i][0:m, :], in0=t1[i][0:m, :], in1=t2[i][0:m, :], op=ALU.add), [t1[i], t2[i]], [t3[i]])
                        vop(POOL, lambda: nc.gpsimd.tensor_tensor(out=out_ap, in0=t3[i][0:m, :], in1=extra[0:m, :], op=ALU.mult), [t3[i], extra], W)

                def next_ob():
                    o = ob[cnt["o"] % 4]
                    cnt["o"] += 1
                    return o

                def load_chunk(c):
                    tk = c * 512
                    s0, S, pos0 = seq_of_chunk[c]
                    xc = xTc[c % 2]
                    dma(QS, xc[:], xT[:, :, tk:tk + 512].rearrange("j p t -> p j t"), [DR("xT", c)], [xc])
                    tb = tabs[c % 2]
                    for k in "ACB":
                        dma(QS, tb[k][0][:], c_cos[k][:, pos0:pos0 + 512], [], [tb[k][0]])
                        dma(QS, tb[k][1][:], c_sin[k][:, pos0:pos0 + 512], [], [tb[k][1]])

                PSTOP = os.environ.get("PSTOP")
                if PSTOP == "w":
                    kb.barrier()
                    return
                load_chunk(0)
                for c in range(NCH):
                    if c + 1 < NCH:
                        load_chunk(c + 1)
                    tk = c * 512
                    s0, S, pos0 = seq_of_chunk[c]
                    xc = xTc[c % 2]
                    tb = tabs[c % 2]
                    for ga in range(4):
                        i = proj_fm(c, xc, ga * 128, 128, ga * 128)
                        o = next_ob()
                        rope_evac(i, 128, tb["A"], o[:], [o])
                        dma(QS, QKA[ga, :, tk:tk + 512], o[:], [o], [DR("qka", ga, c)])
                    if PSTOP == "A":
                        kb.barrier()
                        return
                    for gc in range(6):
                        r = C_R[gc % 3]
                        L = S // r
                        i = proj_fm(c, xc, COLC + gc * 128, 128, 544 + gc * 128)
                        o = next_ob()
                        ct, st = tb["C"]
                        vop(DVE, lambda: nc.vector.tensor_tensor(out=t1[i][:], in0=pq[i][:], in1=ct[:], op=ALU.mult), [pq[i], ct], [t1[i]])
                        vop(DVE, lambda: nc.vector.tensor_tensor(out=t2[i][:], in0=pr[i][:], in1=st[:], op=ALU.mult), [pr[i], st], [t2[i]])
                        vop(POOL, lambda: nc.gpsimd.tensor_tensor(out=o[:], in0=t1[i][:], in1=t2[i][:], op=ALU.add), [t1[i], t2[i]], [o])
                        dma(QS, QKC[gc, :, tk:tk + 512], o[:], [o], [DR("qkc", gc, c)])
                    if PSTOP == "C":
                        kb.barrier()
                        return
                    for g in range(3):
                        i = proj_fm(c, xc, COLB + g * 128, 128, None)
                        vop(ACT, lambda: nc.scalar.copy(out=cqT[:, g, :], in_=pq[i][:]), [pq[i]], [cqT])
                        vop(ACT, lambda: nc.scalar.activation(out=sq[:, g, :], in_=pq[i][:], func=AF.Square), [pq[i]], [sq])
                    i = proj_fm(c, xc, COLKR, 32, 512)
                    rope_evac(i, 32, tb["A"], kr[0:32, :], [kr])
                    for j in range(2):
                        mm(pm[0][:], ones_f[:], sq[:, j, :], j == 0, j == 1, [ones_f, sq], [pm[0]])
                    vop(ACT, lambda: nc.scalar.activation(out=t3[0][:], in_=pm[0][:], func=AF.Sqrt, bias=epsr[:, 1:2], scale=1.0 / 256), [pm[0], epsr], [t3[0]])
                    vop(DVE, lambda: nc.vector.reciprocal(out=rq[:], in_=t3[0][:]), [t3[0]], [rq])
                    mm(pm[1][:], ones_f[:], sq[:, 2, :], True, True, [ones_f, sq], [pm[1]])
                    vop(ACT, lambda: nc.scalar.activation(out=t3[1][:], in_=pm[1][:], func=AF.Sqrt, bias=epsr[:, 1:2], scale=1.0 / 128), [pm[1], epsr], [t3[1]])
                    vop(DVE, lambda: nc.vector.reciprocal(out=rkv[:], in_=t3[1][:]), [t3[1]], [rkv])
                    if PSTOP == "Bs":
                        kb.barrier()
                        return
                    for h in range(6):
                        i = cnt["g"] % 2
                        cnt["g"] += 1
                        for j in range(2):
                            mm(pq[i][0:96, :], wuq[:, j, h * 96:(h + 1) * 96], cqT[:, j, :], j == 0, j == 1, [wuq, cqT], [pq[i]])
                        for j in range(2):
                            mm(pr[i][0:96, :], wuqr[:, j, h * 96:(h + 1) * 96], cqT[:, j, :], j == 0, j == 1, [wuqr, cqT], [pr[i]])
                        o = next_ob()
                        rope_evac(i, 96, tb["B"], o[0:96, :], [o], extra=rq)
                        dma(QS, QB[h, :, tk:tk + 512], o[0:96, :], [o], [DR("qb", h, c)])
                        i = cnt["g"] % 2
                        cnt["g"] += 1
                        mm(pq[i][0:64, :], wukv[:, h * 128:h * 128 + 64], cqT[:, 2, :], True, True, [wukv, cqT], [pq[i]])
                        o = next_ob()
                        vop(DVE, lambda: nc.vector.tensor_tensor(out=o[0:64, :], in0=pq[i][0:64, :], in1=rkv[0:64, :], op=ALU.mult), [pq[i], rkv], [o])
                        vop(ACT, lambda: nc.scalar.copy(out=o[64:96, :], in_=kr[0:32, :]), [kr], [o])
                        dma(QS, KBd[h, :, tk:tk + 512], o[0:96, :], [o], [DR("kb", h, c)])
                    if PSTOP == "Bh":
                        kb.barrier()
                        return
                    for a in range(4):
                        i = a % 2
                        tsl = slice(a * 128, (a + 1) * 128)
                        for j in range(8):
                            mm(pv[i][:, 0:256], xc[:, j, tsl], win[:, j, 512:768], j == 0, j == 7, [xc, win], [pv[i]])
                        for j in range(8):
                            mm(pm[i][:, 0:384], xc[:, j, tsl], win[:, j, COLC + 768:COLC + 1152], j == 0, j == 7, [xc, win], [pm[i]])
                        vop(ACT, lambda: nc.scalar.copy(out=vo[i][:, 0:256], in_=pv[i][:, 0:256]), [pv[i]], [vo[i]])
                        vop(ACT, lambda: nc.scalar.copy(out=vo[i][:, 256:640], in_=pm[i][:, 0:384]), [pm[i]], [vo[i]])
                        dma(QS, VA[tk + a * 128:tk + (a + 1) * 128, :], vo[i][:, 0:256], [vo[i]], [DR("va", c, a)])
                        dma(QS, VC[tk + a * 128:tk + (a + 1) * 128, :], vo[i][:, 256:640], [vo[i]], [DR("vc", c, a)])
                        mm(pv[i][:, 256:272], sq[:, 2, tsl], ones_f[:, 0:16], True, True, [sq, ones_f], [pv[i]])
                        vop(ACT, lambda: nc.scalar.activation(out=rtok[:, 2 * i:2 * i + 1], in_=pv[i][:, 256:257], func=AF.Sqrt, bias=epsr[:, 1:2], scale=1.0 / 128),
                            [pv[i], epsr], [rtok])
                        vop(DVE, lambda: nc.vector.reciprocal(out=rtok[:, 2 * i + 1:2 * i + 2], in_=rtok[:, 2 * i:2 * i + 1]), [rtok], [rtok])
                        mm(pm[i][:, 0:384], cqT[:, 2, tsl], wukv_v[:], True, True, [cqT, wukv_v], [pm[i]])
                        vop(DVE, lambda: nc.vector.tensor_scalar_mul(out=vbo[i][:], in0=pm[i][:, 0:384], scalar1=rtok[:, 2 * i + 1:2 * i + 2]), [pm[i], rtok], [vbo[i]])
                        dma(QS, VB[tk + a * 128:tk + (a + 1) * 128, :], vbo[i][:], [vbo[i]], [DR("vb", c, a)])
                kb.barrier()

        def phase_full_attn(l):
            lam_init = 0.8 - 0.6 * math.exp(-0.3 * l)
            with ExitStack() as ph:
                lv = sb(ph, "p2_lv", [1, 128], F32)
                lw = sb(ph, "p2_lw", [1, 8], F32)
                lp = sb(ph, "p2_lp", [1, 64], F32)
                gs = sb(ph, "p2_gs", [64, 4], F32)
                pl = pst(ph, "p2_pl", [128, 512])
                dma(QS, lv[:], diff_lambda[l:l + 1].rearrange("o a b -> o (a b)"), [], [lv])
                dma(QS, gs[:, 0:1], diff_subln[l].rearrange("(p o) -> p o", o=1), [], [gs])
                lvv = lv[:].rearrange("o (a two b) -> o a two b", two=2, b=32)
                vop(DVE, lambda: nc.vector.tensor_tensor(out=lp[:].rearrange("o (a b) -> o a b", b=32), in0=lvv[:, :, 0, :], in1=lvv[:, :, 1, :], op=ALU.mult), [lv], [lp])
                vop(DVE, lambda: nc.vector.reduce_sum(out=lw[:, 0:2], in_=lp[:].rearrange("o (a b) -> o a b", b=32), axis=AX.X), [lp], [lw])
                vop(ACT, lambda: nc.scalar.activation(out=lw[:, 2:4], in_=lw[:, 0:2], func=AF.Exp), [lw], [lw])
                vop(DVE, lambda: nc.vector.tensor_tensor(out=lw[:, 4:5], in0=lw[:, 3:4], in1=lw[:, 2:3], op=ALU.subtract), [lw], [lw])
                vop(DVE, lambda: nc.vector.tensor_scalar_add(out=lw[:, 5:6], in0=lw[:, 4:5], scalar1=-lam_init), [lw], [lw])
                l16 = sb(ph, "p2_l16", [1, 16], F32)
                vop(DVE, lambda: nc.vector.memset(l16[:], 0.0), [], [l16])
                vop(DVE, lambda: nc.vector.tensor_scalar(out=l16[:], in0=l16[:], scalar1=lw[:, 5:6], scalar2=None, op0=ALU.add), [l16, lw], [l16])
                mm(pl[0:64, 0:16], ones_f[0:1, 0:64], l16[0:1, :], True, True, [ones_f, l16], [pl])
                vop(ACT, lambda: nc.scalar.copy(out=gs[:, 1:2], in_=pl[0:64, 0:1]), [pl], [gs])
                vop(DVE, lambda: nc.vector.tensor_scalar_mul(out=gs[:, 2:3], in0=gs[:, 0:1], scalar1=1.0 - lam_init), [gs], [gs])

                qT = [sb(ph, "p2_q%d" % i, [96, SMAX], BF16) for i in range(4)]
                kT = [sb(ph, "p2_k%d" % i, [96, SMAX], BF16) for i in range(4)]
                vt = [sb(ph, "p2_v%d" % i, [128, SMAX // 128, 128], BF16) for i in range(2)]
                pT = [sb(ph, "p2_p%d" % i, [128, 1024], BF16) for i in range(3)]
                psc = [pst(ph, "p2_s%d" % i, [128, 1024]) for i in range(2)]
                pac = [pst(ph, "p2_a%d" % i, [128, 512]) for i in range(2)]
                prm = pst(ph, "p2_rm", [128, 512])
                dsh = [sb(ph, "p2_d%d" % i, [64, 512], F32) for i in range(2)]
                oc = [sb(ph, "p2_o%d" % i, [64, 512], F32) for i in range(3)]
                tq = sb(ph, "p2_tq", [64, 512], F32)
                obf = [sb(ph, "p2_ob%d" % i, [64, 512], BF16) for i in range(2)]
                for v in vt:
                    vop(DVE, lambda: nc.vector.memset(v[:, :, 64:128], 1.0), [], [v])
                st = {"s": 0, "p": 0, "a": 0, "o": 0, "qk": 0, "v": 0}
                LA = 1
                groups = []

                def norm_post(acc):
                    d = dsh[st["o"] % 2]
                    o = oc[st["o"] % 3]
                    st["o"] += 1
                    vop(ACT, lambda: nc.scalar.copy(out=d[:], in_=acc[64:128, :]), [acc], [d])
                    vop(DVE, lambda: nc.vector.reciprocal(out=d[:], in_=d[:]), [d], [d])
                    vop(DVE, lambda: nc.vector.tensor_tensor(out=o[:], in0=acc[0:64, :], in1=d[:], op=ALU.mult), [acc, d], [o])
                    return o

                def make_its(q_, k_, dk, scale, v, S, qc, post):
                    nkt = S // 128
                    acc = pac[st["a"] % 2]
                    st["a"] += 1
                    return [dict(q=q_, k=k_, dk=dk, scale=scale, v=v, kt=kt, qc=qc, acc=acc, first=(kt == 0), last=(kt == nkt - 2),
                                 post=(post if kt == nkt - 2 else None)) for kt in range(0, nkt, 2)]

                def load_v(src, h, s0, S, v):
                    for k0 in range(0, S // 128, 8):
                        dma(QS, v[:, k0:k0 + 8, 0:64], src[s0 + k0 * 128:s0 + (k0 + 8) * 128, h * 64:(h + 1) * 64].rearrange("(kt p) d -> p kt d", p=128), [DR("vsrc")], [v])

                def load_qk(qd, kd, dk, s0, S, slot):
                    dma(QS, slot[0][0:dk, 0:S], qd[:, s0:s0 + S], [DR("qsrc")], [slot[0]])
                    dma(QS, slot[1][0:dk, 0:S], kd[:, s0:s0 + S], [DR("ksrc")], [slot[1]])

                def take_slots(n):
                    r_ = [(qT[(st["qk"] + c) % 4], kT[(st["qk"] + c) % 4]) for c in range(n)]
                    st["qk"] += n
                    v = vt[st["v"] % 2]
                    st["v"] += 1
                    return r_, v

                sa = 32.0 ** -0.5
                sbq = 96.0 ** -0.5
                for (s0, S) in zip(seq_starts, seqs):
                    for h in range(4):
                        slots, v = take_slots(2)

                        def load(h=h, s0=s0, S=S, slots=slots, v=v):
                            load_v(VA, h, s0, S, v)
                            for cpt in range(2):
                                u = h * 2 + cpt
                                load_qk(QKA[u // 4, (u % 4) * 32:(u % 4) * 32 + 32, :], QKA[2 + u // 4, (u % 4) * 32:(u % 4) * 32 + 32, :], 32, s0, S, slots[cpt])

                        its = []
                        for qc in range(S // 512):
                            hold = {}

                            def post0(acc, hold=hold):
                                hold["o0"] = norm_post(acc)

                            def post1(acc, hold=hold, qc=qc, h=h, s0=s0):
                                o1 = norm_post(acc)
                                o0 = hold["o0"]
                                vop(DVE, lambda: nc.vector.scalar_tensor_tensor(out=o0[:], in0=o1[:], scalar=gs[:, 1:2], in1=o0[:], op0=ALU.mult, op1=ALU.add), [o1, o0, gs], [o0])
                                vop(ACT, lambda: nc.scalar.activation(out=tq[:], in_=o0[:], func=AF.Square), [o0], [tq])
                                mm(prm[0:64, :], ones_f[0:64, 0:64], tq[:], True, True, [ones_f, tq], [prm])
                                vop(ACT, lambda: nc.scalar.activation(out=tq[:], in_=prm[0:64, :], func=AF.Sqrt, bias=epsr[0:64, 1:2], scale=1.0 / 64), [prm, epsr], [tq])
                                vop(DVE, lambda: nc.vector.reciprocal(out=tq[:], in_=tq[:]), [tq], [tq])
                                ob_ = obf[qc % 2]
                                vop(DVE, lambda: nc.vector.scalar_tensor_tensor(out=ob_[:], in0=o0[:], scalar=gs[:, 2:3], in1=tq[:], op0=ALU.mult, op1=ALU.mult), [o0, tq, gs], [ob_])
                                tk = s0 + qc * 512
                                dma(QS, mixT[h // 2, (h % 2) * 64:(h % 2) * 64 + 64, tk:tk + 512], ob_[:], [ob_], [DR("mixT", tk // 512)])

                            its += make_its(slots[0][0], slots[0][1], 32, sa, v, S, qc, post0)
                            its += make_its(slots[1][0], slots[1][1], 32, sa, v, S, qc, post1)
                        groups.append(dict(load=load, its=its))
                    for h in range(6):
                        slots, v = take_slots(1)

                        def load(h=h, s0=s0, S=S, slots=slots, v=v):
                            load_v(VB, h, s0, S, v)
                            load_qk(QB[h], KBd[h], 96, s0, S, slots[0])

                        its = []
                        for qc in range(S // 512):
                            def post(acc, qc=qc, h=h, s0=s0):
                                o0 = norm_post(acc)
                                ob_ = obf[qc % 2]
                                vop(POOL, lambda: nc.gpsimd.tensor_copy(out=ob_[:], in_=o0[:]), [o0], [ob_])
                                tk = s0 + qc * 512
                                f0 = 256 + h * 64
                                dma(QS, mixT[f0 // 128, f0 % 128:f0 % 128 + 64, tk:tk + 512], ob_[:], [ob_], [DR("mixT", tk // 512)])

                            its += make_its(slots[0][0], slots[0][1], 96, sbq, v, S, qc, post)
                        groups.append(dict(load=load, its=its))

                flat = []
                for gi, g in enumerate(groups):
                    for ii, it in enumerate(g["its"]):
                        it["g"] = gi
                        it["ii"] = ii
                        flat.append(it)

                def emit_qk(it):
                    ps = psc[st["s"] % 2]
                    st["s"] += 1
                    it["ps"] = ps
                    dk, kt, qc = it["dk"], it["kt"], it["qc"]
                    for u in range(2):
                        mm(ps[:, u * 512:(u + 1) * 512], it["k"][0:dk, (kt + u) * 128:(kt + u + 1) * 128], it["q"][0:dk, qc * 512:(qc + 1) * 512], True, True,
                           [it["k"], it["q"]], [ps], sig=(u == 1))

                def flush(it):
                    ps = it["ps"]
                    p = pT[st["p"] % 3]
                    st["p"] += 1
                    vop(ACT, lambda: nc.scalar.activation(out=p[:], in_=ps[:], func=AF.Exp, scale=it["scale"]), [ps], [p])
                    for u in range(2):
                        mm(it["acc"][:], it["v"][:, it["kt"] + u, :], p[:, u * 512:(u + 1) * 512], it["first"] and u == 0, it["last"] and u == 1, [it["v"], p], [it["acc"]],
                           sig=(u == 1))
                    if it["post"] is not None:
                        it["post"](it["acc"])

                pending = []
                groups[0]["load"]()
                for it in flat:
                    if it["ii"] == LA + 1 and it["g"] + 1 < len(groups):
                        groups[it["g"] + 1]["load"]()
                    emit_qk(it)
                    pending.append(it)
                    if len(pending) > LA:
                        flush(pending.pop(0))
                while pending:
                    flush(pending.pop(0))
                kb.barrier()

        def phase_dilated(l):
            with ExitStack() as ph:
                band = sb(ph, "p4_band", [128, 384], F32)
                dma(QS, band[:], c_band, [], [band])
                qT = [sb(ph, "p4_q%d" % i, [128, SMAX], BF16) for i in range(2)]
                kT = [sb(ph, "p4_k%d" % i, [128, SMAX], BF16) for i in range(2)]
                stq = sb(ph, "p4_stq", [128, SMAX], BF16)
                stk = sb(ph, "p4_stk", [128, SMAX], BF16)
                vt = [sb(ph, "p4_v%d" % i, [128, SMAX // 128, 2, 128], BF16) for i in range(2)]
                nsb = [sb(ph, "p4_n%d" % i, [64, SMAX], BF16) for i in range(2)]
                dsb = [sb(ph, "p4_d%d" % i, [64, SMAX], F32) for i in range(2)]
                psc = [pst(ph, "p4_s%d" % i, [128, 512]) for i in range(3)]
                pac = [pst(ph, "p4_a%d" % i, [128, 512]) for i in range(3)]
                pe_ = [sb(ph, "p4_e%d" % i, [128, 384], F32) for i in range(3)]
                pT = [sb(ph, "p4_p%d" % i, [128, 384], BF16) for i in range(3)]
                for v in vt:
                    vop(DVE, lambda: nc.vector.memset(v[:, :, :, 64:128], 1.0), [], [v])
                st = {"s": 0, "a": 0, "b": 0}
                scale = 64.0 ** -0.5
                glist = [(si, s0, S, g) for si, (s0, S) in enumerate(zip(seq_starts, seqs)) for g in range(3)]

                def load_group(gi):
                    si, s0, S, g = glist[gi]
                    r = C_R[g]
                    nlt = (S // r) // 128
                    q_, k_, v_ = qT[gi % 2], kT[gi % 2], vt[gi % 2]
                    if r == 1:
                        dma(QS, q_[:, 0:S], QKC[g, :, s0:s0 + S], [DR("qkc")], [q_])
                        dma(QS, k_[:, 0:S], QKC[3 + g, :, s0:s0 + S], [DR("qkc")], [k_])
                    else:
                        dma(QS, stq[:, 0:S], QKC[g, :, s0:s0 + S], [DR("qkc")], [stq])
                        dma(QS, stk[:, 0:S], QKC[3 + g, :, s0:s0 + S], [DR("qkc")], [stk])
                        vop(DVE, lambda: nc.vector.tensor_copy(out=q_[:, 0:S].rearrange("p (r m) -> p m r", r=r), in_=stq[:, 0:S].rearrange("p (m r) -> p m r", r=r)), [stq], [q_])
                        vop(POOL, lambda: nc.gpsimd.tensor_copy(out=k_[:, 0:S].rearrange("p (r m) -> p m r", r=r), in_=stk[:, 0:S].rearrange("p (m r) -> p m r", r=r)), [stk], [k_])
                    for rho in range(r):
                        for j in range(2):
                            src = VC[s0:s0 + S, g * 128 + j * 64:g * 128 + (j + 1) * 64].rearrange("(kt p r) d -> r p kt d", p=128, r=r)[rho]
                            for k0 in range(0, nlt, 8):
                                k1 = min(nlt, k0 + 8)
                                dma(QS, v_[:, rho * nlt + k0:rho * nlt + k1, j, 0:64], src[:, k0:k1, :], [DR("vc")], [v_])

                load_group(0)
                for gi, (si, s0, S, g) in enumerate(glist):
                    if True:
                        r = C_R[g]
                        L = S // r
                        nlt = L // 128
                        q_, k_, v_ = qT[gi % 2], kT[gi % 2], vt[gi % 2]
                        if gi + 1 < len(glist):
                            load_group(gi + 1)
                        for j in range(2):
                            nb_, db_ = nsb[j], dsb[j]
                            def emit_qk4(rho, t, j=j, q_=q_, k_=k_, L=L, nlt=nlt):
                                base = rho * L + t * 128
                                tiles = [tt for tt in (t - 1, t, t + 1) if 0 <= tt < nlt]
                                i3 = st["s"] % 3
                                st["s"] += 1
                                ps = psc[i3]
                                for tt in tiles:
                                    mi = tt - t + 1
                                    kb0 = rho * L + tt * 128
                                    mm(ps[:, mi * 128:(mi + 1) * 128], k_[j * 64:(j + 1) * 64, kb0:kb0 + 128], q_[j * 64:(j + 1) * 64, base:base + 128],
                                       True, True, [k_, q_], [ps], sig=(tt == tiles[-1]))
                                return dict(rho=rho, t=t, tiles=tiles, i3=i3)

                            def flush4(cx, j=j, v_=v_, nb_=nb_, db_=db_, r=r, S=S, nlt=nlt):
                                rho, t, tiles, i3 = cx["rho"], cx["t"], cx["tiles"], cx["i3"]
                                ps, e_, p_ = psc[i3], pe_[i3], pT[i3]
                                lo, hi = (tiles[0] - t + 1) * 128, (tiles[-1] - t + 2) * 128
                                vop(ACT, lambda: nc.scalar.activation(out=e_[:, lo:hi], in_=ps[:, lo:hi], func=AF.Exp, scale=scale), [ps], [e_])
                                vop(POOL, lambda: nc.gpsimd.tensor_tensor(out=p_[:, lo:hi], in0=e_[:, lo:hi], in1=band[:, lo:hi], op=ALU.mult), [e_, band], [p_])
                                acc = pac[st["a"] % 3]
                                st["a"] += 1
                                for n_, tt in enumerate(tiles):
                                    mi = tt - t + 1
                                    mm(acc[:, 0:128], v_[:, rho * nlt + tt, j, :], p_[:, mi * 128:(mi + 1) * 128], n_ == 0, n_ == len(tiles) - 1, [v_, p_], [acc])
                                if r == 1:
                                    no = nb_[:, t * 128:(t + 1) * 128]
                                    do = db_[:, t * 128:(t + 1) * 128]
                                else:
                                    no = nb_[:, 0:S].rearrange("p (m r) -> p r m", r=r)[:, rho, t * 128:(t + 1) * 128]
                                    do = db_[:, 0:S].rearrange("p (m r) -> p r m", r=r)[:, rho, t * 128:(t + 1) * 128]
                                vop(DVE, lambda: nc.vector.tensor_copy(out=no, in_=acc[0:64, 0:128]), [acc], [nb_])
                                vop(ACT, lambda: nc.scalar.copy(out=do, in_=acc[64:128, 0:128]), [acc], [db_])

                            pend = []
                            for rho in range(r):
                                for t in range(nlt):
                                    pend.append(emit_qk4(rho, t))
                                    if len(pend) > 2:
                                        flush4(pend.pop(0))
                            while pend:
                                flush4(pend.pop(0))
                            dma(QS, NCd[g * 2 + j, :, s0:s0 + S], nb_[:, 0:S], [nb_], [DR("ncd", si, g, j)])
                            dma(QS, DCd[g * 2 + j, :, s0:s0 + S], db_[:, 0:S], [db_], [DR("dcd", si, g, j)])
                kb.barrier()
            with ExitStack() as ph:
                nn = [sb(ph, "p4_nn%d" % i, [64, 6, 512], BF16) for i in range(2)]
                dd = [sb(ph, "p4_dd%d" % i, [64, 6, 512], F32) for i in range(2)]
                tt_ = [sb(ph, "p4_tt%d" % i, [64, 2, 512], F32) for i in range(2)]
                oo = [sb(ph, "p4_oo%d" % i, [64, 6, 512], BF16) for i in range(2)]
                for c in range(NCH):
                    tk = c * 512
                    n_, d_, t_, o_ = nn[c % 2], dd[c % 2], tt_[c % 2], oo[c % 2]
                    dma(QS, n_[:], NCd[:, :, tk:tk + 512].rearrange("h p t -> p h t"), [], [n_])
                    dma(QS, d_[:], DCd[:, :, tk:tk + 512].rearrange("h p t -> p h t"), [], [d_])
                    vop(DVE, lambda: nc.vector.tensor_tensor(out=t_[:], in0=d_[:, 0:2, :], in1=d_[:, 2:4, :], op=ALU.add), [d_], [t_])
                    vop(DVE, lambda: nc.vector.tensor_tensor(out=t_[:], in0=t_[:], in1=d_[:, 4:6, :], op=ALU.add), [d_, t_], [t_])
                    vop(DVE, lambda: nc.vector.reciprocal(out=t_[:], in_=t_[:]), [t_], [t_])
                    for g in range(3):
                        vop(POOL, lambda: nc.gpsimd.tensor_tensor(out=o_[:, 2 * g:2 * g + 2, :], in0=n_[:, 2 * g:2 * g + 2, :], in1=t_[:], op=ALU.mult), [n_, t_], [o_])
                    for hh in range(6):
                        f0 = 640 + hh * 64
                        dma(QS, mixT[f0 // 128, f0 % 128:f0 % 128 + 64, tk:tk + 512], o_[:, hh, :], [o_], [DR("mixT", c)])
                kb.barrier()

        def layer_norm(z, gb, bb, st4, junk):
            vop(DVE, lambda: nc.vector.memset(st4[:, 0:2], 0.0), [], [st4])
            vop(ACT, lambda: nc.scalar.activation(out=junk[:], in_=z[:], func=AF.Identity, accum_out=st4[:, 0:1]), [z, st4], [junk, st4])
            vop(ACT, lambda: nc.scalar.activation(out=junk[:], in_=z[:], func=AF.Square, accum_out=st4[:, 1:2]), [z, st4], [junk, st4])
            vop(DVE, lambda: nc.vector.tensor_scalar_mul(out=st4[:, 2:3], in0=st4[:, 0:1], scalar1=1.0 / D), [st4], [st4])
            vop(DVE, lambda: nc.vector.tensor_tensor(out=st4[:, 3:4], in0=st4[:, 2:3], in1=st4[:, 2:3], op=ALU.mult), [st4], [st4])
            vop(DVE, lambda: nc.vector.scalar_tensor_tensor(out=st4[:, 4:5], in0=st4[:, 1:2], scalar=1.0 / D, in1=st4[:, 3:4], op0=ALU.mult, op1=ALU.subtract), [st4], [st4])
            vop(ACT, lambda: nc.scalar.activation(out=st4[:, 5:6], in_=st4[:, 4:5], func=AF.Sqrt, bias=epsr[:, 0:1], scale=1.0), [st4, epsr], [st4])
            vop(DVE, lambda: nc.vector.reciprocal(out=st4[:, 6:7], in_=st4[:, 5:6]), [st4], [st4])
            vop(DVE, lambda: nc.vector.tensor_scalar(out=z[:], in0=z[:], scalar1=st4[:, 2:3], scalar2=st4[:, 6:7], op0=ALU.subtract, op1=ALU.mult), [z, st4], [z])
            vop(POOL, lambda: nc.gpsimd.tensor_tensor(out=z[:], in0=z[:], in1=gb[:], op=ALU.mult), [z, gb], [z])
            vop(POOL, lambda: nc.gpsimd.tensor_tensor(out=z[:], in0=z[:], in1=bb[:], op=ALU.add), [z, bb], [z])

        def phase_out_ln1(l, xsrc):
            with ExitStack() as ph:
                wout = sb(ph, "p5_wout", [128, 8, D], BF16)
                wr = sb(ph, "p5_wr", [128, 8, 36], F32)
                gb = sb(ph, "p5_g", [128, D], F32)
                bb = sb(ph, "p5_b", [128, D], F32)
                for j in range(8):
                    dma(QP, wout[:, j, :], w_out[l, j * 128:(j + 1) * 128, :], [], [wout])
                dma(QS, wr[:, :, 0:4], w_coarse[l].rearrange("(j p) g -> p j g", p=128), [], [wr])
                for g in range(4):
                    dma(QS, wr[:, :, 4 + g * 8:12 + g * 8], w_fine[l, g].rearrange("(j p) e -> p j e", p=128), [], [wr])
                dma(QS, gb[:], ln1_g[l:l + 1, :].partition_broadcast(128), [], [gb])
                dma(QS, bb[:], ln1_b[l:l + 1, :].partition_broadcast(128), [], [bb])
                mx = [sb(ph, "p5_m%d" % i, [128, 8, 512], BF16) for i in range(2)]
                xr = [sb(ph, "p5_x%d" % i, [128, 4, D], F32) for i in range(2)]
                z = [sb(ph, "p5_z%d" % i, [128, D], F32) for i in range(3)]
                zb = [sb(ph, "p5_zb%d" % i, [128, D], BF16) for i in range(2)]
                junk = sb(ph, "p5_junk", [128, D], F32)
                st4 = [sb(ph, "p5_st%d" % i, [128, 8], F32) for i in range(2)]
                x1T = [sb(ph, "p5_xT%d" % i, [128, 8, 128], F32) for i in range(2)]
                po = [pst(ph, "p5_po%d" % i, [128, 1024]) for i in range(2)]
                pt = [pst(ph, "p5_pt%d" % i, [128, 512]) for i in range(2)]
                plg = pst(ph, "p5_plg", [128, 512])
                lg = sb(ph, "p5_lg", [128, 36], F32)
                sm = sb(ph, "p5_sm", [128, 64], F32)

                def load(c):
                    dma(QS, mx[c % 2][:], mixT[:, :, c * 512:(c + 1) * 512].rearrange("j p t -> p j t"), [DR("mixT", c)], [mx[c % 2]])
                    dma(QS, xr[c % 2][:], xsrc[c * 512:(c + 1) * 512, :].rearrange("(a p) d -> p a d", p=128), [DR("x", c)], [xr[c % 2]])

                load(0)
                for c in range(NCH):
                    if c + 1 < NCH:
                        load(c + 1)
                    m_, x_ = mx[c % 2], xr[c % 2]
                    for a in range(4):
                        it = c * 4 + a
                        tok0 = it * 128
                        p_ = po[it % 2]
                        for hf in range(2):
                            for j in range(8):
                                mm(p_[:, hf * 512:(hf + 1) * 512], m_[:, j, a * 128:(a + 1) * 128], wout[:, j, hf * 512:(hf + 1) * 512], j == 0, j == 7, [m_, wout], [p_])
                        z_ = z[it % 3]
                        vop(DVE, lambda: nc.vector.scalar_tensor_tensor(out=z_[:], in0=x_[:, a, :], scalar=DN_ALPHA, in1=p_[:], op0=ALU.mult, op1=ALU.add), [x_, p_], [z_])
                        layer_norm(z_, gb, bb, st4[it % 2], junk)
                        dma(QS, x1d[tok0:tok0 + 128, :], z_[:], [z_], [DR("x1", it)])
                        zb_ = zb[it % 2]
                        vop(ACT, lambda: nc.scalar.copy(out=zb_[:], in_=z_[:]), [z_], [zb_])
                        dma(QS, x1b[tok0:tok0 + 128, :], zb_[:], [zb_], [DR("x1b", it)])
                        xT_ = x1T[it % 2]
                        for hf in range(2):
                            for jj in range(4):
                                j = hf * 4 + jj
                                tr(pt[hf][:, jj * 128:(jj + 1) * 128], z_[:, j * 128:(j + 1) * 128], ident_f[:], [z_, ident_f], [pt[hf]], sig=(jj == 3))
                            vop(ACT, lambda: nc.scalar.copy(out=xT_[:, hf * 4:(hf + 1) * 4, :], in_=pt[hf][:].rearrange("p (j t) -> p j t", j=4)), [pt[hf]], [xT_])
                        for j in range(8):
                            mm(plg[:, 0:36], xT_[:, j, :], wr[:, j, :], j == 0, j == 7, [xT_, wr], [plg])
                        vop(DVE, lambda: nc.vector.tensor_copy(out=lg[:], in_=plg[:, 0:36]), [plg], [lg])
                        vop(DVE, lambda: nc.vector.reduce_max(out=sm[:, 0:1], in_=lg[:, 0:4], axis=AX.X), [lg], [sm])
                        vop(DVE, lambda: nc.vector.tensor_scalar_mul(out=sm[:, 1:2], in0=sm[:, 0:1], scalar1=-1.0), [sm], [sm])
                        vop(DVE, lambda: nc.vector.memset(sm[:, 2:3], 0.0), [], [sm])
                        vop(ACT, lambda: nc.scalar.activation(out=sm[:, 44:48], in_=lg[:, 0:4], func=AF.Exp, bias=sm[:, 1:2], scale=1.0, accum_out=sm[:, 2:3]), [lg, sm], [sm])
                        vop(DVE, lambda: nc.vector.reciprocal(out=sm[:, 3:4], in_=sm[:, 2:3]), [sm], [sm])
                        vop(DVE, lambda: nc.vector.tensor_scalar(out=sm[:, 4:8], in0=lg[:, 0:4], scalar1=sm[:, 0:1], scalar2=None, op0=ALU.is_equal), [lg, sm], [sm])
                        vop(DVE, lambda: nc.vector.tensor_scalar_mul(out=sm[:, 8:16], in0=lg[:, 4:12], scalar1=sm[:, 4:5]), [lg, sm], [sm])
                        for g in range(1, 4):
                            vop(DVE, lambda: nc.vector.scalar_tensor_tensor(out=sm[:, 8:16], in0=lg[:, 4 + g * 8:12 + g * 8], scalar=sm[:, 4 + g:5 + g], in1=sm[:, 8:16],
                                                                            op0=ALU.mult, op1=ALU.add), [lg, sm], [sm])
                        vop(DVE, lambda: nc.vector.max(out=sm[:, 16:24], in_=sm[:, 8:16]), [sm], [sm])
                        vop(DVE, lambda: nc.vector.tensor_scalar(out=sm[:, 24:32], in0=sm[:, 8:16], scalar1=sm[:, 16:17], scalar2=None, op0=ALU.is_equal), [sm], [sm])
                        vop(DVE, lambda: nc.vector.tensor_scalar(out=sm[:, 32:40], in0=sm[:, 8:16], scalar1=sm[:, 17:18], scalar2=None, op0=ALU.is_equal), [sm], [sm])
                        vop(DVE, lambda: nc.vector.tensor_tensor(out=sm[:, 40:41], in0=sm[:, 17:18], in1=sm[:, 16:17], op=ALU.subtract), [sm], [sm])
                        vop(ACT, lambda: nc.scalar.activation(out=sm[:, 41:42], in_=sm[:, 40:41], func=AF.Exp), [sm], [sm])
                        vop(DVE, lambda: nc.vector.tensor_scalar_add(out=sm[:, 42:43], in0=sm[:, 41:42], scalar1=1.0), [sm], [sm])
                        vop(DVE, lambda: nc.vector.reciprocal(out=sm[:, 43:44], in_=sm[:, 42:43]), [sm], [sm])
                        vop(DVE, lambda: nc.vector.tensor_tensor(out=wts[:, it, 0:1], in0=sm[:, 43:44], in1=sm[:, 3:4], op=ALU.mult), [sm], [wts])
                        vop(DVE, lambda: nc.vector.tensor_tensor(out=wts[:, it, 1:2], in0=wts[:, it, 0:1], in1=sm[:, 41:42], op=ALU.mult), [sm, wts], [wts])
                        for g in range(4):
                            vop(DVE, lambda: nc.vector.tensor_scalar_mul(out=E0[:, it, g * 8:(g + 1) * 8], in0=sm[:, 24:32], scalar1=sm[:, 4 + g:5 + g]), [sm], [E0])
                            vop(DVE, lambda: nc.vector.tensor_scalar_mul(out=E1[:, it, g * 8:(g + 1) * 8], in0=sm[:, 32:40], scalar1=sm[:, 4 + g:5 + g]), [sm], [E1])
                kb.barrier()

        def phase_route(l):
            with ExitStack() as ph:
                e01 = sb(ph, "p6_e01", [128, NT, 32], F32)
                run = sb(ph, "p6_run", [128, NT + 1, 32], F32)
                cb = sb(ph, "p6_cb", [128, 32], F32)
                ci = sb(ph, "p6_ci", [128, 32], I32)
                inc = [sb(ph, "p6_inc%d" % i, [128, 32], F32) for i in range(2)]
                pstart = sb(ph, "p6_ps", [128, 32], F32)
                pend = sb(ph, "p6_pe", [128, 32], F32)
                destf = sb(ph, "p6_df", [128, NT, 2], F32)
                tmp = [sb(ph, "p6_t%d" % i, [128, 32], F32) for i in range(2)]
                tmp2 = [sb(ph, "p6_u%d" % i, [128, 32], F32) for i in range(2)]
                bex = sb(ph, "p6_bex", [128, NB], F32)
                pc = [pst(ph, "p6_pc%d" % i, [128, 512]) for i in range(2)]
                vop(POOL, lambda: nc.gpsimd.tensor_tensor(out=e01[:], in0=E0[:], in1=E1[:], op=ALU.add), [E0, E1], [e01])
                vop(DVE, lambda: nc.vector.memset(run[:, 0, :], 0.0), [], [run])
                for i in range(NT):
                    vop(DVE, lambda: nc.vector.tensor_tensor(out=run[:, i + 1, :], in0=run[:, i, :], in1=e01[:, i, :], op=ALU.add), [run, e01], [run])
                mm(pc[0][:, 0:32], ones_f[:], run[:, NT, :], True, True, [ones_f, run], [pc[0]])
                vop(DVE, lambda: nc.vector.tensor_scalar_add(out=cb[:], in0=pc[0][:, 0:32], scalar1=float(BLK - 1)), [pc[0]], [cb])
                vop(DVE, lambda: nc.vector.tensor_copy(out=ci[:], in_=cb[:]), [cb], [ci])
                sh = int(math.log2(BLK))
                vop(DVE, lambda: nc.vector.tensor_single_scalar(out=ci[:], in_=ci[:], scalar=sh, op=ALU.arith_shift_right), [ci], [ci])
                vop(DVE, lambda: nc.vector.tensor_single_scalar(out=ci[:], in_=ci[:], scalar=sh, op=ALU.logical_shift_left), [ci], [ci])
                vop(DVE, lambda: nc.vector.tensor_copy(out=cb[:], in_=ci[:]), [ci], [cb])
                vop(DVE, lambda: nc.vector.tensor_copy(out=inc[0][:], in_=cb[:]), [cb], [inc[0]])
                cur = 0
                s = 1
                while s < 32:
                    a_, b_ = inc[cur], inc[1 - cur]
                    vop(DVE, lambda: nc.vector.tensor_copy(out=b_[:, 0:s], in_=a_[:, 0:s]), [a_], [b_])
                    vop(DVE, lambda: nc.vector.tensor_tensor(out=b_[:, s:32], in0=a_[:, s:32], in1=a_[:, 0:32 - s], op=ALU.add), [a_], [b_])
                    cur = 1 - cur
                    s *= 2
                vop(DVE, lambda: nc.vector.tensor_copy(out=pend[:], in_=inc[cur][:]), [inc[cur]], [pend])
                vop(DVE, lambda: nc.vector.tensor_tensor(out=pstart[:], in0=pend[:], in1=cb[:], op=ALU.subtract), [pend, cb], [pstart])
                for i in range(NT):
                    p_ = pc[i % 2]
                    mm(p_[:, 0:32], tri_f[:], e01[:, i, :], True, False, [tri_f, e01], [p_], sig=False)
                    mm(p_[:, 0:32], ones_f[:], run[:, i, :], False, True, [ones_f, run], [p_])
                    t_ = tmp[i % 2]
                    u_ = tmp2[i % 2]
                    vop(DVE, lambda: nc.vector.tensor_tensor(out=t_[:], in0=p_[:, 0:32], in1=pstart[:], op=ALU.add), [p_, pstart], [t_])
                    for k, EE in enumerate((E0, E1)):
                        vop(POOL, lambda: nc.gpsimd.tensor_tensor(out=u_[:], in0=t_[:], in1=EE[:, i, :], op=ALU.mult), [t_, EE], [u_])
                        vop(DVE, lambda: nc.vector.reduce_sum(out=destf[:, i, k:k + 1], in_=u_[:], axis=AX.X), [u_], [destf])
                vop(DVE, lambda: nc.vector.tensor_copy(out=dest_i[:], in_=destf[:]), [destf], [dest_i])
                vop(DVE, lambda: nc.vector.memset(bex[:], 0.0), [], [bex])
                for e in range(NEXP):
                    vop(DVE, lambda: nc.vector.scalar_tensor_tensor(out=bex[:], in0=misc[:, 1:1 + NB], scalar=pend[:, e:e + 1], in1=bex[:], op0=ALU.is_ge, op1=ALU.add),
                        [misc, pend, bex], [bex])
                vop(DVE, lambda: nc.vector.tensor_scalar(out=bex[:], in0=bex[:], scalar1=float(NEXP - 1), scalar2=float(l * NEXP), op0=ALU.min, op1=ALU.add), [bex], [bex])
                vop(DVE, lambda: nc.vector.tensor_scalar(out=bex[:], in0=bex[:], scalar1=128.0, scalar2=misc[:, 0:1], op0=ALU.mult, op1=ALU.add), [bex, misc], [bex])
                vop(DVE, lambda: nc.vector.tensor_copy(out=widx[:], in_=bex[:]), [bex], [widx])
                kb.barrier()
                xb = [sb(ph, "p7_x%d" % i, [128, D], BF16) for i in range(4)]
                for i in range(NT):
                    b_ = xb[i % 4]
                    dma(QS, b_[:], x1b[i * 128:(i + 1) * 128, :], [], [b_])
                    for k in range(2):
                        kb.dma(QP, lambda: nc.gpsimd.indirect_dma_start(out=xs, out_offset=bass.IndirectOffsetOnAxis(ap=dest_i[:, i, k:k + 1], axis=0),
                                                                         in_=b_[:], in_offset=None), R=rs(b_, dest_i), W=[])
                kb.barrier()

        def phase_experts(l):
            w1v = w1.rearrange("l e (p j) c -> (l e p) (j c)", j=8)
            w3v = w3.rearrange("l e (p j) c -> (l e p) (j c)", j=8)
            w2v = w2.rearrange("l e (p j) c -> (l e p) (j c)", j=4)
            with ExitStack() as ph:
                W1 = [sb(ph, "p8_w1%d" % i, [128, 8, DE], BF16) for i in range(2)]
                W3 = [sb(ph, "p8_w3%d" % i, [128, 8, DE], BF16) for i in range(2)]
                W2 = [sb(ph, "p8_w2%d" % i, [128, 4, D], BF16) for i in range(2)]
                xt = [sb(ph, "p8_x%d" % i, [128, 4, D], BF16) for i in range(2)]
                xsT = [sb(ph, "p8_xT%d" % i, [128, 8, 512], BF16) for i in range(2)]
                hT = [sb(ph, "p8_h%d" % i, [128, 4, 512], BF16) for i in range(2)]
                sl = [sb(ph, "p8_sl%d" % i, [128, 512], F32) for i in range(2)]
                yo = [sb(ph, "p8_y%d" % i, [128, D], F32) for i in range(2)]
                ptr = [pst(ph, "p8_pt%d" % i, [128, 1024], BF16) for i in range(2)]
                p1 = [pst(ph, "p8_p1%d" % i, [128, 512]) for i in range(1)]
                p3 = [pst(ph, "p8_p3%d" % i, [128, 512]) for i in range(1)]
                py = [pst(ph, "p8_py%d" % i, [128, 1024]) for i in range(2)]

                def load(b):
                    i = b % 2
                    for (Wt, src) in ((W1[i], w1v), (W3[i], w3v), (W2[i], w2v)):
                        kb.dma(QP, lambda: nc.gpsimd.indirect_dma_start(out=Wt[:].rearrange("p a c -> p (a c)"), out_offset=None, in_=src,
                                                                         in_offset=bass.IndirectOffsetOnAxis(ap=widx[:, b:b + 1], axis=0)), R=rs(widx), W=rs(Wt))
                    dma(QS, xt[i][:], xs[b * BLK:(b + 1) * BLK, :].rearrange("(a p) d -> p a d", p=128), [DR("xs")], [xt[i]])

                load(0)
                for b in range(NB):
                    if b + 1 < NB:
                        load(b + 1)
                    i = b % 2
                    x_, xT_, h_ = xt[i], xsT[i], hT[i]
                    for a in range(4):
                        ps = ptr[a % 2]
                        for j in range(8):
                            tr(ps[:, j * 128:(j + 1) * 128], x_[:, a, :].rearrange("p (q j) -> p j q", j=8)[:, j, :], ident_b[:], [x_, ident_b], [ps], sig=(j == 7))
                        vop(DVE, lambda: nc.vector.tensor_copy(out=xT_[:, :, a * 128:(a + 1) * 128], in_=ps[:].rearrange("p (j t) -> p j t", j=8)), [ps], [xT_])
                    for jp in range(4):
                        w1s = W1[i][:].rearrange("p j (q f) -> p j f q", f=4)
                        w3s = W3[i][:].rearrange("p j (q f) -> p j f q", f=4)
                        for j in range(8):
                            mm(p1[0][:], w1s[:, j, jp, :], xT_[:, j, :], j == 0, j == 7, [W1[i], xT_], [p1[0]])
                        for j in range(8):
                            mm(p3[0][:], w3s[:, j, jp, :], xT_[:, j, :], j == 0, j == 7, [W3[i], xT_], [p3[0]])
                        s_ = sl[jp % 2]
                        vop(ACT, lambda: nc.scalar.activation(out=s_[:], in_=p1[0][:], func=AF.Silu), [p1[0]], [s_])
                        vop(DVE, lambda: nc.vector.tensor_tensor(out=h_[:, jp, :], in0=p3[0][:], in1=s_[:], op=ALU.mult), [p3[0], s_], [h_])
                    for a in range(4):
                        p_ = py[a % 2]
                        for hf in range(2):
                            for jp in range(4):
                                mm(p_[:, hf * 512:(hf + 1) * 512], h_[:, jp, a * 128:(a + 1) * 128], W2[i][:, jp, hf * 512:(hf + 1) * 512], jp == 0, jp == 3, [h_, W2[i]], [p_])
                        y_ = yo[a % 2]
                        vop(ACT, lambda: nc.scalar.copy(out=y_[:], in_=p_[:]), [p_], [y_])
                        r0 = b * BLK + a * 128
                        dma(QS, ys[r0:r0 + 128, :], y_[:], [y_], [DR("ys", b, a)])
                kb.barrier()

        def phase_ln2(l, dst, make_xT):
            with ExitStack() as ph:
                gb = sb(ph, "p9_g", [128, D], F32)
                bb = sb(ph, "p9_b", [128, D], F32)
                dma(QS, gb[:], ln2_g[l:l + 1, :].partition_broadcast(128), [], [gb])
                dma(QS, bb[:], ln2_b[l:l + 1, :].partition_broadcast(128), [], [bb])
                x1t = [sb(ph, "p9_x%d" % i, [128, D], F32) for i in range(2)]
                y0 = [sb(ph, "p9_y0%d" % i, [128, D], F32) for i in range(2)]
                y1 = [sb(ph, "p9_y1%d" % i, [128, D], F32) for i in range(2)]
                z = [sb(ph, "p9_z%d" % i, [128, D], F32) for i in range(3)]
                zb = [sb(ph, "p9_zb%d" % i, [128, D], BF16) for i in range(2)]
                junk = sb(ph, "p9_junk", [128, D], F32)
                st4 = [sb(ph, "p9_st%d" % i, [128, 8], F32) for i in range(2)]
                xTs = [sb(ph, "p9_t%d" % i, [128, 8, 512], BF16) for i in range(2)]
                pss = [pst(ph, "p9_ps%d" % i, [128, 1024], BF16) for i in range(2)]

                def load(i):
                    dma(QS, x1t[i % 2][:], x1d[i * 128:(i + 1) * 128, :], [], [x1t[i % 2]])
                    for k, yy in enumerate((y0, y1)):
                        kb.dma(QP, lambda: nc.gpsimd.indirect_dma_start(out=yy[i % 2][:], out_offset=None, in_=ys,
                                                                         in_offset=bass.IndirectOffsetOnAxis(ap=dest_i[:, i, k:k + 1], axis=0)), R=rs(dest_i), W=rs(yy[i % 2]))

                load(0)
                for i in range(NT):
                    if i + 1 < NT:
                        load(i + 1)
                    c, a = divmod(i, 4)
                    z_ = z[i % 3]
                    vop(DVE, lambda: nc.vector.tensor_scalar_mul(out=z_[:], in0=y0[i % 2][:], scalar1=wts[:, i, 0:1]), [y0[i % 2], wts], [z_])
                    vop(DVE, lambda: nc.vector.scalar_tensor_tensor(out=z_[:], in0=y1[i % 2][:], scalar=wts[:, i, 1:2], in1=z_[:], op0=ALU.mult, op1=ALU.add), [y1[i % 2], wts, z_], [z_])
                    vop(DVE, lambda: nc.vector.scalar_tensor_tensor(out=z_[:], in0=x1t[i % 2][:], scalar=DN_ALPHA, in1=z_[:], op0=ALU.mult, op1=ALU.add), [x1t[i % 2], z_], [z_])
                    layer_norm(z_, gb, bb, st4[i % 2], junk)
                    dma(QS, dst[i * 128:(i + 1) * 128, :], z_[:], [z_], [DR("x", c)])
                    if make_xT:
                        zb_ = zb[i % 2]
                        vop(ACT, lambda: nc.scalar.copy(out=zb_[:], in_=z_[:]), [z_], [zb_])
                        emit_xT_tile(pss[i % 2], zb_, a, xTs[c % 2])
                        if a == 3:
                            dma(QS, xT[:, :, c * 512:(c + 1) * 512].rearrange("j p t -> p j t"), xTs[c % 2][:], [xTs[c % 2]], [DR("xT", c)])
                kb.barrier()

        phase_xprep(x_in)
        for l in range(depth):
            xsrc = x_in if l == 0 else xa
            last = (l == depth - 1)
            if stop_after == "xprep":
                break
            phase_proj(l)
            if stop_after == "proj":
                break
            phase_full_attn(l)
            phase_dilated(l)
            if stop_after == "attn":
                break
            phase_out_ln1(l, xsrc)
            if stop_after == "ln1":
                break
            phase_route(l)
            phase_experts(l)
            phase_ln2(l, y_out if last else xa, not last)
        kb.barrier()
        stats = dict(pe=kb.pe.n, act=kb.act.n, dve=kb.dve.n, pool=kb.pool.n, qs=kb.qs.k, qp=kb.qp.k, qs_max=max(kb.qs.cnt), qp_max=max(kb.qp.cnt))
    build.stats = stats
    return nc


WNAMES = ["w_in", "diff_lambda", "diff_subln", "mla_q_norm", "mla_w_uq", "mla_kv_norm", "mla_w_ukv", "w_out", "ln1_g", "ln1_b",
          "moe_w_coarse", "moe_w_fine", "moe_w1", "moe_w3", "moe_w2", "ln2_g", "ln2_b"]


def kernel(x_prompt, x_sample, **w):
    x_prompt = np.asarray(x_prompt, dtype=np.float32)
    x_sample = np.asarray(x_sample, dtype=np.float32)
    n = 8
    seqs = [2048, 2048, 4096]
    nc = build(seqs, DEPTH)
    T = sum(seqs)
    NB = -(-(2 * T + NEXP * (BLK - 1)) // BLK)
    consts = host_consts(max(seqs), NB)
    wd = {k: np.ascontiguousarray(np.asarray(w[k], dtype=np.float32)) for k in WNAMES}
    in_maps = []
    for c in range(n):
        xc = np.concatenate([x_prompt[2 * c], x_prompt[2 * c + 1], x_sample[c]], axis=0)
        m = {"x": np.ascontiguousarray(xc)}
        m.update(wd)
        m.update(consts)
        in_maps.append(m)
    res = run_bass_kernel_spmd(nc, in_maps, core_ids=list(range(n)))
    yp = np.empty_like(x_prompt)
    ysm = np.empty_like(x_sample)
    for c in range(n):
        y = res.results[c]["y"]
        yp[2 * c] = y[0:2048]
        yp[2 * c + 1] = y[2048:4096]
        ysm[c] = y[4096:8192]
    return (yp, ysm)
```

```python
import math
import os
from contextlib import ExitStack

import numpy as np
import concourse.bass as bass
import concourse.mybir as mybir
from concourse.bass_utils import run_bass_kernel_spmd

F32 = mybir.dt.float32
BF16 = mybir.dt.bfloat16
I32 = mybir.dt.int32
ALU = mybir.AluOpType
AF = mybir.ActivationFunctionType
AX = mybir.AxisListType

D = 1024
DEPTH = 4
IN_COLS = 2336
COLB = 768
COLKV = 1024
COLKR = 1152
COLC = 1184
NEXP = 32
DE = 512
LN_EPS = 1e-5
RMS_EPS = 1e-6
DN_ALPHA = (2 * DEPTH) ** 0.25
BLK = 512
C_R = (1, 4, 16)


class Res:
    __slots__ = ("w", "r", "ep")

    def __init__(self):
        self.w = None
        self.r = {}
        self.ep = -1


class Eng:
    def __init__(self, e, sem, is_pe=False):
        self.e = e
        self.sem = sem
        self.n = 0
        self.seen = {}
        self.is_pe = is_pe


class DQ:
    def __init__(self, eng, sems):
        self.eng = eng
        self.sems = sems
        self.cnt = [0] * len(sems)
        self.k = 0


class KB:
    def __init__(self, nc, es, nq=22):
        self.nc = nc
        self.ep = 0
        mk = lambda n: es.enter_context(nc.semaphore(n))
        self.pe = Eng(nc.tensor, mk("s_pe"), True)
        self.act = Eng(nc.scalar, mk("s_act"))
        self.dve = Eng(nc.vector, mk("s_dve"))
        self.pool = Eng(nc.gpsimd, mk("s_pool"))
        self.sp = Eng(nc.sync, None)
        self.engs = [self.pe, self.act, self.dve, self.pool]
        self.qs = DQ(self.sp, [mk("q_s%d" % i) for i in range(16)])
        self.qp = DQ(self.pool, [mk("q_p%d" % i) for i in range(6)])
        self.queues = [self.qs, self.qp]

    def _fresh(self, b):
        if b.ep != self.ep:
            b.w = None
            b.r = {}
            b.ep = self.ep

    def wait(self, eng, sem, val):
        if val > 0 and eng.seen.get(id(sem), 0) < val:
            eng.e.wait_ge(sem, val)
            eng.seen[id(sem)] = val

    def deps(self, eng, R, W, is_dma):
        for b in R:
            self._fresh(b)
            if b.w is not None:
                sem, val = b.w
                if sem is eng.sem and not is_dma and eng.is_pe:
                    continue
                self.wait(eng, sem, val)
        for b in W:
            self._fresh(b)
            if b.w is not None:
                sem, val = b.w
                if not (sem is eng.sem and not is_dma):
                    self.wait(eng, sem, val)
            for sem, val in b.r.values():
                if sem is eng.sem and not is_dma:
                    continue
                self.wait(eng, sem, val)

    def _mark(self, tok, R, W):
        sem, val = tok
        for b in R:
            old = b.r.get(id(sem))
            if old is None or old[1] < val:
                b.r[id(sem)] = (sem, val)
        for b in W:
            b.w = tok
            b.r = {}

    def op(self, eng, fn, R=(), W=(), sig=True):
        self.deps(eng, R, W, False)
        ins = fn()
        if sig:
            eng.n += 1
            ins.then_inc(eng.sem, 1)
            tok = (eng.sem, eng.n)
        else:
            tok = (eng.sem, eng.n + 1)
        self._mark(tok, R, W)
        return ins

    def dma(self, q, fn, R=(), W=()):
        eng = q.eng
        self.deps(eng, R, W, True)
        i = q.k % len(q.sems)
        q.k += 1
        sem = q.sems[i]
        self.wait(eng, sem, q.cnt[i])
        ins = fn()
        ins.then_inc(sem, 16)
        q.cnt[i] += 16
        self._mark((sem, q.cnt[i]), R, W)
        return ins

    def barrier(self):
        for E in self.engs + [self.sp]:
            for X in self.engs:
                if X is not E:
                    self.wait(E, X.sem, X.n)
            for q in self.queues:
                for i, sem in enumerate(q.sems):
                    self.wait(E, sem, q.cnt[i])
        self.ep += 1


class Tl:
    def __init__(self, t):
        self.t = t
        self.res = Res()

    def __getitem__(self, k):
        return self.t[k]


def host_consts(smax, nb):
    def rope(dim):
        inv = (1.0 / (np.float32(10000.0) ** (np.arange(0, dim, 2, dtype=np.float32) / np.float32(dim)))).astype(np.float32)
        ang = (np.arange(smax, dtype=np.float32)[:, None] * inv[None, :]).astype(np.float32)
        return np.cos(ang).astype(np.float32), np.sin(ang).astype(np.float32)

    ca, sa = rope(32)
    cc, sc = rope(64)
    p = np.arange(128)
    cosA = ca[:, p % 16].T.copy()
    sinA = sa[:, p % 16].T.copy()
    cosC = cc[:, p % 32].T.copy()
    sinC = sc[:, p % 32].T.copy()
    cosB = np.ones((128, smax), np.float32)
    sinB = np.zeros((128, smax), np.float32)
    cosB[64:96] = cosA[0:32]
    sinB[64:96] = sinA[0:32]
    k = np.arange(128)[:, None]
    q = np.arange(128)[None, :]
    band = np.stack([((k - 128 - q) >= -64), (np.abs(k - q) <= 64), ((k + 128 - q) <= 64)], axis=1).astype(np.float32)
    misc = np.zeros((128, 512), np.float32)
    misc[:, 0] = np.arange(128)
    misc[:, 1:1 + nb] = (np.arange(nb) * BLK)[None, :]
    tri = (np.arange(128)[:, None] < np.arange(128)[None, :]).astype(np.float32)
    return {
        "c_ident": np.eye(128, dtype=np.float32), "c_cosA": cosA, "c_sinA": sinA, "c_cosC": cosC, "c_sinC": sinC,
        "c_cosB": cosB, "c_sinB": sinB, "c_band": band.reshape(128, 384).copy(), "c_misc": misc, "c_tri": tri,
    }


def build(seqs, depth, wdepth=DEPTH, stop_after=None, dbg=()):
    T = sum(seqs)
    NT = T // 128
    NCH = T // 512
    NB = -(-(2 * T + NEXP * (BLK - 1)) // BLK)
    NSLOT = NB * BLK
    SMAX = max(seqs)
    seq_of_chunk = []
    t0 = 0
    seq_starts = []
    for S in seqs:
        seq_starts.append(t0)
        for c in range(S // 512):
            seq_of_chunk.append((t0, S, c * 512))
        t0 += S

    nc = bass.Bass("TRN2", target_bir_lowering=False)
    dt_in = lambda n, s, d=F32: nc.dram_tensor(n, list(s), d, kind="ExternalInput").ap()
    dt_sc = lambda n, s, d: nc.dram_tensor(n, list(s), d, kind=("ExternalOutput" if n in dbg else "Internal")).ap()
    x_in = dt_in("x", [T, D])
    w_in = dt_in("w_in", [wdepth, D, IN_COLS])
    diff_lambda = dt_in("diff_lambda", [wdepth, 4, 32])
    diff_subln = dt_in("diff_subln", [wdepth, 64])
    mla_q_norm = dt_in("mla_q_norm", [wdepth, 256])
    mla_w_uq = dt_in("mla_w_uq", [wdepth, 256, 576])
    mla_kv_norm = dt_in("mla_kv_norm", [wdepth, 128])
    mla_w_ukv = dt_in("mla_w_ukv", [wdepth, 128, 768])
    w_out = dt_in("w_out", [wdepth, D, D])
    ln1_g = dt_in("ln1_g", [wdepth, D])
    ln1_b = dt_in("ln1_b", [wdepth, D])
    w_coarse = dt_in("moe_w_coarse", [wdepth, D, 4])
    w_fine = dt_in("moe_w_fine", [wdepth, 4, D, 8])
    w1 = dt_in("moe_w1", [wdepth, NEXP, D, DE])
    w3 = dt_in("moe_w3", [wdepth, NEXP, D, DE])
    w2 = dt_in("moe_w2", [wdepth, NEXP, DE, D])
    ln2_g = dt_in("ln2_g", [wdepth, D])
    ln2_b = dt_in("ln2_b", [wdepth, D])
    c_ident = dt_in("c_ident", [128, 128])
    c_cos = {"A": dt_in("c_cosA", [128, SMAX]), "C": dt_in("c_cosC", [128, SMAX]), "B": dt_in("c_cosB", [128, SMAX])}
    c_sin = {"A": dt_in("c_sinA", [128, SMAX]), "C": dt_in("c_sinC", [128, SMAX]), "B": dt_in("c_sinB", [128, SMAX])}
    c_band = dt_in("c_band", [128, 384])
    c_misc = dt_in("c_misc", [128, 512])
    c_tri = dt_in("c_tri", [128, 128])
    y_out = nc.dram_tensor("y", [T, D], F32, kind="ExternalOutput").ap()

    xa = dt_sc("s_xa", [T, D], F32)
    x1d = dt_sc("s_x1", [T, D], F32)
    x1b = dt_sc("s_x1b", [T, D], BF16)
    xT = dt_sc("s_xT", [8, 128, T], BF16)
    mixT = dt_sc("s_mixT", [8, 128, T], BF16)
    QKA = dt_sc("s_qka", [4, 128, T], BF16)
    QKC = dt_sc("s_qkc", [6, 128, T], BF16)
    QB = dt_sc("s_qb", [6, 96, T], BF16)
    KBd = dt_sc("s_kb", [6, 96, T], BF16)
    VA = dt_sc("s_va", [T, 256], BF16)
    VB = dt_sc("s_vb", [T, 384], BF16)
    VC = dt_sc("s_vc", [T, 384], BF16)
    NCd = dt_sc("s_nc", [6, 64, T], BF16)
    DCd = dt_sc("s_dc", [6, 64, T], F32)
    xs = dt_sc("s_xs", [NSLOT, D], BF16)
    ys = dt_sc("s_ys", [NSLOT, D], F32)

    es = ExitStack()
    with es:
        kb = KB(nc, es)
        PE, ACT, DVE, POOL = kb.pe, kb.act, kb.dve, kb.pool
        QS, QP = kb.qs, kb.qp
        es.enter_context(nc.Block())
        dres = {}

        def DR(*key):
            r = dres.get(key)
            if r is None:
                r = dres[key] = Res()
            return r

        uid = [0]

        def sb(stack, name, shape, dtype):
            uid[0] += 1
            return Tl(stack.enter_context(nc.sbuf_tensor("%s_%d" % (name, uid[0]), list(shape), dtype)))

        def pst(stack, name, shape, dtype=F32):
            uid[0] += 1
            return Tl(stack.enter_context(nc.psum_tensor("%s_%d" % (name, uid[0]), list(shape), dtype)))

        def rs(*ts):
            return [t.res if isinstance(t, Tl) else t for t in ts]

        def mm(out, lhsT, rhs, start, stop, R, W, sig=None):
            kb.op(PE, lambda: nc.tensor.matmul(out, lhsT=lhsT, rhs=rhs, start=start, stop=stop), R=rs(*R), W=rs(*W),
                  sig=(stop if sig is None else sig))

        def tr(out, in_, ident, R, W, sig=True):
            kb.op(PE, lambda: nc.tensor.transpose(out, in_, ident), R=rs(*R), W=rs(*W), sig=sig)

        def vop(eng, fn, R, W):
            kb.op(eng, fn, R=rs(*R), W=rs(*W))

        def dma(q, out, in_, R, W):
            kb.dma(q, lambda: q.eng.e.dma_start(out=out, in_=in_), R=rs(*R), W=rs(*W))

        ident_f = sb(es, "ident_f", [128, 128], F32)
        ident_b = sb(es, "ident_b", [128, 128], BF16)
        ones_f = sb(es, "ones_f", [128, 128], F32)
        ones_b = sb(es, "ones_b", [128, 128], BF16)
        tri_f = sb(es, "tri_f", [128, 128], F32)
        misc = sb(es, "misc", [128, 512], F32)
        epsr = sb(es, "epsr", [128, 2], F32)
        dma(QS, ident_f[:], c_ident, [], [ident_f])
        dma(QP, ident_b[:], c_ident, [], [ident_b])
        dma(QS, tri_f[:], c_tri, [], [tri_f])
        dma(QS, misc[:], c_misc, [], [misc])
        vop(DVE, lambda: nc.vector.memset(ones_f[:], 1.0), [], [ones_f])
        vop(DVE, lambda: nc.vector.memset(ones_b[:], 1.0), [], [ones_b])
        vop(DVE, lambda: nc.vector.memset(epsr[:, 0:1], LN_EPS), [], [epsr])
        vop(DVE, lambda: nc.vector.memset(epsr[:, 1:2], RMS_EPS), [], [epsr])
        E0 = sb(es, "E0", [128, NT, 32], F32)
        E1 = sb(es, "E1", [128, NT, 32], F32)
        wts = sb(es, "wts", [128, NT, 2], F32)
        dest_i = sb(es, "dest_i", [128, NT, 2], I32)
        widx = sb(es, "widx", [128, NB], I32)
        kb.barrier()

        def emit_xT_tile(ps, src_bf, a, xTs):
            for j in range(8):
                tr(ps[:, j * 128:(j + 1) * 128], src_bf[:, j * 128:(j + 1) * 128], ident_b[:], [src_bf, ident_b], [ps], sig=(j == 7))
            vop(ACT, lambda: nc.scalar.copy(out=xTs[:, :, a * 128:(a + 1) * 128], in_=ps[:].rearrange("p (j t) -> p j t", j=8)),
                [ps], [xTs])

        def rstd_from(sums_sq_ap, out_tl, n, eps_col, tmp_tl, R):
            vop(ACT, lambda: nc.scalar.activation(out=tmp_tl, in_=sums_sq_ap, func=AF.Sqrt, bias=epsr[0:tmp_tl.shape[0], eps_col:eps_col + 1], scale=1.0 / n),
                R + [epsr], [])
            return None

        def phase_xprep(src):
            with ExitStack() as ph:
                xin = [sb(ph, "p0_x%d" % i, [128, 4, D], F32) for i in range(2)]
                xbf = [sb(ph, "p0_b%d" % i, [128, D], BF16) for i in range(2)]
                xTs = [sb(ph, "p0_t%d" % i, [128, 8, 512], BF16) for i in range(2)]
                pss = [pst(ph, "p0_ps%d" % i, [128, 1024], BF16) for i in range(2)]
                for c in range(NCH):
                    xi = xin[c % 2]
                    dma(QS, xi[:], src[c * 512:(c + 1) * 512, :].rearrange("(a p) d -> p a d", p=128), [DR("x", c)], [xi])
                    xt = xTs[c % 2]
                    for a in range(4):
                        xb = xbf[a % 2]
                        vop(POOL, lambda: nc.gpsimd.tensor_copy(out=xb[:], in_=xi[:, a, :]), [xi], [xb])
                        emit_xT_tile(pss[a % 2], xb, a, xt)
                    dma(QS, xT[:, :, c * 512:(c + 1) * 512].rearrange("j p t -> p j t"), xt[:], [xt], [DR("xT", c)])
                kb.barrier()

        def phase_proj(l):
            with ExitStack() as ph:
                win = sb(ph, "p1_win", [128, 8, IN_COLS], BF16)
                wrot = sb(ph, "p1_wrot", [128, 8, 1312], BF16)
                wuq = sb(ph, "p1_wuq", [128, 2, 576], BF16)
                wuqr = sb(ph, "p1_wuqr", [128, 2, 576], BF16)
                wukv = sb(ph, "p1_wukv", [128, 768], BF16)
                wukv_v = sb(ph, "p1_wukvv", [128, 384], BF16)
                stg = sb(ph, "p1_stg", [128, 2, 768], F32)
                gq = sb(ph, "p1_gq", [128, 3], F32)
                for j in range(8):
                    dma(QP, win[:, j, :], w_in[l, j * 128:(j + 1) * 128, :], [], [win])
                def rot(dst_lo, src_lo, ncols, half):
                    dv = wrot[:, :, dst_lo:dst_lo + ncols].rearrange("p j (b two h) -> p j b two h", two=2, h=half)
                    sv = win[:, :, src_lo:src_lo + ncols].rearrange("p j (b two h) -> p j b two h", two=2, h=half)
                    vop(DVE, lambda: nc.vector.tensor_scalar_mul(out=dv[:, :, :, 0, :], in0=sv[:, :, :, 1, :], scalar1=-1.0), [win], [wrot])
                    vop(DVE, lambda: nc.vector.tensor_copy(out=dv[:, :, :, 1, :], in_=sv[:, :, :, 0, :]), [win], [wrot])
                rot(0, 0, 512, 16)
                rot(512, COLKR, 32, 16)
                rot(544, COLC, 768, 32)
                for j in range(2):
                    dma(QS, gq[:, j:j + 1], mla_q_norm[l, j * 128:(j + 1) * 128].rearrange("(p o) -> p o", o=1), [], [gq])
                dma(QS, gq[:, 2:3], mla_kv_norm[l].rearrange("(p o) -> p o", o=1), [], [gq])
                dma(QS, stg[:, :, 0:576], mla_w_uq[l].rearrange("(j p) c -> p j c", p=128), [], [stg])
                for j in range(2):
                    vop(DVE, lambda: nc.vector.tensor_scalar_mul(out=wuq[:, j, :], in0=stg[:, j, 0:576], scalar1=gq[:, j:j + 1]), [stg, gq], [wuq])
                vop(DVE, lambda: nc.vector.memset(wuqr[:], 0.0), [], [wuqr])
                dv = wuqr[:].rearrange("p j (h c) -> p j h c", c=96)[:, :, :, 64:96].rearrange("p j h (two f) -> p j h two f", two=2)
                sv = wuq[:].rearrange("p j (h c) -> p j h c", c=96)[:, :, :, 64:96].rearrange("p j h (two f) -> p j h two f", two=2)
                vop(DVE, lambda: nc.vector.tensor_scalar_mul(out=dv[:, :, :, 0, :], in0=sv[:, :, :, 1, :], scalar1=-1.0), [wuq], [wuqr])
                vop(DVE, lambda: nc.vector.tensor_copy(out=dv[:, :, :, 1, :], in_=sv[:, :, :, 0, :]), [wuq], [wuqr])
                stg2 = sb(ph, "p1_stg2", [128, 768], F32)
                dma(QS, stg2[:], mla_w_ukv[l], [], [stg2])
                vop(DVE, lambda: nc.vector.tensor_scalar_mul(out=wukv[:], in0=stg2[:], scalar1=gq[:, 2:3]), [stg2, gq], [wukv])
                vop(DVE, lambda: nc.vector.tensor_copy(out=wukv_v[:].rearrange("p (h c) -> p h c", c=64),
                                                       in_=wukv[:].rearrange("p (h c) -> p h c", c=128)[:, :, 64:128]), [wukv], [wukv_v])

                xTc = [sb(ph, "p1_x%d" % i, [128, 8, 512], BF16) for i in range(2)]
                tabs = [{k: (sb(ph, "p1_c%s%d" % (k, i), [128, 512], F32), sb(ph, "p1_s%s%d" % (k, i), [128, 512], F32)) for k in "ACB"} for i in range(2)]
                pq = [pst(ph, "p1_pq%d" % i, [128, 512]) for i in range(2)]
                pr = [pst(ph, "p1_pr%d" % i, [128, 512]) for i in range(2)]
                pv = [pst(ph, "p1_pv%d" % i, [128, 512]) for i in range(2)]
                pm = [pst(ph, "p1_pm%d" % i, [128, 512]) for i in range(2)]
                t1 = [sb(ph, "p1_t1%d" % i, [128, 512], F32) for i in range(2)]
                t2 = [sb(ph, "p1_t2%d" % i, [128, 512], F32) for i in range(2)]
                t3 = [sb(ph, "p1_t3%d" % i, [128, 512], F32) for i in range(2)]
                ob = [sb(ph, "p1_ob%d" % i, [128, 512], BF16) for i in range(4)]
                vo = [sb(ph, "p1_vo%d" % i, [128, 640], BF16) for i in range(2)]
                vbo = [sb(ph, "p1_vbo%d" % i, [128, 384], BF16) for i in range(2)]
                cqT = sb(ph, "p1_cqT", [128, 3, 512], BF16)
                sq = sb(ph, "p1_sq", [128, 3, 512], F32)
                rq = sb(ph, "p1_rq", [128, 512], F32)
                rkv = sb(ph, "p1_rkv", [128, 512], F32)
                kr = sb(ph, "p1_kr", [128, 512], BF16)
                rtok = sb(ph, "p1_rtok", [128, 4], F32)
                cnt = {"g": 0, "o": 0}

                def proj_fm(c, xc, lo, m, rlo):
                    i = cnt["g"] % 2
                    cnt["g"] += 1
                    for j in range(8):
                        mm(pq[i][0:m, :], win[:, j, lo:lo + m], xc[:, j, :], j == 0, j == 7, [win, xc], [pq[i]])
                    if rlo is not None:
                        for j in range(8):
                            mm(pr[i][0:m, :], wrot[:, j, rlo:rlo + m], xc[:, j, :], j == 0, j == 7, [wrot, xc], [pr[i]])
                    return i

                def rope_evac(i, m, tab, out_ap, W, extra=None):
                    ct, st = tab
                    vop(DVE, lambda: nc.vector.tensor_tensor(out=t1[i][0:m, :], in0=pq[i][0:m, :], in1=ct[0:m, :], op=ALU.mult), [pq[i], ct], [t1[i]])
                    vop(DVE, lambda: nc.vector.tensor_tensor(out=t2[i][0:m, :], in0=pr[i][0:m, :], in1=st[0:m, :], op=ALU.mult), [pr[i], st], [t2[i]])
                    if extra is None:
                        vop(POOL, lambda: nc.gpsimd.tensor_tensor(out=out_ap, in0=t1[i][0:m, :], in1=t2[i][0:m, :], op=ALU.add), [t1[i], t2[i]], W)
                    else:
                        vop(POOL, lambda: nc.gpsimd.tensor_tensor(out=t3[i][0:m, :], in0=t1[i][0:m, :], in1=t2[i][0:m, :], op=ALU.add), [t1[i], t2[i]], [t3[i]])
                        vop(POOL, lambda: nc.gpsimd.tensor_tensor(out=out_ap, in0=t3[i][0:m, :], in1=extra[0:m, :], op=ALU.mult), [t3[i], extra], W)

                def next_ob():
                    o = ob[cnt["o"] % 4]
                    cnt["o"] += 1
                    return o

                def load_chunk(c):
                    tk = c * 512
                    s0, S, pos0 = seq_of_chunk[c]
                    xc = xTc[c % 2]
                    dma(QS, xc[:], xT[:, :, tk:tk + 512].rearrange("j p t -> p j t"), [DR("xT", c)], [xc])
                    tb = tabs[c % 2]
                    for k in "ACB":
                        dma(QS, tb[k][0][:], c_cos[k][:, pos0:pos0 + 512], [], [tb[k][0]])
                        dma(QS, tb[k][1][:], c_sin[k][:, pos0:pos0 + 512], [], [tb[k][1]])

                PSTOP = os.environ.get("PSTOP")
                if PSTOP == "w":
                    kb.barrier()
                    return
                load_chunk(0)
                for c in range(NCH):
                    if c + 1 < NCH:
                        load_chunk(c + 1)
                    tk = c * 512
                    s0, S, pos0 = seq_of_chunk[c]
                    xc = xTc[c % 2]
                    tb = tabs[c % 2]
                    for ga in range(4):
                        i = proj_fm(c, xc, ga * 128, 128, ga * 128)
                        o = next_ob()
                        rope_evac(i, 128, tb["A"], o[:], [o])
                        dma(QS, QKA[ga, :, tk:tk + 512], o[:], [o], [DR("qka", ga, c)])
                    if PSTOP == "A":
                        kb.barrier()
                        return
                    for gc in range(6):
                        r = C_R[gc % 3]
                        L = S // r
                        i = proj_fm(c, xc, COLC + gc * 128, 128, 544 + gc * 128)
                        o = next_ob()
                        ct, st = tb["C"]
                        vop(DVE, lambda: nc.vector.tensor_tensor(out=t1[i][:], in0=pq[i][:], in1=ct[:], op=ALU.mult), [pq[i], ct], [t1[i]])
                        vop(DVE, lambda: nc.vector.tensor_tensor(out=t2[i][:], in0=pr[i][:], in1=st[:], op=ALU.mult), [pr[i], st], [t2[i]])
                        vop(POOL, lambda: nc.gpsimd.tensor_tensor(out=o[:], in0=t1[i][:], in1=t2[i][:], op=ALU.add), [t1[i], t2[i]], [o])
                        dma(QS, QKC[gc, :, tk:tk + 512], o[:], [o], [DR("qkc", gc, c)])
                    if PSTOP == "C":
                        kb.barrier()
                        return
                    for g in range(3):
                        i = proj_fm(c, xc, COLB + g * 128, 128, None)
                        vop(ACT, lambda: nc.scalar.copy(out=cqT[:, g, :], in_=pq[i][:]), [pq[i]], [cqT])
                        vop(ACT, lambda: nc.scalar.activation(out=sq[:, g, :], in_=pq[i][:], func=AF.Square), [pq[i]], [sq])
                    i = proj_fm(c, xc, COLKR, 32, 512)
                    rope_evac(i, 32, tb["A"], kr[0:32, :], [kr])
                    for j in range(2):
                        mm(pm[0][:], ones_f[:], sq[:, j, :], j == 0, j == 1, [ones_f, sq], [pm[0]])
                    vop(ACT, lambda: nc.scalar.activation(out=t3[0][:], in_=pm[0][:], func=AF.Sqrt, bias=epsr[:, 1:2], scale=1.0 / 256), [pm[0], epsr], [t3[0]])
                    vop(DVE, lambda: nc.vector.reciprocal(out=rq[:], in_=t3[0][:]), [t3[0]], [rq])
                    mm(pm[1][:], ones_f[:], sq[:, 2, :], True, True, [ones_f, sq], [pm[1]])
                    vop(ACT, lambda: nc.scalar.activation(out=t3[1][:], in_=pm[1][:], func=AF.Sqrt, bias=epsr[:, 1:2], scale=1.0 / 128), [pm[1], epsr], [t3[1]])
                    vop(DVE, lambda: nc.vector.reciprocal(out=rkv[:], in_=t3[1][:]), [t3[1]], [rkv])
                    if PSTOP == "Bs":
                        kb.barrier()
                        return
                    for h in range(6):
                        i = cnt["g"] % 2
                        cnt["g"] += 1
                        for j in range(2):
                            mm(pq[i][0:96, :], wuq[:, j, h * 96:(h + 1) * 96], cqT[:, j, :], j == 0, j == 1, [wuq, cqT], [pq[i]])
                        for j in range(2):
                            mm(pr[i][0:96, :], wuqr[:, j, h * 96:(h + 1) * 96], cqT[:, j, :], j == 0, j == 1, [wuqr, cqT], [pr[i]])
                        o = next_ob()
                        rope_evac(i, 96, tb["B"], o[0:96, :], [o], extra=rq)
                        dma(QS, QB[h, :, tk:tk + 512], o[0:96, :], [o], [DR("qb", h, c)])
                        i = cnt["g"] % 2
                        cnt["g"] += 1
                        mm(pq[i][0:64, :], wukv[:, h * 128:h * 128 + 64], cqT[:, 2, :], True, True, [wukv, cqT], [pq[i]])
                        o = next_ob()
                        vop(DVE, lambda: nc.vector.tensor_tensor(out=o[0:64, :], in0=pq[i][0:64, :], in1=rkv[0:64, :], op=ALU.mult), [pq[i], rkv], [o])
                        vop(ACT, lambda: nc.scalar.copy(out=o[64:96, :], in_=kr[0:32, :]), [kr], [o])
                        dma(QS, KBd[h, :, tk:tk + 512], o[0:96, :], [o], [DR("kb", h, c)])
                    if PSTOP == "Bh":
                        kb.barrier()
                        return
                    for a in range(4):
                        i = a % 2
                        tsl = slice(a * 128, (a + 1) * 128)
                        for j in range(8):
                            mm(pv[i][:, 0:256], xc[:, j, tsl], win[:, j, 512:768], j == 0, j == 7, [xc, win], [pv[i]])
                        for j in range(8):
                            mm(pm[i][:, 0:384], xc[:, j, tsl], win[:, j, COLC + 768:COLC + 1152], j == 0, j == 7, [xc, win], [pm[i]])
                        vop(ACT, lambda: nc.scalar.copy(out=vo[i][:, 0:256], in_=pv[i][:, 0:256]), [pv[i]], [vo[i]])
                        vop(ACT, lambda: nc.scalar.copy(out=vo[i][:, 256:640], in_=pm[i][:, 0:384]), [pm[i]], [vo[i]])
                        dma(QS, VA[tk + a * 128:tk + (a + 1) * 128, :], vo[i][:, 0:256], [vo[i]], [DR("va", c, a)])
                        dma(QS, VC[tk + a * 128:tk + (a + 1) * 128, :], vo[i][:, 256:640], [vo[i]], [DR("vc", c, a)])
                        mm(pv[i][:, 256:272], sq[:, 2, tsl], ones_f[:, 0:16], True, True, [sq, ones_f], [pv[i]])
                        vop(ACT, lambda: nc.scalar.activation(out=rtok[:, 2 * i:2 * i + 1], in_=pv[i][:, 256:257], func=AF.Sqrt, bias=epsr[:, 1:2], scale=1.0 / 128),
                            [pv[i], epsr], [rtok])
                        vop(DVE, lambda: nc.vector.reciprocal(out=rtok[:, 2 * i + 1:2 * i + 2], in_=rtok[:, 2 * i:2 * i + 1]), [rtok], [rtok])
                        mm(pm[i][:, 0:384], cqT[:, 2, tsl], wukv_v[:], True, True, [cqT, wukv_v], [pm[i]])
                        vop(DVE, lambda: nc.vector.tensor_scalar_mul(out=vbo[i][:], in0=pm[i][:, 0:384], scalar1=rtok[:, 2 * i + 1:2 * i + 2]), [pm[i], rtok], [vbo[i]])
                        dma(QS, VB[tk + a * 128:tk + (a + 1) * 128, :], vbo[i][:], [vbo[i]], [DR("vb", c, a)])
                kb.barrier()

        def phase_full_attn(l):
            lam_init = 0.8 - 0.6 * math.exp(-0.3 * l)
            with ExitStack() as ph:
                lv = sb(ph, "p2_lv", [1, 128], F32)
                lw = sb(ph, "p2_lw", [1, 8], F32)
                lp = sb(ph, "p2_lp", [1, 64], F32)
                gs = sb(ph, "p2_gs", [64, 4], F32)
                pl = pst(ph, "p2_pl", [128, 512])
                dma(QS, lv[:], diff_lambda[l:l + 1].rearrange("o a b -> o (a b)"), [], [lv])
                dma(QS, gs[:, 0:1], diff_subln[l].rearrange("(p o) -> p o", o=1), [], [gs])
                lvv = lv[:].rearrange("o (a two b) -> o a two b", two=2, b=32)
                vop(DVE, lambda: nc.vector.tensor_tensor(out=lp[:].rearrange("o (a b) -> o a b", b=32), in0=lvv[:, :, 0, :], in1=lvv[:, :, 1, :], op=ALU.mult), [lv], [lp])
                vop(DVE, lambda: nc.vector.reduce_sum(out=lw[:, 0:2], in_=lp[:].rearrange("o (a b) -> o a b", b=32), axis=AX.X), [lp], [lw])
                vop(ACT, lambda: nc.scalar.activation(out=lw[:, 2:4], in_=lw[:, 0:2], func=AF.Exp), [lw], [lw])
                vop(DVE, lambda: nc.vector.tensor_tensor(out=lw[:, 4:5], in0=lw[:, 3:4], in1=lw[:, 2:3], op=ALU.subtract), [lw], [lw])
                vop(DVE, lambda: nc.vector.tensor_scalar_add(out=lw[:, 5:6], in0=lw[:, 4:5], scalar1=-lam_init), [lw], [lw])
                l16 = sb(ph, "p2_l16", [1, 16], F32)
                vop(DVE, lambda: nc.vector.memset(l16[:], 0.0), [], [l16])
                vop(DVE, lambda: nc.vector.tensor_scalar(out=l16[:], in0=l16[:], scalar1=lw[:, 5:6], scalar2=None, op0=ALU.add), [l16, lw], [l16])
                mm(pl[0:64, 0:16], ones_f[0:1, 0:64], l16[0:1, :], True, True, [ones_f, l16], [pl])
                vop(ACT, lambda: nc.scalar.copy(out=gs[:, 1:2], in_=pl[0:64, 0:1]), [pl], [gs])
                vop(DVE, lambda: nc.vector.tensor_scalar_mul(out=gs[:, 2:3], in0=gs[:, 0:1], scalar1=1.0 - lam_init), [gs], [gs])

                qT = [sb(ph, "p2_q%d" % i, [96, SMAX], BF16) for i in range(4)]
                kT = [sb(ph, "p2_k%d" % i, [96, SMAX], BF16) for i in range(4)]
                vt = [sb(ph, "p2_v%d" % i, [128, SMAX // 128, 128], BF16) for i in range(2)]
                pT = [sb(ph, "p2_p%d" % i, [128, 1024], BF16) for i in range(3)]
                psc = [pst(ph, "p2_s%d" % i, [128, 1024]) for i in range(2)]
                pac = [pst(ph, "p2_a%d" % i, [128, 512]) for i in range(2)]
                prm = pst(ph, "p2_rm", [128, 512])
                dsh = [sb(ph, "p2_d%d" % i, [64, 512], F32) for i in range(2)]
                oc = [sb(ph, "p2_o%d" % i, [64, 512], F32) for i in range(3)]
                tq = sb(ph, "p2_tq", [64, 512], F32)
                obf = [sb(ph, "p2_ob%d" % i, [64, 512], BF16) for i in range(2)]
                for v in vt:
                    vop(DVE, lambda: nc.vector.memset(v[:, :, 64:128], 1.0), [], [v])
                st = {"s": 0, "p": 0, "a": 0, "o": 0, "qk": 0, "v": 0}
                LA = 1
                groups = []

                def norm_post(acc):
                    d = dsh[st["o"] % 2]
                    o = oc[st["o"] % 3]
                    st["o"] += 1
                    vop(ACT, lambda: nc.scalar.copy(out=d[:], in_=acc[64:128, :]), [acc], [d])
                    vop(DVE, lambda: nc.vector.reciprocal(out=d[:], in_=d[:]), [d], [d])
                    vop(DVE, lambda: nc.vector.tensor_tensor(out=o[:], in0=acc[0:64, :], in1=d[:], op=ALU.mult), [acc, d], [o])
                    return o

                def make_its(q_, k_, dk, scale, v, S, qc, post):
                    nkt = S // 128
                    acc = pac[st["a"] % 2]
                    st["a"] += 1
                    return [dict(q=q_, k=k_, dk=dk, scale=scale, v=v, kt=kt, qc=qc, acc=acc, first=(kt == 0), last=(kt == nkt - 2),
                                 post=(post if kt == nkt - 2 else None)) for kt in range(0, nkt, 2)]

                def load_v(src, h, s0, S, v):
                    for k0 in range(0, S // 128, 8):
                        dma(QS, v[:, k0:k0 + 8, 0:64], src[s0 + k0 * 128:s0 + (k0 + 8) * 128, h * 64:(h + 1) * 64].rearrange("(kt p) d -> p kt d", p=128), [DR("vsrc")], [v])

                def load_qk(qd, kd, dk, s0, S, slot):
                    dma(QS, slot[0][0:dk, 0:S], qd[:, s0:s0 + S], [DR("qsrc")], [slot[0]])
                    dma(QS, slot[1][0:dk, 0:S], kd[:, s0:s0 + S], [DR("ksrc")], [slot[1]])

                def take_slots(n):
                    r_ = [(qT[(st["qk"] + c) % 4], kT[(st["qk"] + c) % 4]) for c in range(n)]
                    st["qk"] += n
                    v = vt[st["v"] % 2]
                    st["v"] += 1
                    return r_, v

                sa = 32.0 ** -0.5
                sbq = 96.0 ** -0.5
                for (s0, S) in zip(seq_starts, seqs):
                    for h in range(4):
                        slots, v = take_slots(2)

                        def load(h=h, s0=s0, S=S, slots=slots, v=v):
                            load_v(VA, h, s0, S, v)
                            for cpt in range(2):
                                u = h * 2 + cpt
                                load_qk(QKA[u // 4, (u % 4) * 32:(u % 4) * 32 + 32, :], QKA[2 + u // 4, (u % 4) * 32:(u % 4) * 32 + 32, :], 32, s0, S, slots[cpt])

                        its = []
                        for qc in range(S // 512):
                            hold = {}

                            def post0(acc, hold=hold):
                                hold["o0"] = norm_post(acc)

                            def post1(acc, hold=hold, qc=qc, h=h, s0=s0):
                                o1 = norm_post(acc)
                                o0 = hold["o0"]
                                vop(DVE, lambda: nc.vector.scalar_tensor_tensor(out=o0[:], in0=o1[:], scalar=gs[:, 1:2], in1=o0[:], op0=ALU.mult, op1=ALU.add), [o1, o0, gs], [o0])
                                vop(ACT, lambda: nc.scalar.activation(out=tq[:], in_=o0[:], func=AF.Square), [o0], [tq])
                                mm(prm[0:64, :], ones_f[0:64, 0:64], tq[:], True, True, [ones_f, tq], [prm])
                                vop(ACT, lambda: nc.scalar.activation(out=tq[:], in_=prm[0:64, :], func=AF.Sqrt, bias=epsr[0:64, 1:2], scale=1.0 / 64), [prm, epsr], [tq])
                                vop(DVE, lambda: nc.vector.reciprocal(out=tq[:], in_=tq[:]), [tq], [tq])
                                ob_ = obf[qc % 2]
                                vop(DVE, lambda: nc.vector.scalar_tensor_tensor(out=ob_[:], in0=o0[:], scalar=gs[:, 2:3], in1=tq[:], op0=ALU.mult, op1=ALU.mult), [o0, tq, gs], [ob_])
                                tk = s0 + qc * 512
                                dma(QS, mixT[h // 2, (h % 2) * 64:(h % 2) * 64 + 64, tk:tk + 512], ob_[:], [ob_], [DR("mixT", tk // 512)])

                            its += make_its(slots[0][0], slots[0][1], 32, sa, v, S, qc, post0)
                            its += make_its(slots[1][0], slots[1][1], 32, sa, v, S, qc, post1)
                        groups.append(dict(load=load, its=its))
                    for h in range(6):
                        slots, v = take_slots(1)

                        def load(h=h, s0=s0, S=S, slots=slots, v=v):
                            load_v(VB, h, s0, S, v)
                            load_qk(QB[h], KBd[h], 96, s0, S, slots[0])

                        its = []
                        for qc in range(S // 512):
                            def post(acc, qc=qc, h=h, s0=s0):
                                o0 = norm_post(acc)
                                ob_ = obf[qc % 2]
                                vop(POOL, lambda: nc.gpsimd.tensor_copy(out=ob_[:], in_=o0[:]), [o0], [ob_])
                                tk = s0 + qc * 512
                                f0 = 256 + h * 64
                                dma(QS, mixT[f0 // 128, f0 % 128:f0 % 128 + 64, tk:tk + 512], ob_[:], [ob_], [DR("mixT", tk // 512)])

                            its += make_its(slots[0][0], slots[0][1], 96, sbq, v, S, qc, post)
                        groups.append(dict(load=load, its=its))

                flat = []
                for gi, g in enumerate(groups):
                    for ii, it in enumerate(g["its"]):
                        it["g"] = gi
                        it["ii"] = ii
                        flat.append(it)

                def emit_qk(it):
                    ps = psc[st["s"] % 2]
                    st["s"] += 1
                    it["ps"] = ps
                    dk, kt, qc = it["dk"], it["kt"], it["qc"]
                    for u in range(2):
                        mm(ps[:, u * 512:(u + 1) * 512], it["k"][0:dk, (kt + u) * 128:(kt + u + 1) * 128], it["q"][0:dk, qc * 512:(qc + 1) * 512], True, True,
                           [it["k"], it["q"]], [ps], sig=(u == 1))

                def flush(it):
                    ps = it["ps"]
                    p = pT[st["p"] % 3]
                    st["p"] += 1
                    vop(ACT, lambda: nc.scalar.activation(out=p[:], in_=ps[:], func=AF.Exp, scale=it["scale"]), [ps], [p])
                    for u in range(2):
                        mm(it["acc"][:], it["v"][:, it["kt"] + u, :], p[:, u * 512:(u + 1) * 512], it["first"] and u == 0, it["last"] and u == 1, [it["v"], p], [it["acc"]],
                           sig=(u == 1))
                    if it["post"] is not None:
                        it["post"](it["acc"])

                pending = []
                groups[0]["load"]()
                for it in flat:
                    if it["ii"] == LA + 1 and it["g"] + 1 < len(groups):
                        groups[it["g"] + 1]["load"]()
                    emit_qk(it)
                    pending.append(it)
                    if len(pending) > LA:
                        flush(pending.pop(0))
                while pending:
                    flush(pending.pop(0))
                kb.barrier()

        def phase_dilated(l):
            with ExitStack() as ph:
                band = sb(ph, "p4_band", [128, 384], F32)
                dma(QS, band[:], c_band, [], [band])
                qT = [sb(ph, "p4_q%d" % i, [128, SMAX], BF16) for i in range(2)]
                kT = [sb(ph, "p4_k%d" % i, [128, SMAX], BF16) for i in range(2)]
                stq = sb(ph, "p4_stq", [128, SMAX], BF16)
                stk = sb(ph, "p4_stk", [128, SMAX], BF16)
                vt = [sb(ph, "p4_v%d" % i, [128, SMAX // 128, 2, 128], BF16) for i in range(2)]
                nsb = [sb(ph, "p4_n%d" % i, [64, SMAX], BF16) for i in range(2)]
                dsb = [sb(ph, "p4_d%d" % i, [64, SMAX], F32) for i in range(2)]
                psc = [pst(ph, "p4_s%d" % i, [128, 512]) for i in range(3)]
                pac = [pst(ph, "p4_a%d" % i, [128, 512]) for i in range(3)]
                pe_ = [sb(ph, "p4_e%d" % i, [128, 384], F32) for i in range(3)]
                pT = [sb(ph, "p4_p%d" % i, [128, 384], BF16) for i in range(3)]
                for v in vt:
                    vop(DVE, lambda: nc.vector.memset(v[:, :, :, 64:128], 1.0), [], [v])
                st = {"s": 0, "a": 0, "b": 0}
                scale = 64.0 ** -0.5
                glist = [(si, s0, S, g) for si, (s0, S) in enumerate(zip(seq_starts, seqs)) for g in range(3)]

                def load_group(gi):
                    si, s0, S, g = glist[gi]
                    r = C_R[g]
                    nlt = (S // r) // 128
                    q_, k_, v_ = qT[gi % 2], kT[gi % 2], vt[gi % 2]
                    if r == 1:
                        dma(QS, q_[:, 0:S], QKC[g, :, s0:s0 + S], [DR("qkc")], [q_])
                        dma(QS, k_[:, 0:S], QKC[3 + g, :, s0:s0 + S], [DR("qkc")], [k_])
                    else:
                        dma(QS, stq[:, 0:S], QKC[g, :, s0:s0 + S], [DR("qkc")], [stq])
                        dma(QS, stk[:, 0:S], QKC[3 + g, :, s0:s0 + S], [DR("qkc")], [stk])
                        vop(DVE, lambda: nc.vector.tensor_copy(out=q_[:, 0:S].rearrange("p (r m) -> p m r", r=r), in_=stq[:, 0:S].rearrange("p (m r) -> p m r", r=r)), [stq], [q_])
                        vop(POOL, lambda: nc.gpsimd.tensor_copy(out=k_[:, 0:S].rearrange("p (r m) -> p m r", r=r), in_=stk[:, 0:S].rearrange("p (m r) -> p m r", r=r)), [stk], [k_])
                    for rho in range(r):
                        for j in range(2):
                            src = VC[s0:s0 + S, g * 128 + j * 64:g * 128 + (j + 1) * 64].rearrange("(kt p r) d -> r p kt d", p=128, r=r)[rho]
                            for k0 in range(0, nlt, 8):
                                k1 = min(nlt, k0 + 8)
                                dma(QS, v_[:, rho * nlt + k0:rho * nlt + k1, j, 0:64], src[:, k0:k1, :], [DR("vc")], [v_])

                load_group(0)
                for gi, (si, s0, S, g) in enumerate(glist):
                    if True:
                        r = C_R[g]
                        L = S // r
                        nlt = L // 128
                        q_, k_, v_ = qT[gi % 2], kT[gi % 2], vt[gi % 2]
                        if gi + 1 < len(glist):
                            load_group(gi + 1)
                        for j in range(2):
                            nb_, db_ = nsb[j], dsb[j]
                            def emit_qk4(rho, t, j=j, q_=q_, k_=k_, L=L, nlt=nlt):
                                base = rho * L + t * 128
                                tiles = [tt for tt in (t - 1, t, t + 1) if 0 <= tt < nlt]
                                i3 = st["s"] % 3
                                st["s"] += 1
                                ps = psc[i3]
                                for tt in tiles:
                                    mi = tt - t + 1
                                    kb0 = rho * L + tt * 128
                                    mm(ps[:, mi * 128:(mi + 1) * 128], k_[j * 64:(j + 1) * 64, kb0:kb0 + 128], q_[j * 64:(j + 1) * 64, base:base + 128],
                                       True, True, [k_, q_], [ps], sig=(tt == tiles[-1]))
                                return dict(rho=rho, t=t, tiles=tiles, i3=i3)

                            def flush4(cx, j=j, v_=v_, nb_=nb_, db_=db_, r=r, S=S, nlt=nlt):
                                rho, t, tiles, i3 = cx["rho"], cx["t"], cx["tiles"], cx["i3"]
                                ps, e_, p_ = psc[i3], pe_[i3], pT[i3]
                                lo, hi = (tiles[0] - t + 1) * 128, (tiles[-1] - t + 2) * 128
                                vop(ACT, lambda: nc.scalar.activation(out=e_[:, lo:hi], in_=ps[:, lo:hi], func=AF.Exp, scale=scale), [ps], [e_])
                                vop(POOL, lambda: nc.gpsimd.tensor_tensor(out=p_[:, lo:hi], in0=e_[:, lo:hi], in1=band[:, lo:hi], op=ALU.mult), [e_, band], [p_])
                                acc = pac[st["a"] % 3]
                                st["a"] += 1
                                for n_, tt in enumerate(tiles):
                                    mi = tt - t + 1
                                    mm(acc[:, 0:128], v_[:, rho * nlt + tt, j, :], p_[:, mi * 128:(mi + 1) * 128], n_ == 0, n_ == len(tiles) - 1, [v_, p_], [acc])
                                if r == 1:
                                    no = nb_[:, t * 128:(t + 1) * 128]
                                    do = db_[:, t * 128:(t + 1) * 128]
                                else:
                                    no = nb_[:, 0:S].rearrange("p (m r) -> p r m", r=r)[:, rho, t * 128:(t + 1) * 128]
                                    do = db_[:, 0:S].rearrange("p (m r) -> p r m", r=r)[:, rho, t * 128:(t + 1) * 128]
                                vop(DVE, lambda: nc.vector.tensor_copy(out=no, in_=acc[0:64, 0:128]), [acc], [nb_])
                                vop(ACT, lambda: nc.scalar.copy(out=do, in_=acc[64:128, 0:128]), [acc], [db_])

                            pend = []
                            for rho in range(r):
                                for t in range(nlt):
                                    pend.append(emit_qk4(rho, t))
                                    if len(pend) > 2:
                                        flush4(pend.pop(0))
                            while pend:
                                flush4(pend.pop(0))
                            dma(QS, NCd[g * 2 + j, :, s0:s0 + S], nb_[:, 0:S], [nb_], [DR("ncd", si, g, j)])
                            dma(QS, DCd[g * 2 + j, :, s0:s0 + S], db_[:, 0:S], [db_], [DR("dcd", si, g, j)])
                kb.barrier()
            with ExitStack() as ph:
                nn = [sb(ph, "p4_nn%d" % i, [64, 6, 512], BF16) for i in range(2)]
                dd = [sb(ph, "p4_dd%d" % i, [64, 6, 512], F32) for i in range(2)]
                tt_ = [sb(ph, "p4_tt%d" % i, [64, 2, 512], F32) for i in range(2)]
                oo = [sb(ph, "p4_oo%d" % i, [64, 6, 512], BF16) for i in range(2)]
                for c in range(NCH):
                    tk = c * 512
                    n_, d_, t_, o_ = nn[c % 2], dd[c % 2], tt_[c % 2], oo[c % 2]
                    dma(QS, n_[:], NCd[:, :, tk:tk + 512].rearrange("h p t -> p h t"), [], [n_])
                    dma(QS, d_[:], DCd[:, :, tk:tk + 512].rearrange("h p t -> p h t"), [], [d_])
                    vop(DVE, lambda: nc.vector.tensor_tensor(out=t_[:], in0=d_[:, 0:2, :], in1=d_[:, 2:4, :], op=ALU.add), [d_], [t_])
                    vop(DVE, lambda: nc.vector.tensor_tensor(out=t_[:], in0=t_[:], in1=d_[:, 4:6, :], op=ALU.add), [d_, t_], [t_])
                    vop(DVE, lambda: nc.vector.reciprocal(out=t_[:], in_=t_[:]), [t_], [t_])
                    for g in range(3):
                        vop(POOL, lambda: nc.gpsimd.tensor_tensor(out=o_[:, 2 * g:2 * g + 2, :], in0=n_[:, 2 * g:2 * g + 2, :], in1=t_[:], op=ALU.mult), [n_, t_], [o_])
                    for hh in range(6):
                        f0 = 640 + hh * 64
                        dma(QS, mixT[f0 // 128, f0 % 128:f0 % 128 + 64, tk:tk + 512], o_[:, hh, :], [o_], [DR("mixT", c)])
                kb.barrier()

        def layer_norm(z, gb, bb, st4, junk):
            vop(DVE, lambda: nc.vector.memset(st4[:, 0:2], 0.0), [], [st4])
            vop(ACT, lambda: nc.scalar.activation(out=junk[:], in_=z[:], func=AF.Identity, accum_out=st4[:, 0:1]), [z, st4], [junk, st4])
            vop(ACT, lambda: nc.scalar.activation(out=junk[:], in_=z[:], func=AF.Square, accum_out=st4[:, 1:2]), [z, st4], [junk, st4])
            vop(DVE, lambda: nc.vector.tensor_scalar_mul(out=st4[:, 2:3], in0=st4[:, 0:1], scalar1=1.0 / D), [st4], [st4])
            vop(DVE, lambda: nc.vector.tensor_tensor(out=st4[:, 3:4], in0=st4[:, 2:3], in1=st4[:, 2:3], op=ALU.mult), [st4], [st4])
            vop(DVE, lambda: nc.vector.scalar_tensor_tensor(out=st4[:, 4:5], in0=st4[:, 1:2], scalar=1.0 / D, in1=st4[:, 3:4], op0=ALU.mult, op1=ALU.subtract), [st4], [st4])
            vop(ACT, lambda: nc.scalar.activation(out=st4[:, 5:6], in_=st4[:, 4:5], func=AF.Sqrt, bias=epsr[:, 0:1], scale=1.0), [st4, epsr], [st4])
            vop(DVE, lambda: nc.vector.reciprocal(out=st4[:, 6:7], in_=st4[:, 5:6]), [st4], [st4])
            vop(DVE, lambda: nc.vector.tensor_scalar(out=z[:], in0=z[:], scalar1=st4[:, 2:3], scalar2=st4[:, 6:7], op0=ALU.subtract, op1=ALU.mult), [z, st4], [z])
            vop(POOL, lambda: nc.gpsimd.tensor_tensor(out=z[:], in0=z[:], in1=gb[:], op=ALU.mult), [z, gb], [z])
            vop(POOL, lambda: nc.gpsimd.tensor_tensor(out=z[:], in0=z[:], in1=bb[:], op=ALU.add), [z, bb], [z])

        def phase_out_ln1(l, xsrc):
            with ExitStack() as ph:
                wout = sb(ph, "p5_wout", [128, 8, D], BF16)
                wr = sb(ph, "p5_wr", [128, 8, 36], F32)
                gb = sb(ph, "p5_g", [128, D], F32)
                bb = sb(ph, "p5_b", [128, D], F32)
                for j in range(8):
                    dma(QP, wout[:, j, :], w_out[l, j * 128:(j + 1) * 128, :], [], [wout])
                dma(QS, wr[:, :, 0:4], w_coarse[l].rearrange("(j p) g -> p j g", p=128), [], [wr])
                for g in range(4):
                    dma(QS, wr[:, :, 4 + g * 8:12 + g * 8], w_fine[l, g].rearrange("(j p) e -> p j e", p=128), [], [wr])
                dma(QS, gb[:], ln1_g[l:l + 1, :].partition_broadcast(128), [], [gb])
                dma(QS, bb[:], ln1_b[l:l + 1, :].partition_broadcast(128), [], [bb])
                mx = [sb(ph, "p5_m%d" % i, [128, 8, 512], BF16) for i in range(2)]
                xr = [sb(ph, "p5_x%d" % i, [128, 4, D], F32) for i in range(2)]
                z = [sb(ph, "p5_z%d" % i, [128, D], F32) for i in range(3)]
                zb = [sb(ph, "p5_zb%d" % i, [128, D], BF16) for i in range(2)]
                junk = sb(ph, "p5_junk", [128, D], F32)
                st4 = [sb(ph, "p5_st%d" % i, [128, 8], F32) for i in range(2)]
                x1T = [sb(ph, "p5_xT%d" % i, [128, 8, 128], F32) for i in range(2)]
                po = [pst(ph, "p5_po%d" % i, [128, 1024]) for i in range(2)]
                pt = [pst(ph, "p5_pt%d" % i, [128, 512]) for i in range(2)]
                plg = pst(ph, "p5_plg", [128, 512])
                lg = sb(ph, "p5_lg", [128, 36], F32)
                sm = sb(ph, "p5_sm", [128, 64], F32)

                def load(c):
                    dma(QS, mx[c % 2][:], mixT[:, :, c * 512:(c + 1) * 512].rearrange("j p t -> p j t"), [DR("mixT", c)], [mx[c % 2]])
                    dma(QS, xr[c % 2][:], xsrc[c * 512:(c + 1) * 512, :].rearrange("(a p) d -> p a d", p=128), [DR("x", c)], [xr[c % 2]])

                def stage_a(it):
                    c, a = divmod(it, 4)
                    m_ = mx[c % 2]
                    p_ = po[it % 2]
                    for hf in range(2):
                        for j in range(8):
                            mm(p_[:, hf * 512:(hf + 1) * 512], m_[:, j, a * 128:(a + 1) * 128], wout[:, j, hf * 512:(hf + 1) * 512], j == 0, j == 7, [m_, wout], [p_])

                def stage_b1(it):
                    c, a = divmod(it, 4)
                    x_ = xr[c % 2]
                    tok0 = it * 128
                    p_ = po[it % 2]
                    z_ = z[it % 3]
                    vop(DVE, lambda: nc.vector.scalar_tensor_tensor(out=z_[:], in0=x_[:, a, :], scalar=DN_ALPHA, in1=p_[:], op0=ALU.mult, op1=ALU.add), [x_, p_], [z_])
                    layer_norm(z_, gb, bb, st4[it % 2], junk)
                    dma(QS, x1d[tok0:tok0 + 128, :], z_[:], [z_], [DR("x1", it)])
                    zb_ = zb[it % 2]
                    vop(ACT, lambda: nc.scalar.copy(out=zb_[:], in_=z_[:]), [z_], [zb_])
                    dma(QS, x1b[tok0:tok0 + 128, :], zb_[:], [zb_], [DR("x1b", it)])

                def stage_b2(it):
                    z_ = z[it % 3]
                    xT_ = x1T[it % 2]
                    for hf in range(2):
                        for jj in range(4):
                            j = hf * 4 + jj
                            tr(pt[hf][:, jj * 128:(jj + 1) * 128], z_[:, j * 128:(j + 1) * 128], ident_f[:], [z_, ident_f], [pt[hf]], sig=(jj == 3))
                        vop(ACT, lambda: nc.scalar.copy(out=xT_[:, hf * 4:(hf + 1) * 4, :], in_=pt[hf][:].rearrange("p (j t) -> p j t", j=4)), [pt[hf]], [xT_])
                    for j in range(8):
                        mm(plg[:, 0:36], xT_[:, j, :], wr[:, j, :], j == 0, j == 7, [xT_, wr], [plg])
                    vop(DVE, lambda: nc.vector.tensor_copy(out=lg[:], in_=plg[:, 0:36]), [plg], [lg])
                    vop(DVE, lambda: nc.vector.reduce_max(out=sm[:, 0:1], in_=lg[:, 0:4], axis=AX.X), [lg], [sm])
                    vop(DVE, lambda: nc.vector.tensor_scalar_mul(out=sm[:, 1:2], in0=sm[:, 0:1], scalar1=-1.0), [sm], [sm])
                    vop(DVE, lambda: nc.vector.memset(sm[:, 2:3], 0.0), [], [sm])
                    vop(ACT, lambda: nc.scalar.activation(out=sm[:, 44:48], in_=lg[:, 0:4], func=AF.Exp, bias=sm[:, 1:2], scale=1.0, accum_out=sm[:, 2:3]), [lg, sm], [sm])
                    vop(DVE, lambda: nc.vector.reciprocal(out=sm[:, 3:4], in_=sm[:, 2:3]), [sm], [sm])
                    vop(DVE, lambda: nc.vector.tensor_scalar(out=sm[:, 4:8], in0=lg[:, 0:4], scalar1=sm[:, 0:1], scalar2=None, op0=ALU.is_equal), [lg, sm], [sm])
                    vop(DVE, lambda: nc.vector.tensor_scalar_mul(out=sm[:, 8:16], in0=lg[:, 4:12], scalar1=sm[:, 4:5]), [lg, sm], [sm])
                    for g in range(1, 4):
                        vop(DVE, lambda: nc.vector.scalar_tensor_tensor(out=sm[:, 8:16], in0=lg[:, 4 + g * 8:12 + g * 8], scalar=sm[:, 4 + g:5 + g], in1=sm[:, 8:16],
                                                                        op0=ALU.mult, op1=ALU.add), [lg, sm], [sm])
                    vop(DVE, lambda: nc.vector.max(out=sm[:, 16:24], in_=sm[:, 8:16]), [sm], [sm])
                    vop(DVE, lambda: nc.vector.tensor_scalar(out=sm[:, 24:32], in0=sm[:, 8:16], scalar1=sm[:, 16:17], scalar2=None, op0=ALU.is_equal), [sm], [sm])
                    vop(DVE, lambda: nc.vector.tensor_scalar(out=sm[:, 32:40], in0=sm[:, 8:16], scalar1=sm[:, 17:18], scalar2=None, op0=ALU.is_equal), [sm], [sm])
                    vop(DVE, lambda: nc.vector.tensor_tensor(out=sm[:, 40:41], in0=sm[:, 17:18], in1=sm[:, 16:17], op=ALU.subtract), [sm], [sm])
                    vop(ACT, lambda: nc.scalar.activation(out=sm[:, 41:42], in_=sm[:, 40:41], func=AF.Exp), [sm], [sm])
                    vop(DVE, lambda: nc.vector.tensor_scalar_add(out=sm[:, 42:43], in0=sm[:, 41:42], scalar1=1.0), [sm], [sm])
                    vop(DVE, lambda: nc.vector.reciprocal(out=sm[:, 43:44], in_=sm[:, 42:43]), [sm], [sm])
                    vop(DVE, lambda: nc.vector.tensor_tensor(out=wts[:, it, 0:1], in0=sm[:, 43:44], in1=sm[:, 3:4], op=ALU.mult), [sm], [wts])
                    vop(DVE, lambda: nc.vector.tensor_tensor(out=wts[:, it, 1:2], in0=wts[:, it, 0:1], in1=sm[:, 41:42], op=ALU.mult), [sm, wts], [wts])
                    for g in range(4):
                        vop(DVE, lambda: nc.vector.tensor_scalar_mul(out=E0[:, it, g * 8:(g + 1) * 8], in0=sm[:, 24:32], scalar1=sm[:, 4 + g:5 + g]), [sm], [E0])
                        vop(DVE, lambda: nc.vector.tensor_scalar_mul(out=E1[:, it, g * 8:(g + 1) * 8], in0=sm[:, 32:40], scalar1=sm[:, 4 + g:5 + g]), [sm], [E1])

                load(0)
                stage_a(0)
                for it in range(NCH * 4):
                    c, a = divmod(it, 4)
                    if a == 0 and c + 1 < NCH:
                        load(c + 1)
                    if it + 1 < NCH * 4:
                        stage_a(it + 1)
                    stage_b1(it)
                    if it >= 1:
                        stage_b2(it - 1)
                stage_b2(NCH * 4 - 1)
                kb.barrier()

        def phase_route(l):
            with ExitStack() as ph:
                e01 = sb(ph, "p6_e01", [128, NT, 32], F32)
                run = sb(ph, "p6_run", [128, NT + 1, 32], F32)
                cb = sb(ph, "p6_cb", [128, 32], F32)
                ci = sb(ph, "p6_ci", [128, 32], I32)
                inc = [sb(ph, "p6_inc%d" % i, [128, 32], F32) for i in range(2)]
                pstart = sb(ph, "p6_ps", [128, 32], F32)
                pend = sb(ph, "p6_pe", [128, 32], F32)
                destf = sb(ph, "p6_df", [128, NT, 2], F32)
                tmp = [sb(ph, "p6_t%d" % i, [128, 32], F32) for i in range(2)]
                tmp2 = [sb(ph, "p6_u%d" % i, [128, 32], F32) for i in range(2)]
                bex = sb(ph, "p6_bex", [128, NB], F32)
                pc = [pst(ph, "p6_pc%d" % i, [128, 512]) for i in range(2)]
                vop(POOL, lambda: nc.gpsimd.tensor_tensor(out=e01[:], in0=E0[:], in1=E1[:], op=ALU.add), [E0, E1], [e01])
                vop(DVE, lambda: nc.vector.memset(run[:, 0, :], 0.0), [], [run])
                for i in range(NT):
                    vop(DVE, lambda: nc.vector.tensor_tensor(out=run[:, i + 1, :], in0=run[:, i, :], in1=e01[:, i, :], op=ALU.add), [run, e01], [run])
                mm(pc[0][:, 0:32], ones_f[:], run[:, NT, :], True, True, [ones_f, run], [pc[0]])
                vop(DVE, lambda: nc.vector.tensor_scalar_add(out=cb[:], in0=pc[0][:, 0:32], scalar1=float(BLK - 1)), [pc[0]], [cb])
                vop(DVE, lambda: nc.vector.tensor_copy(out=ci[:], in_=cb[:]), [cb], [ci])
                sh = int(math.log2(BLK))
                vop(DVE, lambda: nc.vector.tensor_single_scalar(out=ci[:], in_=ci[:], scalar=sh, op=ALU.arith_shift_right), [ci], [ci])
                vop(DVE, lambda: nc.vector.tensor_single_scalar(out=ci[:], in_=ci[:], scalar=sh, op=ALU.logical_shift_left), [ci], [ci])
                vop(DVE, lambda: nc.vector.tensor_copy(out=cb[:], in_=ci[:]), [ci], [cb])
                vop(DVE, lambda: nc.vector.tensor_copy(out=inc[0][:], in_=cb[:]), [cb], [inc[0]])
                cur = 0
                s = 1
                while s < 32:
                    a_, b_ = inc[cur], inc[1 - cur]
                    vop(DVE, lambda: nc.vector.tensor_copy(out=b_[:, 0:s], in_=a_[:, 0:s]), [a_], [b_])
                    vop(DVE, lambda: nc.vector.tensor_tensor(out=b_[:, s:32], in0=a_[:, s:32], in1=a_[:, 0:32 - s], op=ALU.add), [a_], [b_])
                    cur = 1 - cur
                    s *= 2
                vop(DVE, lambda: nc.vector.tensor_copy(out=pend[:], in_=inc[cur][:]), [inc[cur]], [pend])
                vop(DVE, lambda: nc.vector.tensor_tensor(out=pstart[:], in0=pend[:], in1=cb[:], op=ALU.subtract), [pend, cb], [pstart])
                for i in range(NT):
                    p_ = pc[i % 2]
                    mm(p_[:, 0:32], tri_f[:], e01[:, i, :], True, False, [tri_f, e01], [p_], sig=False)
                    mm(p_[:, 0:32], ones_f[:], run[:, i, :], False, True, [ones_f, run], [p_])
                    t_ = tmp[i % 2]
                    u_ = tmp2[i % 2]
                    vop(DVE, lambda: nc.vector.tensor_tensor(out=t_[:], in0=p_[:, 0:32], in1=pstart[:], op=ALU.add), [p_, pstart], [t_])
                    for k, EE in enumerate((E0, E1)):
                        vop(POOL, lambda: nc.gpsimd.tensor_tensor(out=u_[:], in0=t_[:], in1=EE[:, i, :], op=ALU.mult), [t_, EE], [u_])
                        vop(DVE, lambda: nc.vector.reduce_sum(out=destf[:, i, k:k + 1], in_=u_[:], axis=AX.X), [u_], [destf])
                vop(DVE, lambda: nc.vector.tensor_copy(out=dest_i[:], in_=destf[:]), [destf], [dest_i])
                vop(DVE, lambda: nc.vector.memset(bex[:], 0.0), [], [bex])
                for e in range(NEXP):
                    vop(DVE, lambda: nc.vector.scalar_tensor_tensor(out=bex[:], in0=misc[:, 1:1 + NB], scalar=pend[:, e:e + 1], in1=bex[:], op0=ALU.is_ge, op1=ALU.add),
                        [misc, pend, bex], [bex])
                vop(DVE, lambda: nc.vector.tensor_scalar(out=bex[:], in0=bex[:], scalar1=float(NEXP - 1), scalar2=float(l * NEXP), op0=ALU.min, op1=ALU.add), [bex], [bex])
                vop(DVE, lambda: nc.vector.tensor_scalar(out=bex[:], in0=bex[:], scalar1=128.0, scalar2=misc[:, 0:1], op0=ALU.mult, op1=ALU.add), [bex, misc], [bex])
                vop(DVE, lambda: nc.vector.tensor_copy(out=widx[:], in_=bex[:]), [bex], [widx])
                kb.barrier()
                xb = [sb(ph, "p7_x%d" % i, [128, D], BF16) for i in range(4)]
                for i in range(NT):
                    b_ = xb[i % 4]
                    dma(QS, b_[:], x1b[i * 128:(i + 1) * 128, :], [], [b_])
                    for k in range(2):
                        kb.dma(QP, lambda: nc.gpsimd.indirect_dma_start(out=xs, out_offset=bass.IndirectOffsetOnAxis(ap=dest_i[:, i, k:k + 1], axis=0),
                                                                         in_=b_[:], in_offset=None), R=rs(b_, dest_i), W=[])
                kb.barrier()

        def phase_experts(l):
            w1v = w1.rearrange("l e (p j) c -> (l e p) (j c)", j=8)
            w3v = w3.rearrange("l e (p j) c -> (l e p) (j c)", j=8)
            w2v = w2.rearrange("l e (p j) c -> (l e p) (j c)", j=4)
            with ExitStack() as ph:
                W1 = [sb(ph, "p8_w1%d" % i, [128, 8, DE], BF16) for i in range(2)]
                W3 = [sb(ph, "p8_w3%d" % i, [128, 8, DE], BF16) for i in range(2)]
                W2 = [sb(ph, "p8_w2%d" % i, [128, 4, D], BF16) for i in range(2)]
                xt = [sb(ph, "p8_x%d" % i, [128, 4, D], BF16) for i in range(2)]
                xsT = [sb(ph, "p8_xT%d" % i, [128, 8, 512], BF16) for i in range(2)]
                hT = [sb(ph, "p8_h%d" % i, [128, 4, 512], BF16) for i in range(2)]
                sl = [sb(ph, "p8_sl%d" % i, [128, 512], F32) for i in range(2)]
                yo = [sb(ph, "p8_y%d" % i, [128, D], F32) for i in range(2)]
                ptr = [pst(ph, "p8_pt%d" % i, [128, 1024], BF16) for i in range(2)]
                p1 = [pst(ph, "p8_p1%d" % i, [128, 512]) for i in range(1)]
                p3 = [pst(ph, "p8_p3%d" % i, [128, 512]) for i in range(1)]
                py = [pst(ph, "p8_py%d" % i, [128, 1024]) for i in range(2)]

                def load(b):
                    i = b % 2
                    for (Wt, src) in ((W1[i], w1v), (W3[i], w3v), (W2[i], w2v)):
                        kb.dma(QP, lambda: nc.gpsimd.indirect_dma_start(out=Wt[:].rearrange("p a c -> p (a c)"), out_offset=None, in_=src,
                                                                         in_offset=bass.IndirectOffsetOnAxis(ap=widx[:, b:b + 1], axis=0)), R=rs(widx), W=rs(Wt))
                    dma(QS, xt[i][:], xs[b * BLK:(b + 1) * BLK, :].rearrange("(a p) d -> p a d", p=128), [DR("xs")], [xt[i]])

                load(0)
                for b in range(NB):
                    if b + 1 < NB:
                        load(b + 1)
                    i = b % 2
                    x_, xT_, h_ = xt[i], xsT[i], hT[i]
                    for a in range(4):
                        ps = ptr[a % 2]
                        for j in range(8):
                            tr(ps[:, j * 128:(j + 1) * 128], x_[:, a, :].rearrange("p (q j) -> p j q", j=8)[:, j, :], ident_b[:], [x_, ident_b], [ps], sig=(j == 7))
                        vop(DVE, lambda: nc.vector.tensor_copy(out=xT_[:, :, a * 128:(a + 1) * 128], in_=ps[:].rearrange("p (j t) -> p j t", j=8)), [ps], [xT_])
                    for jp in range(4):
                        w1s = W1[i][:].rearrange("p j (q f) -> p j f q", f=4)
                        w3s = W3[i][:].rearrange("p j (q f) -> p j f q", f=4)
                        for j in range(8):
                            mm(p1[0][:], w1s[:, j, jp, :], xT_[:, j, :], j == 0, j == 7, [W1[i], xT_], [p1[0]])
                        for j in range(8):
                            mm(p3[0][:], w3s[:, j, jp, :], xT_[:, j, :], j == 0, j == 7, [W3[i], xT_], [p3[0]])
                        s_ = sl[jp % 2]
                        vop(ACT, lambda: nc.scalar.activation(out=s_[:], in_=p1[0][:], func=AF.Silu), [p1[0]], [s_])
                        vop(DVE, lambda: nc.vector.tensor_tensor(out=h_[:, jp, :], in0=p3[0][:], in1=s_[:], op=ALU.mult), [p3[0], s_], [h_])
                    for a in range(4):
                        p_ = py[a % 2]
                        for hf in range(2):
                            for jp in range(4):
                                mm(p_[:, hf * 512:(hf + 1) * 512], h_[:, jp, a * 128:(a + 1) * 128], W2[i][:, jp, hf * 512:(hf + 1) * 512], jp == 0, jp == 3, [h_, W2[i]], [p_])
                        y_ = yo[a % 2]
                        vop(ACT, lambda: nc.scalar.copy(out=y_[:], in_=p_[:]), [p_], [y_])
                        r0 = b * BLK + a * 128
                        dma(QS, ys[r0:r0 + 128, :], y_[:], [y_], [DR("ys", b, a)])
                kb.barrier()

        def phase_ln2(l, dst, make_xT):
            with ExitStack() as ph:
                gb = sb(ph, "p9_g", [128, D], F32)
                bb = sb(ph, "p9_b", [128, D], F32)
                dma(QS, gb[:], ln2_g[l:l + 1, :].partition_broadcast(128), [], [gb])
                dma(QS, bb[:], ln2_b[l:l + 1, :].partition_broadcast(128), [], [bb])
                x1t = [sb(ph, "p9_x%d" % i, [128, D], F32) for i in range(2)]
                y0 = [sb(ph, "p9_y0%d" % i, [128, D], F32) for i in range(2)]
                y1 = [sb(ph, "p9_y1%d" % i, [128, D], F32) for i in range(2)]
                z = [sb(ph, "p9_z%d" % i, [128, D], F32) for i in range(3)]
                zb = [sb(ph, "p9_zb%d" % i, [128, D], BF16) for i in range(2)]
                junk = sb(ph, "p9_junk", [128, D], F32)
                st4 = [sb(ph, "p9_st%d" % i, [128, 8], F32) for i in range(2)]
                xTs = [sb(ph, "p9_t%d" % i, [128, 8, 512], BF16) for i in range(2)]
                pss = [pst(ph, "p9_ps%d" % i, [128, 1024], BF16) for i in range(2)]

                def load(i):
                    dma(QS, x1t[i % 2][:], x1d[i * 128:(i + 1) * 128, :], [], [x1t[i % 2]])
                    for k, yy in enumerate((y0, y1)):
                        kb.dma(QP, lambda: nc.gpsimd.indirect_dma_start(out=yy[i % 2][:], out_offset=None, in_=ys,
                                                                         in_offset=bass.IndirectOffsetOnAxis(ap=dest_i[:, i, k:k + 1], axis=0)), R=rs(dest_i), W=rs(yy[i % 2]))

                load(0)
                for i in range(NT):
                    if i + 1 < NT:
                        load(i + 1)
                    c, a = divmod(i, 4)
                    z_ = z[i % 3]
                    vop(DVE, lambda: nc.vector.tensor_scalar_mul(out=z_[:], in0=y0[i % 2][:], scalar1=wts[:, i, 0:1]), [y0[i % 2], wts], [z_])
                    vop(DVE, lambda: nc.vector.scalar_tensor_tensor(out=z_[:], in0=y1[i % 2][:], scalar=wts[:, i, 1:2], in1=z_[:], op0=ALU.mult, op1=ALU.add), [y1[i % 2], wts, z_], [z_])
                    vop(DVE, lambda: nc.vector.scalar_tensor_tensor(out=z_[:], in0=x1t[i % 2][:], scalar=DN_ALPHA, in1=z_[:], op0=ALU.mult, op1=ALU.add), [x1t[i % 2], z_], [z_])
                    layer_norm(z_, gb, bb, st4[i % 2], junk)
                    dma(QS, dst[i * 128:(i + 1) * 128, :], z_[:], [z_], [DR("x", c)])
                    if make_xT:
                        zb_ = zb[i % 2]
                        vop(ACT, lambda: nc.scalar.copy(out=zb_[:], in_=z_[:]), [z_], [zb_])
                        emit_xT_tile(pss[i % 2], zb_, a, xTs[c % 2])
                        if a == 3:
                            dma(QS, xT[:, :, c * 512:(c + 1) * 512].rearrange("j p t -> p j t"), xTs[c % 2][:], [xTs[c % 2]], [DR("xT", c)])
                kb.barrier()

        phase_xprep(x_in)
        for l in range(depth):
            xsrc = x_in if l == 0 else xa
            last = (l == depth - 1)
            if stop_after == "xprep":
                break
            phase_proj(l)
            if stop_after == "proj":
                break
            phase_full_attn(l)
            phase_dilated(l)
            if stop_after == "attn":
                break
            phase_out_ln1(l, xsrc)
            if stop_after == "ln1":
                break
            phase_route(l)
            phase_experts(l)
            phase_ln2(l, y_out if last else xa, not last)
        kb.barrier()
        stats = dict(pe=kb.pe.n, act=kb.act.n, dve=kb.dve.n, pool=kb.pool.n, qs=kb.qs.k, qp=kb.qp.k, qs_max=max(kb.qs.cnt), qp_max=max(kb.qp.cnt))
    build.stats = stats
    return nc


WNAMES = ["w_in", "diff_lambda", "diff_subln", "mla_q_norm", "mla_w_uq", "mla_kv_norm", "mla_w_ukv", "w_out", "ln1_g", "ln1_b",
          "moe_w_coarse", "moe_w_fine", "moe_w1", "moe_w3", "moe_w2", "ln2_g", "ln2_b"]


def kernel(x_prompt, x_sample, **w):
    x_prompt = np.asarray(x_prompt, dtype=np.float32)
    x_sample = np.asarray(x_sample, dtype=np.float32)
    n = 8
    seqs = [2048, 2048, 4096]
    nc = build(seqs, DEPTH)
    T = sum(seqs)
    NB = -(-(2 * T + NEXP * (BLK - 1)) // BLK)
    consts = host_consts(max(seqs), NB)
    wd = {k: np.ascontiguousarray(np.asarray(w[k], dtype=np.float32)) for k in WNAMES}
    in_maps = []
    for c in range(n):
        xc = np.concatenate([x_prompt[2 * c], x_prompt[2 * c + 1], x_sample[c]], axis=0)
        m = {"x": np.ascontiguousarray(xc)}
        m.update(wd)
        m.update(consts)
        in_maps.append(m)
    res = run_bass_kernel_spmd(nc, in_maps, core_ids=list(range(n)))
    yp = np.empty_like(x_prompt)
    ysm = np.empty_like(x_sample)
    for c in range(n):
        y = res.results[c]["y"]
        yp[2 * c] = y[0:2048]
        yp[2 * c + 1] = y[2048:4096]
        ysm[c] = y[4096:8192]
    return (yp, ysm)
```
